# Optimizing a Trainium2 kernel written in Bass

```python
import jax, jax.numpy as jnp
from jax import lax
import numpy as np

D_MODEL = 1024
BATCH = 8
SEQ = 2048
DEPTH = 1

M_INNER = D_MODEL
M_HEADDIM = 64
M_HEADS = M_INNER // M_HEADDIM
M_GROUPS = 2
M_STATE = 64
M_CONV = 4
M_CHUNK = 128
M_CONV_DIM = M_INNER + 2 * M_GROUPS * M_STATE
R_HEADS = 8
R_QK_DIM = D_MODEL // 2
R_HEAD_QK = R_QK_DIM // R_HEADS
R_V_DIM = D_MODEL
R_HEAD_V = R_V_DIM // R_HEADS
R_CHUNK = 128
ROPE_BASE = 10000.0
EPS = 1e-6
SPLITS = (M_INNER, M_CONV_DIM, M_HEADS, R_QK_DIM, R_QK_DIM, R_V_DIM, R_V_DIM, D_MODEL, D_MODEL)
D_IN_PROJ = sum(SPLITS)

kernel_name = 'hybrid_ssd_retention_gated_block'


def rms_norm(x, w=None):
    xf = x.astype(jnp.float32)
    y = xf * lax.rsqrt(jnp.mean(xf * xf, axis=-1, keepdims=True) + EPS)
    if w is not None:
        y = y * w.astype(jnp.float32)
    return y


def causal_dwconv(u, w, b):
    K = w.shape[0]
    out = lax.conv_general_dilated(
        u, w.astype(u.dtype)[:, None, :], window_strides=(1,), padding=[(K - 1, 0)],
        dimension_numbers=('NWC', 'WIO', 'NWC'), feature_group_count=u.shape[-1])
    return out + b.astype(u.dtype)


def ssd_chunked(x, dt, A, Bm, Cm):
    Bsz, L, H, P = x.shape
    G, N = Bm.shape[-2], Bm.shape[-1]
    Hg = H // G
    Q = M_CHUNK
    nc = L // Q
    x = x.reshape(Bsz, nc, Q, G, Hg, P)
    dt = dt.reshape(Bsz, nc, Q, G, Hg)
    Bm = Bm.reshape(Bsz, nc, Q, G, N)
    Cm = Cm.reshape(Bsz, nc, Q, G, N)
    a = dt * A.reshape(G, Hg)
    a_cs = jnp.cumsum(a, axis=2)
    seg = a_cs[:, :, :, None] - a_cs[:, :, None, :]
    causal = jnp.tril(jnp.ones((Q, Q), dtype=bool))[:, :, None, None]
    Lmat = jnp.exp(jnp.where(causal, seg, -jnp.inf))
    cb = jnp.einsum('bcign,bcjgn->bcijg', Cm, Bm)
    w_ij = cb[..., None] * Lmat * dt[:, :, None]
    y_diag = jnp.einsum('bcijgh,bcjghp->bcighp', w_ij, x)
    decay_to_end = jnp.exp(a_cs[:, :, -1:] - a_cs)
    x_w = x * (decay_to_end * dt)[..., None]
    states = jnp.einsum('bcjgn,bcjghp->bcghpn', Bm, x_w)
    chunk_decay = jnp.exp(a_cs[:, :, -1])

    def step(S, inp):
        st, dec = inp
        return S * dec[..., None, None] + st, S

    S0 = jnp.zeros_like(states[:, 0])
    _, S_prev = lax.scan(step, S0, (jnp.moveaxis(states, 1, 0), jnp.moveaxis(chunk_decay, 1, 0)))
    S_prev = jnp.moveaxis(S_prev, 0, 1)
    y_off = jnp.einsum('bcign,bcghpn->bcighp', Cm, S_prev) * jnp.exp(a_cs)[..., None]
    return (y_diag + y_off).reshape(Bsz, L, H, P)


def rotary(t, pos):
    half = t.shape[-1] // 2
    inv = ROPE_BASE ** (-jnp.arange(half, dtype=jnp.float32) / half)
    ang = pos[:, None] * inv[None, :]
    cos = jnp.cos(ang)[:, None, :]
    sin = jnp.sin(ang)[:, None, :]
    t1, t2 = t[..., :half], t[..., half:]
    return jnp.concatenate([t1 * cos - t2 * sin, t1 * sin + t2 * cos], axis=-1)


def retention_chunked(q, k, v):
    Bsz, L, H, dk = q.shape
    dv = v.shape[-1]
    Q = R_CHUNK
    nc = L // Q
    log_g = jnp.log1p(-jnp.exp2(-5.0 - jnp.arange(H, dtype=jnp.float32)))
    idx = jnp.arange(Q, dtype=jnp.float32)
    rel = idx[:, None] - idx[None, :]
    dmat = jnp.exp(jnp.where(rel[None] >= 0, rel[None] * log_g[:, None, None], -jnp.inf))
    q = q.reshape(Bsz, nc, Q, H, dk)
    k = k.reshape(Bsz, nc, Q, H, dk)
    v = v.reshape(Bsz, nc, Q, H, dv)
    s = jnp.einsum('bcihd,bcjhd->bchij', q, k) * dmat
    inner = jnp.einsum('bchij,bcjhe->bcihe', s, v)
    k_w = k * jnp.exp((Q - 1 - idx)[:, None] * log_g[None, :])[:, :, None]
    kv = jnp.einsum('bcjhd,bcjhe->bchde', k_w, v)
    chunk_decay = jnp.exp(Q * log_g)

    def step(R, kv_c):
        return R * chunk_decay[:, None, None] + kv_c, R

    R0 = jnp.zeros_like(kv[:, 0])
    _, R_prev = lax.scan(step, R0, jnp.moveaxis(kv, 1, 0))
    R_prev = jnp.moveaxis(R_prev, 0, 1)
    q_w = q * jnp.exp((idx + 1)[:, None] * log_g[None, :])[:, :, None]
    cross = jnp.einsum('bcihd,bchde->bcihe', q_w, R_prev)
    return (inner + cross).reshape(Bsz, L, H, dv)


def hybrid_layer(x, c, w_ada, b_ada, norm_w, w_in, conv_w, conv_b, dt_bias, a_log, d_skip,
                 m_norm_w, w_proj_m, w_proj_r, w_out):
    f32 = jnp.float32
    Bsz, L, _ = x.shape
    mod = c.astype(f32) @ w_ada.astype(f32) + b_ada.astype(f32)
    shift, scale, gate = jnp.split(mod, 3, axis=-1)
    h = rms_norm(x, norm_w) * (1.0 + scale[:, None]) + shift[:, None]
    proj = h @ w_in.astype(f32)
    offs = []
    acc = 0
    for s_ in SPLITS[:-1]:
        acc += s_
        offs.append(acc)
    z_m, xbc, dt_raw, q, k, v, g_r, ga_m, ga_r = jnp.split(proj, offs, axis=-1)

    xbc = jax.nn.silu(causal_dwconv(xbc, conv_w, conv_b))
    xm, Bm, Cm = jnp.split(xbc, [M_INNER, M_INNER + M_GROUPS * M_STATE], axis=-1)
    xm = xm.reshape(Bsz, L, M_HEADS, M_HEADDIM)
    dt = jax.nn.softplus(dt_raw + dt_bias.astype(f32))
    A = -jnp.exp(a_log.astype(f32))
    y = ssd_chunked(xm, dt, A,
                    Bm.reshape(Bsz, L, M_GROUPS, M_STATE), Cm.reshape(Bsz, L, M_GROUPS, M_STATE))
    y = (y + d_skip.astype(f32)[:, None] * xm).reshape(Bsz, L, M_INNER)
    yz = (y * jax.nn.silu(z_m)).reshape(Bsz, L, M_GROUPS, M_INNER // M_GROUPS)
    y_m = rms_norm(yz).reshape(Bsz, L, M_INNER) * m_norm_w.astype(f32)
    u_m = y_m @ w_proj_m.astype(f32)

    pos = jnp.arange(L, dtype=f32)
    q = rotary(q.reshape(Bsz, L, R_HEADS, R_HEAD_QK), pos)
    k = rotary(k.reshape(Bsz, L, R_HEADS, R_HEAD_QK), pos) * (R_HEAD_QK ** -0.5)
    v = v.reshape(Bsz, L, R_HEADS, R_HEAD_V)
    o = rms_norm(retention_chunked(q, k, v))
    y_r = jax.nn.silu(g_r) * o.reshape(Bsz, L, R_V_DIM)
    u_r = y_r @ w_proj_r.astype(f32)

    merged = jax.nn.sigmoid(ga_m) * u_m + jax.nn.sigmoid(ga_r) * u_r
    out = merged @ w_out.astype(f32)
    return (x.astype(f32) + gate[:, None] * out).astype(x.dtype)


def setup_inputs(seed: int = 0) -> dict:
    key = jax.random.key(seed)
    ks = jax.random.split(key, 18)
    D = D_MODEL
    nrm = jax.random.normal
    x = nrm(ks[0], (BATCH, SEQ, D), jnp.float32)
    c = nrm(ks[1], (BATCH, D), jnp.float32)
    w_ada = nrm(ks[2], (DEPTH, D, 3 * D), jnp.float32) * (0.5 * D ** -0.5)
    b_ada = nrm(ks[3], (DEPTH, 3 * D), jnp.float32) * 0.01
    norm_w = 1.0 + 0.02 * nrm(ks[4], (DEPTH, D), jnp.float32)
    w_in = nrm(ks[5], (DEPTH, D, D_IN_PROJ), jnp.float32) * D ** -0.5
    conv_w = nrm(ks[6], (DEPTH, M_CONV, M_CONV_DIM), jnp.float32) * M_CONV ** -0.5
    conv_b = nrm(ks[7], (DEPTH, M_CONV_DIM), jnp.float32) * 0.01
    dt0 = jnp.exp(jax.random.uniform(ks[8], (DEPTH, M_HEADS), jnp.float32,
                                     jnp.log(1e-3), jnp.log(1e-1)))
    dt_bias = dt0 + jnp.log(-jnp.expm1(-dt0))
    a_log = jnp.log(jax.random.uniform(ks[9], (DEPTH, M_HEADS), jnp.float32, 1.0, 16.0))
    d_skip = 1.0 + 0.02 * nrm(ks[10], (DEPTH, M_HEADS), jnp.float32)
    m_norm_w = 1.0 + 0.02 * nrm(ks[11], (DEPTH, M_INNER), jnp.float32)
    w_proj_m = nrm(ks[12], (DEPTH, M_INNER, D), jnp.float32) * M_INNER ** -0.5
    w_proj_r = nrm(ks[13], (DEPTH, R_V_DIM, D), jnp.float32) * R_V_DIM ** -0.5
    w_out = nrm(ks[14], (DEPTH, D, D), jnp.float32) * D ** -0.5
    final_norm_w = 1.0 + 0.02 * nrm(ks[15], (D,), jnp.float32)
    return {'x': x, 'c': c, 'w_ada': w_ada, 'b_ada': b_ada, 'norm_w': norm_w, 'w_in': w_in,
            'conv_w': conv_w, 'conv_b': conv_b, 'dt_bias': dt_bias, 'a_log': a_log,
            'd_skip': d_skip, 'm_norm_w': m_norm_w, 'w_proj_m': w_proj_m,
            'w_proj_r': w_proj_r, 'w_out': w_out, 'final_norm_w': final_norm_w}


def reference(x, c, w_ada, b_ada, norm_w, w_in, conv_w, conv_b, dt_bias, a_log, d_skip,
              m_norm_w, w_proj_m, w_proj_r, w_out, final_norm_w):
    for l in range(DEPTH):
        x = hybrid_layer(x, c, w_ada[l], b_ada[l], norm_w[l], w_in[l], conv_w[l], conv_b[l],
                         dt_bias[l], a_log[l], d_skip[l], m_norm_w[l], w_proj_m[l],
                         w_proj_r[l], w_out[l])
    return rms_norm(x, final_norm_w).astype(x.dtype)
```

```python
import numpy as np
from contextlib import ExitStack
import concourse.bass as bass
import concourse.mybir as mybir
from concourse.bass_utils import run_bass_kernel_spmd

F32, BF16 = mybir.dt.float32, mybir.dt.bfloat16
AF = mybir.ActivationFunctionType
ALU = mybir.AluOpType
AX = mybir.AxisListType

L, D = 2048, 1024
NCH = 16
NTC = 4
T = NTC * 128
NSC = NCH // NTC
DIN = 7440
NSLOT = 6
REORDER = True
SLACK = 0.0
XLAT = 0.55
ATTACH_WAIT = True
ACT_FREE = True
PESCALE = 0.85
FREEZE = set()
OFF = {'rhsA', 'Wt'}
PSX = True


def OFFE(name):
    return 'pool' if name in OFF else 'dve'
EPS = 1e-6
C_Z, C_XBC, C_DT, C_Q, C_K, C_V, C_G, C_GAM, C_GAR = 0, 1024, 2304, 2320, 2832, 3344, 4368, 5392, 6416
K_DMAT, K_GAM, K_KWV, K_RD, K_U, K_L, K_ONES, K_ID = 0, 1024, 1536, 1544, 1548, 1676, 1804, 1932
K_MSK = 2060
NCST = 2062


ALIAS = {}
for _b in range(10):
    ALIAS["xbcT%d" % _b] = ["u%d" % _b]
ALIAS["qr"] = ["u%d" % i for i in range(10, 14)]
ALIAS["kr"] = ["u%d" % i for i in range(14, 18)]
ALIAS["v"] = ["u%d" % i for i in range(18, 26)]
ALIAS["ymT"] = ["u%d" % i for i in range(0, 8)]
ALIAS["yrT"] = ["u%d" % i for i in range(8, 16)]
ALIAS["mrg"] = ["u%d" % i for i in range(16, 24)]
ALIAS["mT"] = ["u%d" % i for i in range(26, 34)]


class _Op:
    __slots__ = ("eng", "fn", "deps", "idx", "sig", "sigidx", "dmakey", "dmaval", "alldeps", "cidx", "cost", "lat",
                 "aseg", "pos", "fin", "nrem", "users", "ready", "tag", "st", "bl", "ks", "vc")


class _FirstWait:
    def __init__(self, e, sem, val):
        self._e, self._sem, self._val, self._done = e, sem, val, False

    def __getattr__(self, name):
        attr = getattr(self._e, name)
        if not callable(attr):
            return attr

        def call(*a, **k):
            ins = attr(*a, **k)
            if not self._done:
                ins._wait_ge(self._sem, self._val)
                self._done = True
            return ins
        return call


class Sched:
    ENGS = ("pe", "act", "dve", "pool", "sp")

    def __init__(self):
        self.ops = {e: [] for e in self.ENGS}
        self.all = []
        self.lastw = {}
        self.readers = {}
        self.dmacount = {}
        self.aseg = 0
        self.agroup = None

    def add(self, eng, fn, reads=(), writes=(), dma=None, cost=0.1, lat=0.0, agroup=None):
        op = _Op()
        op.eng, op.fn, op.sig, op.sigidx = eng, fn, False, 0
        op.cidx = len(self.all)
        op.cost, op.lat = cost, lat
        import sys as _sys
        fr = _sys._getframe(1)
        while fr.f_code.co_name in ("dma", "act", "tt", "ts", "stt", "cp", "pe", "rstd_pool"):
            fr = fr.f_back
        op.tag = fr.f_lineno
        op.dmakey = dma
        if dma is not None:
            self.dmacount[dma] = self.dmacount.get(dma, 0) + 1
            op.dmaval = 16 * self.dmacount[dma]
        if eng == "act" and agroup is not None and agroup != self.agroup:
            self.agroup = agroup
            self.aseg += 1
        op.aseg = self.aseg if (agroup is not None or not ACT_FREE) else -1
        deps = {}
        reads = [kk for k in reads for kk in ALIAS.get(k, (k,))]
        writes = [kk for k in writes for kk in ALIAS.get(k, (k,))]
        if PSX and eng in ("act", "dve"):
            extra = ["rd_" + k for k in reads if k.startswith("ps")]
            if extra:
                writes = list(writes) + extra

        def consider(d):
            if d is not None and d is not op:
                deps[d.cidx] = d

        for k in reads:
            consider(self.lastw.get(k))
        for k in writes:
            consider(self.lastw.get(k))
            for r in self.readers.get(k, ()):
                consider(r)
        op.alldeps = list(deps.values())
        for k in reads:
            self.readers.setdefault(k, []).append(op)
        for k in writes:
            self.lastw[k] = op
            self.readers[k] = []
        self.ops[eng].append(op)
        self.all.append(op)
        return op

    def reorder(self, enable=True):
        for op in self.all:
            op.users = []
            op.nrem = 0
            op.ready = 0.0
        for op in self.all:
            for d in op.alldeps:
                d.users.append(op)
                op.nrem += 1
        if not enable:
            neword = {e: list(self.ops[e]) for e in self.ENGS}
        else:
            avail = {e: [] for e in self.ENGS}
            for op in self.all:
                if op.nrem == 0:
                    avail[op.eng].append(op)
            tE = {e: 0.0 for e in self.ENGS}
            neword = {e: [] for e in self.ENGS}
            act_rem = {}
            for op in self.ops["act"]:
                if op.aseg >= 0:
                    act_rem[op.aseg] = act_rem.get(op.aseg, 0) + 1
            act_cur = min(act_rem) if act_rem else 0
            nleft = len(self.all)
            ptr = {e: 0 for e in self.ENGS}
            orig = {e: list(self.ops[e]) for e in self.ENGS}
            nxt = {e: (orig[e][0].cidx if orig[e] else -1) for e in self.ENGS}
            for op in reversed(self.all):
                m = 0.0
                for u in op.users:
                    if u.bl > m:
                        m = u.bl
                op.bl = op.cost + op.lat + m
            while nleft:
                best = None
                for e in self.ENGS:
                    te = tE[e]
                    cands = []
                    t0 = None
                    for op in avail[e]:
                        if e == "act" and op.aseg >= 0 and op.aseg != act_cur:
                            continue
                        if e in FREEZE and op.cidx != nxt[e]:
                            continue
                        st = op.ready if op.ready > te else te
                        cands.append((st, op))
                        if t0 is None or st < t0:
                            t0 = st
                    if not cands:
                        continue
                    pick = None
                    for st, op in cands:
                        if st <= t0 + SLACK:
                            if pick is None or (op.bl, -op.cidx) > (pick[1].bl, -pick[1].cidx):
                                pick = (st, op)
                    key = (pick[0], pick[1].cidx)
                    if best is None or key < best[0]:
                        best = (key, pick[1])
                assert best is not None, "scheduler stuck"
                (st, _), op = best
                e = op.eng
                avail[e].remove(op)
                tE[e] = st + op.cost
                op.st = st
                op.fin = st + op.cost + op.lat
                neword[e].append(op)
                nleft -= 1
                if e in FREEZE:
                    ptr[e] += 1
                    nxt[e] = orig[e][ptr[e]].cidx if ptr[e] < len(orig[e]) else -1
                if e == "act" and op.aseg >= 0:
                    act_rem[op.aseg] -= 1
                    while act_rem.get(act_cur, 0) == 0 and act_rem:
                        act_rem.pop(act_cur, None)
                        if not act_rem:
                            break
                        act_cur = min(act_rem)
                for u in op.users:
                    f_ = op.fin + (XLAT if u.eng != op.eng else 0.0)
                    if f_ > u.ready:
                        u.ready = f_
                    u.nrem -= 1
                    if u.nrem == 0:
                        avail[u.eng].append(u)
            self.est = max(tE.values())
        self.ops = neword
        for e in self.ENGS:
            for i, op in enumerate(self.ops[e]):
                op.pos = i
        prev = {}
        for e in self.ENGS:
            p = None
            for op in self.ops[e]:
                prev[id(op)] = p
                p = op
        order = sorted(self.all, key=lambda o: (o.st, o.cidx)) if enable else list(self.all)
        for op in order:
            p = prev[id(op)]
            k = dict(p.ks) if p is not None else {}
            waits = []
            deps = sorted(op.alldeps, key=lambda d: -d.fin) if enable else list(op.alldeps)
            for d in deps:
                if d.dmakey is not None:
                    if k.get("dma:" + d.dmakey, 0) >= d.dmaval:
                        continue
                    waits.append(d)
                    k["dma:" + d.dmakey] = d.dmaval
                    continue
                if d.eng == "pe" and op.eng == "pe":
                    continue
                if k.get(d.eng, -1) >= d.pos:
                    continue
                waits.append(d)
                for kk, vv in d.vc.items():
                    if k.get(kk, -1) < vv:
                        k[kk] = vv
            op.ks = k
            op.deps = waits
            vc = dict(k)
            vc[op.eng] = op.pos
            if op.dmakey is not None:
                vc = {"dma:" + op.dmakey: op.dmaval}
                vc.update({kk: vv for kk, vv in k.items()})
                vc.pop(op.eng, None) if False else None
            op.vc = vc
            for d in waits:
                if d.dmakey is None:
                    d.sig = True
        for e in self.ENGS:
            n = 0
            for op in self.ops[e]:
                if op.sig and op.dmakey is None:
                    n += 1
                    op.sigidx = n

    def emit(self, eng_name, e, engsem, dmasem):
        waited = {}
        for op in self.ops[eng_name]:
            need = []
            for d in op.deps:
                if d.dmakey is not None:
                    sem, val = dmasem[d.dmakey], d.dmaval
                else:
                    if d.eng == "pe" and eng_name == "pe":
                        continue
                    sem, val = engsem[d.eng], d.sigidx
                key = id(sem)
                if waited.get(key, 0) >= val:
                    continue
                waited[key] = val
                need.append((sem, val))
            attach = None
            if ATTACH_WAIT and need and op.fn is not None:
                attach = need.pop()
            for sem, val in need:
                e.wait_ge(sem, val)
            if attach is not None:
                ins = op.fn(_FirstWait(e, attach[0], attach[1]))
            else:
                ins = op.fn(e) if op.fn is not None else None
            if op.dmakey is not None:
                ins.then_inc(dmasem[op.dmakey], 16)
            elif op.sig:
                assert ins is not None
                ins.then_inc(engsem[eng_name], 1)


def build(dbg=False, stop=None, nsc=NSC):
    nc = bass.Bass("TRN2", target_bir_lowering=False)
    S = Sched()

    def din(name, shape):
        return nc.dram_tensor(name, list(shape), F32, kind="ExternalInput").ap()

    x_d = din("x", [L, D])
    cT_d = din("cT", [128, 16])
    wada_d = din("w_ada", [D, 3 * D])
    badaT_d = din("b_adaT", [128, 24])
    bgate_d = din("b_gate", [1, D])
    nwT_d = din("norm_wT", [128, 8])
    win_d = din("w_in", [D, DIN])
    cwT_d = din("conv_wT", [128, 40])
    cbT_d = din("conv_bT", [128, 10])
    dtb_d = din("dt_bias", [1, 16])
    alog_d = din("a_log", [1, 16])
    dsk_d = din("d_skip", [1, 16])
    dskT_d = din("dskT", [128, 8])
    mnwT_d = din("m_norm_wT", [128, 8])
    wm_d = din("w_proj_m", [D, D])
    wr_d = din("w_proj_r", [D, D])
    wo_d = din("w_out", [D, D])
    fnw_d = din("fnw", [1, D])
    cos_d = din("cosT", [L, 32])
    sin_d = din("sinT", [L, 32])
    cst_d = din("cst", [128, NCST])
    out_d = nc.dram_tensor("out", [L, D], F32, kind="ExternalOutput").ap()
    dbg_outs = {}

    es = ExitStack()

    def sb(name, shape, dt):
        return es.enter_context(nc.sbuf_tensor("s_" + name, list(shape), dt))

    def rowsize(h):
        r = 1
        for s in h.shape[1:]:
            r *= s
        return r

    def mk(h, p0, npart, off, dims):
        return bass.AP(h, p0 * rowsize(h) + off, [[rowsize(h), npart]] + [list(d) for d in dims])

    cst = sb("cst", [128, NCST], F32)
    ident_b = sb("ident_b", [128, 128], BF16)
    vec = sb("vec", [128, 144], F32)
    V_CT, V_BADA, V_NW, V_CW, V_CB, V_MNW, V_G, V_SH, V_NH, V_DSK = 0, 16, 40, 48, 88, 98, 106, 114, 122, 128
    rowv = sb("rowv", [128, 64], F32)
    gate_bc = sb("gate_bc", [128, D], F32)
    fnw_bc = sb("fnw_bc", [128, D], F32)
    wdt = sb("wdt", [128, 8, 16], BF16)
    ring = sb("ring", [128, NSLOT * 2048], F32)
    x1 = sb("x1", [128, 2, D], F32)
    xo = sb("xo", [128, 1, D], F32)
    xs2 = sb("xs", [128, 2, D], BF16)
    junk = xs2[:, 0, :]
    hT = sb("hT", [128, 8, T], BF16)
    halo = sb("halo", [128, 10, 3], F32)
    rawt = sb("rawt", [128, 2, T + 3], F32)
    acc = sb("acc", [128, 2, T], F32)
    cs_t = sb("cs_t", [128, 2, NTC, 32], F32)
    dtt = sb("dtt", [128, NTC, 16], F32)
    uni = sb("uni", [128, 17408], BF16)
    y = sb("y", [128, NTC, D], F32)
    on = sb("on", [128, NTC, D], BF16)
    st = sb("st", [128, 64], F32)
    sm = sb("sm", [128, 128], F32)
    rhsA = sb("rhsA", [128, 1, 8, 128], F32)
    expseg = sb("expseg", [128, 2, 8, 128], BF16)
    Gm = sb("Gm", [128, 2, 128], F32)
    Wt = sb("Wt", [128, 16, 128], BF16)
    x_dt = sb("x_dt", [128, D], BF16)
    xw = sb("xw", [128, D], BF16)
    DD = sb("DD", [128, 8, 128], BF16)
    junk2 = sb("junk2", [128, 128], BF16)
    cTb = sb("cTb", [128, 16], BF16)
    Btok = sb("Btok", [128, 128], BF16)
    t1 = sb("t1", [128, D], F32)
    Sst = sb("Sst", [128, 512], F32)
    Sbf = sb("Sbf", [128, 512], BF16)
    qT = sb("qT", [128, 4, 128], BF16)
    kT = sb("kT", [128, 2, 4, 128], BF16)
    BCz = sb("BCz", [128, 1, 2, T], BF16)
    qwT = sb("qwT", [128, 4, 128], BF16)
    kw = sb("kw", [128, 512], BF16)
    PT = sb("PT", [128, 8, 128], BF16)
    Rst = sb("Rst", [128, 4, 128], F32)
    Rbf = sb("Rbf", [128, 2, 4, 128], BF16)
    sz = sb("sz", [128, 2, 512], F32)
    ymb = sb("ymb", [128, 2, 512], BF16)
    tmg = sz
    amr = sb("amr", [128, 2, 512], F32)

    def uview(off, dims):
        return mk(uni, 0, 128, off, dims)

    O_XBC, O_QR, O_KR, O_V = 0, 10 * T, 10 * T + NTC * 512, 10 * T + 2 * NTC * 512
    O_YMT, O_YRT, O_MRG, O_MT = 0, 8 * T, 16 * T, 26 * 512
    assert O_V + NTC * 1024 <= O_MT and O_MRG + NTC * 1024 <= O_MT and O_MT + 8 * T <= 17408

    PS = [es.enter_context(nc.psum_tensor(f"ps{i}", [128, 512], F32)) for i in range(8)]

    def psb(i):
        return PS[i][:].bitcast(BF16)

    engsem = {e: es.enter_context(nc.semaphore("sem_" + e)) for e in ("pe", "act", "dve", "pool")}
    dmasem = {}

    def dsem(key):
        if key not in dmasem:
            dmasem[key] = es.enter_context(nc.semaphore("d_" + key))
        return key

    def fsz(ap):
        n = 1
        for d in ap.shape[1:]:
            n *= d
        return n

    def dma(eng, out, in_, reads, writes, key, nbytes=None):
        dsem(key)
        if nbytes is None:
            nbytes = 128 * fsz(out) * 4
        return S.add(eng, lambda e, o=out, i=in_: e.dma_start(out=o, in_=i), reads, writes, dma=key,
                     cost=(1.2 if eng == "pool" else 0.15), lat=2.0 + nbytes / 120e3)

    AGROUP = {AF.Silu: "g18", AF.Tanh: "g18", AF.Exp: "g6", AF.Ln: "g6"}

    def act(out, in_, func, reads, writes, **kw):
        return S.add("act", lambda e: e.activation(out=out, in_=in_, func=func, **kw), reads, writes,
                     cost=0.22 + fsz(out) / 1200.0, agroup=AGROUP.get(func))

    def ecost(eng, out):
        n = fsz(out)
        return (0.07 + n / 960.0) if eng == "dve" else (0.15 + n / 450.0)

    def tt(eng, out, in0, in1, op, reads, writes):
        return S.add(eng, lambda e: e.tensor_tensor(out=out, in0=in0, in1=in1, op=op), reads, writes, cost=ecost(eng, out))

    def ts(eng, out, in0, s1, s2, op0, op1, reads, writes):
        c = ecost(eng, out)
        if s2 is None:
            return S.add(eng, lambda e: e.tensor_scalar(out=out, in0=in0, scalar1=s1, scalar2=None, op0=op0), reads, writes, cost=c)
        return S.add(eng, lambda e: e.tensor_scalar(out=out, in0=in0, scalar1=s1, scalar2=s2, op0=op0, op1=op1), reads, writes, cost=c)

    def stt(out, in0, scalar, in1, op0, op1, reads, writes):
        return S.add("dve", lambda e: e.scalar_tensor_tensor(out=out, in0=in0, scalar=scalar, in1=in1, op0=op0, op1=op1), reads, writes,
                     cost=ecost("dve", out))

    def cp(eng, out, in_, reads, writes):
        return S.add(eng, lambda e: e.tensor_copy(out=out, in_=in_), reads, writes, cost=ecost(eng, out))

    def pe(fn, reads, writes, cost=1.0):
        return S.add("pe", fn, reads, writes, cost=cost * PESCALE)

    def rstd_pool(dst, src, inv_n, reads_key, write_key, tmpcol, reads=None):
        tmp = st[:, tmpcol:tmpcol + src.shape[1]]
        ts("pool", tmp, src, inv_n, EPS, ALU.mult, ALU.add, reads if reads is not None else [reads_key], ["st_tmp%d" % tmpcol])
        nh = vec[:, V_NH:V_NH + 1].to_broadcast([128, src.shape[1]]) if src.shape[1] > 1 else vec[:, V_NH:V_NH + 1]
        tt("pool", dst, tmp, nh, ALU.pow, ["st_tmp%d" % tmpcol, "vec"], [write_key])

    def dump(name, src, shape, reads, dt=F32):
        if not dbg:
            return
        d = nc.dram_tensor("dbg_" + name, list(shape), dt, kind="ExternalOutput").ap()
        dbg_outs[name] = d
        dma("sp", d, src, reads, ["dbgout_" + name], "dbg_" + name)

    wreq = []
    for cb in range(4):
        wreq.append((wada_d, cb * 512, 512))
    for sc in range(NSC):
        for (c0, n) in ((C_XBC, 512), (C_XBC + 512, 512), (C_XBC + 1024, 256), (C_Q, 512), (C_K, 512), (C_V, 512), (C_V + 512, 512),
                        (C_Z, 512), (C_Z + 512, 512), (C_G, 512), (C_G + 512, 512)):
            wreq.append((win_d, c0, n))
        if sc == 0:
            wreq.append((wada_d, 4 * 512, 512))
            wreq.append((wada_d, 5 * 512, 512))
        for hf in range(2):
            wreq.append((wm_d, hf * 512, 512))
            wreq.append((win_d, C_GAM + hf * 512, 512))
            wreq.append((wr_d, hf * 512, 512))
            wreq.append((win_d, C_GAR + hf * 512, 512))
        wreq.append((wo_d, 0, 512))
        wreq.append((wo_d, 512, 512))
    wstate = {"issued": 0, "next": 0}

    def slot_ap(s):
        return ring[:, s * 2048:(s + 1) * 2048].bitcast(BF16).rearrange("p (k n) -> p k n", k=8)

    def wissue_upto(r):
        while wstate["issued"] <= min(r, len(wreq) - 1):
            i = wstate["issued"]
            src, c0, n = wreq[i]
            s = i % NSLOT
            srcap = src.rearrange("(kc p) n -> p kc n", p=128)[:, :, c0:c0 + n]
            dma("pool", slot_ap(s)[:, :, 0:n], srcap, [], ["ring%d" % s], "ring%d" % s)
            wstate["issued"] += 1

    def wgroup(specs):
        r0 = wstate["next"]
        wissue_upto(r0 + NSLOT - 1)
        outl = []
        for (expect_src, expect_c0) in specs:
            r = wstate["next"]
            src, c0, n = wreq[r]
            assert src is expect_src and c0 == expect_c0, (r, c0, expect_c0)
            assert r <= r0 + NSLOT - 1
            wstate["next"] += 1
            s = r % NSLOT
            outl.append((slot_ap(s), "ring%d" % s))
        return outl

    def wnext(expect_src, expect_c0):
        return wgroup([(expect_src, expect_c0)])[0]

    dma("sp", cst[:], cst_d[:, :], [], ["cst"], "cst")
    dma("sp", vec[:, V_CT:V_CT + 16], cT_d[:, :], [], ["vec"], "v0")
    dma("sp", vec[:, V_BADA:V_BADA + 24], badaT_d[:, :], [], ["vec"], "v1")
    dma("sp", vec[:, V_NW:V_NW + 8], nwT_d[:, :], [], ["vec"], "v2")
    dma("sp", vec[:, V_CW:V_CW + 40], cwT_d[:, :], [], ["vec"], "v3")
    dma("sp", vec[:, V_CB:V_CB + 10], cbT_d[:, :], [], ["vec"], "v4")
    dma("sp", vec[:, V_MNW:V_MNW + 8], mnwT_d[:, :], [], ["vec"], "v5")
    dma("sp", vec[:, V_DSK:V_DSK + 8], dskT_d[:, :], [], ["vec"], "v6")
    dma("sp", rowv[:, 0:16], bass.AP(dtb_d.tensor, 0, [[0, 128], [1, 16]]), [], ["rowv"], "r0")
    dma("sp", rowv[:, 48:64], bass.AP(alog_d.tensor, 0, [[0, 128], [1, 16]]), [], ["rowv"], "r1")
    dma("sp", rowv[:, 32:48], bass.AP(dsk_d.tensor, 0, [[0, 128], [1, 16]]), [], ["rowv"], "r2")
    dma("sp", gate_bc[:], bass.AP(bgate_d.tensor, 0, [[0, 128], [1, D]]), [], ["gate_bc"], "gb")
    dma("sp", fnw_bc[:], bass.AP(fnw_d.tensor, 0, [[0, 128], [1, D]]), [], ["fnw_bc"], "fb")
    dma("pool", wdt[:], win_d.rearrange("(kc p) n -> p kc n", p=128)[:, :, C_DT:C_DT + 16], [], ["wdt"], "wdt")
    S.add("pool", lambda e: e.memset(vec[:, V_NH:V_NH + 1], -0.5), [], ["vec"])
    S.add("pool", lambda e: e.memset(halo[:], 0.0), [], ["halo"])
    S.add("pool", lambda e: e.memset(Sst[:], 0.0), [], ["Sst"])
    S.add("pool", lambda e: e.memset(Sbf[:], 0.0), [], ["Sbf"])
    S.add("pool", lambda e: e.memset(Rst[:], 0.0), [], ["Rst"])
    S.add("pool", lambda e: e.memset(Rbf[:], 0.0), [], ["Rbf"])
    cp("dve", ident_b[:], cst[:, K_ID:K_ID + 128], ["cst"], ["ident_b"])
    act(rowv[:, 16:32], rowv[:, 48:64], AF.Exp, ["rowv"], ["rowv"])
    ts("dve", rowv[:, 16:32], rowv[:, 16:32], -1.0, None, ALU.mult, None, ["rowv"], ["rowv"])

    U_f = cst[:, K_U:K_U + 128]
    L_f = cst[:, K_L:K_L + 128]
    ones_f = cst[:, K_ONES:K_ONES + 128]

    cp("dve", cTb[:], vec[:, V_CT:V_CT + 16], ["vec"], ["cTb"])
    cbc = amr[:, 0, :].bitcast(BF16).rearrange("p (k m) -> p k m", k=8)
    for kc in range(8):
        cp("dve", cbc[:, kc, :], vec[:, V_CT + kc:V_CT + kc + 1].to_broadcast([128, 128]), ["vec"], ["amr0"])
    for blk in range(8):
        ts("dve", DD[:, blk, :], ident_b[:], vec[:, V_DSK + blk:V_DSK + blk + 1], None, ALU.mult, None, ["ident_b", "vec"], ["DD"])
    ps_mod = PS[3]
    for cb in range(4):
        W, wkey = wnext(wada_d, cb * 512)

        def f(e, W=W, cb=cb):
            ins = None
            for sub in range(4):
                j = cb * 4 + sub
                for kc in range(8):
                    ins = e.matmul(ps_mod[:, 2 * j:2 * j + 2], lhsT=W[:, kc, sub * 128:(sub + 1) * 128],
                                   rhs=cTb[:, kc:kc + 2], start=(kc == 0), stop=(kc == 7))
            return ins
        pe(f, [wkey, "cTb"], ["ps3"], cost=2.6)

    def gate_setup():
        for hf in range(2):
            W, wkey = wnext(wada_d, (4 + hf) * 512)

            def f(e, W=W, hf=hf):
                ins = None
                for kc in range(8):
                    ins = e.matmul(PS[hf][:], lhsT=cbc[:, kc, :], rhs=W[:, kc, :], start=(kc == 0), stop=(kc == 7))
                return ins
            pe(f, [wkey, "amr0"], ["ps%d" % hf], cost=2.1)
            tt("dve", gate_bc[:, hf * 512:(hf + 1) * 512], PS[hf][:], gate_bc[:, hf * 512:(hf + 1) * 512], ALU.add, ["ps%d" % hf, "gate_bc"], ["gate_bc"])
        ts("dve", gate_bc[:], gate_bc[:], 0.5, None, ALU.mult, None, ["gate_bc"], ["gate_bc"])
    modv = mk(ps_mod, 0, 128, 0, [[2, 16]])
    tt("dve", st[:, 0:16], modv, vec[:, V_BADA:V_BADA + 16], ALU.add, ["ps3", "vec"], ["st_mod"])
    cp("dve", vec[:, V_SH:V_SH + 8], st[:, 0:8], ["st_mod"], ["vec"])
    stt(vec[:, V_G:V_G + 8], st[:, 8:16], 1.0, vec[:, V_NW:V_NW + 8], ALU.add, ALU.mult, ["st_mod", "vec"], ["vec"])

    xbcT = uview(O_XBC, [[T, 10], [1, T]])
    qr = uview(O_QR, [[512, NTC], [1, 512]])
    kr = uview(O_KR, [[512, NTC], [1, 512]])
    vv = uview(O_V, [[1024, NTC], [1, 1024]])
    ymT = uview(O_YMT, [[T, 8], [1, T]])
    yrT = uview(O_YRT, [[T, 8], [1, T]])
    mrg = uview(O_MRG, [[1024, NTC], [1, 1024]])
    mT = uview(O_MT, [[T, 8], [1, T]])
    A_KEYS = ["xbcT%d" % b for b in range(10)] + ["qr", "kr", "v"]
    B_KEYS = ["ymT", "yrT", "mrg", "mT"]

    mmbank = {"i": 0}

    def nextbank(banks=(0, 1)):
        b = banks[mmbank["i"] % len(banks)]
        mmbank["i"] += 1
        return b

    for sc in range(nsc if stop != 'setup' else 0):
        dma("sp", cs_t[:, 0, :, :], cos_d.rearrange("(c p) f -> p c f", p=128)[:, sc * NTC:(sc + 1) * NTC, :], [], ["cs_t"], "cs0")
        dma("sp", cs_t[:, 1, :, :], sin_d.rearrange("(c p) f -> p c f", p=128)[:, sc * NTC:(sc + 1) * NTC, :], [], ["cs_t"], "cs1")
        for c in range(NTC):
            gc = sc * NTC + c
            b = gc % 2
            dma("sp", x1[:, b, :], x_d[gc * 128:(gc + 1) * 128, :], [], ["x1_%d" % b], "x1_%d" % b)
            xs = xs2[:, b, :]
            xk = "xs%d" % b
            sc0 = 16 + 4 * b
            act(xs, x1[:, b, :], AF.Square, ["x1_%d" % b], [xk, "st_ss%d" % b], accum_out=st[:, sc0:sc0 + 1])
            rstd_pool(st[:, sc0 + 1:sc0 + 2], st[:, sc0:sc0 + 1], 1.0 / D, "st_ss%d" % b, "st_rstd%d" % b, sc0 + 2)
            act(xs, x1[:, b, :], AF.Copy, ["x1_%d" % b, "st_rstd%d" % b], [xk], scale=st[:, sc0 + 1:sc0 + 2])
            pb = 2 + b

            def f(e, xs=xs, pb=pb):
                ins = None
                for kc in range(8):
                    ins = e.transpose(psb(pb)[:, kc * 128:(kc + 1) * 128], xs[:, kc * 128:(kc + 1) * 128], ident_b[:])
                return ins
            pe(f, [xk, "ident_b"], ["ps%d" % pb], cost=1.0)
            for kc in range(8):
                if kc % 2 == 0:
                    act(hT[:, kc, c * 128:(c + 1) * 128], psb(pb)[:, kc * 128:(kc + 1) * 128], AF.Identity, ["ps%d" % pb, "vec"], ["hT%d" % c],
                        scale=vec[:, V_G + kc:V_G + kc + 1], bias=vec[:, V_SH + kc:V_SH + kc + 1])
                else:
                    ts("dve", hT[:, kc, c * 128:(c + 1) * 128], psb(pb)[:, kc * 128:(kc + 1) * 128],
                       vec[:, V_G + kc:V_G + kc + 1], vec[:, V_SH + kc:V_SH + kc + 1], ALU.mult, ALU.add,
                       ["ps%d" % pb, "vec"], ["hT%d" % c])
        HT_KEYS = ["hT%d" % c for c in range(NTC)]
        if stop == 'p1':
            continue
        if sc == 0:
            dump("hT", hT[:], [128, 8, T], HT_KEYS, BF16)

        wslots = wgroup([(win_d, C_XBC), (win_d, C_XBC + 512), (win_d, C_XBC + 1024)])
        for blk in range(10):
            W, wkey = wslots[blk // 4]
            sub = blk % 4
            bk = nextbank((4, 5, 6, 7))
            rb = blk % 2

            def f(e, W=W, sub=sub, bk=bk):
                ins = None
                for kc in range(8):
                    ins = e.matmul(PS[bk][:], lhsT=W[:, kc, sub * 128:(sub + 1) * 128], rhs=hT[:, kc, :], start=(kc == 0), stop=(kc == 7))
                return ins
            pe(f, [wkey] + HT_KEYS, ["ps%d" % bk], cost=2.1)
            cp("dve", rawt[:, rb, 0:3], halo[:, blk, :], ["halo"], ["rawt%d" % rb])
            act(rawt[:, rb, 3:3 + T], PS[bk][:], AF.Copy, ["ps%d" % bk], ["rawt%d" % rb])
            cp("dve", halo[:, blk, :], rawt[:, rb, T:T + 3], ["rawt%d" % rb], ["halo"])
            act(acc[:, rb, :], PS[bk][:], AF.Identity, ["ps%d" % bk, "vec"], ["acc%d" % rb],
                scale=vec[:, V_CW + blk * 4 + 3:V_CW + blk * 4 + 4], bias=vec[:, V_CB + blk:V_CB + blk + 1])
            for s_ in (1, 2, 3):
                stt(acc[:, rb, :], rawt[:, rb, 3 - s_:3 - s_ + T], vec[:, V_CW + blk * 4 + 3 - s_:V_CW + blk * 4 + 4 - s_],
                    acc[:, rb, :], ALU.mult, ALU.add, ["rawt%d" % rb, "acc%d" % rb, "vec"], ["acc%d" % rb])
            act(xbcT[:, blk, :], acc[:, rb, :], AF.Silu, ["acc%d" % rb], ["xbcT%d" % blk])
        if sc == 0:
            dump("xbcT", xbcT, [128, 10, T], ["xbcT%d" % b for b in range(10)], BF16)
        for g in range(2):
            act(BCz[:, 0, g, :], xbcT[:, 9, :], AF.Copy, ["xbcT9", "cst"], ["Cz"], scale=cst[:, K_MSK + g:K_MSK + g + 1])
        if stop == 'p2a':
            continue
        for c in range(NTC):
            def f(e, c=c):
                ins = None
                for kc in range(8):
                    ins = e.matmul(PS[3][:, 256:272], lhsT=hT[:, kc, c * 128:(c + 1) * 128], rhs=wdt[:, kc, :], start=(kc == 0), stop=(kc == 7))
                return ins
            pe(f, ["hT%d" % c, "wdt"], ["ps3"], cost=0.6)
            tt("dve", sm[:, 96:112], PS[3][:, 256:272], rowv[:, 0:16], ALU.add, ["ps3", "rowv"], ["sm_dtpre"])
            act(sm[:, 112:128], sm[:, 96:112], AF.Exp, ["sm_dtpre"], ["sm_e"])
            act(dtt[:, c, :], sm[:, 112:128], AF.Ln, ["sm_e"], ["dtt%d" % c], bias=1.0)
        if sc == 0:
            dump("dt", dtt[:], [128, NTC, 16], ["dtt%d" % c for c in range(NTC)])
        for (dst, dkey, c0) in ((qr, "qr", C_Q), (kr, "kr", C_K)):
            W, wkey = wnext(win_d, c0)
            for c in range(NTC):
                bk = nextbank()

                def f(e, W=W, c=c, bk=bk):
                    ins = None
                    for kc in range(8):
                        ins = e.matmul(PS[bk][:], lhsT=hT[:, kc, c * 128:(c + 1) * 128], rhs=W[:, kc, :], start=(kc == 0), stop=(kc == 7))
                    return ins
                pe(f, [wkey, "hT%d" % c], ["ps%d" % bk], cost=2.1)
                psv = mk(PS[bk], 0, 128, 0, [[64, 8], [32, 2], [1, 32]])
                cosb = mk(cs_t, 0, 128, c * 32, [[0, 8], [0, 2], [1, 32]])
                sinb = mk(cs_t, 0, 128, NTC * 32 + c * 32, [[0, 8], [0, 2], [1, 32]])
                tc_ = mk(t1, 0, 128, 0, [[64, 8], [32, 2], [1, 32]])
                ts_ = mk(t1, 0, 128, 512, [[64, 8], [32, 2], [1, 32]])
                tt("dve", tc_, psv, cosb, ALU.mult, ["ps%d" % bk, "cs_t"], ["t1a"])
                tt("dve", ts_, psv, sinb, ALU.mult, ["ps%d" % bk, "cs_t"], ["t1b"])
                dv = mk(uni, 0, 128, (O_QR if dkey == "qr" else O_KR) + c * 512, [[64, 8], [32, 2], [1, 32]])
                tt("dve", dv[:, :, 0, :], tc_[:, :, 0, :], ts_[:, :, 1, :], ALU.subtract, ["t1a", "t1b"], [dkey])
                tt("dve", dv[:, :, 1, :], ts_[:, :, 0, :], tc_[:, :, 1, :], ALU.add, ["t1a", "t1b"], [dkey])
        for half in range(2):
            W, wkey = wnext(win_d, C_V + half * 512)
            for c in range(NTC):
                bk = nextbank((4, 5, 6, 7))

                def f(e, W=W, c=c, bk=bk):
                    ins = None
                    for kc in range(8):
                        ins = e.matmul(PS[bk][:], lhsT=hT[:, kc, c * 128:(c + 1) * 128], rhs=W[:, kc, :], start=(kc == 0), stop=(kc == 7))
                    return ins
                pe(f, [wkey, "hT%d" % c], ["ps%d" % bk], cost=2.1)
                act(vv[:, c, half * 512:(half + 1) * 512], PS[bk][:], AF.Copy, ["ps%d" % bk], ["v"])
        if sc == 0:
            dump("qr", qr, [128, NTC, 512], ["qr"], BF16)
            dump("kr", kr, [128, NTC, 512], ["kr"], BF16)

        if stop == 'p2':
            continue
        for c in range(NTC):
            cs_ = slice(c * 128, (c + 1) * 128)
            a_sb, acs_sb, dE2, E1, CD, dtE2 = sm[:, 0:16], sm[:, 16:32], sm[:, 32:48], sm[:, 48:64], sm[:, 64:80], sm[:, 80:96]
            tt("dve", a_sb, dtt[:, c, :], rowv[:, 16:32], ALU.mult, ["dtt%d" % c, "rowv"], ["sm_a"])

            def f(e):
                e.matmul(PS[3][:, 256:272], lhsT=U_f, rhs=a_sb, start=True, stop=True)
                return e.matmul(PS[3][:, 272:288], lhsT=ones_f, rhs=a_sb, start=True, stop=True)
            pe(f, ["sm_a", "cst"], ["ps3"], cost=0.3)
            act(acs_sb, PS[3][:, 256:272], AF.Copy, ["ps3"], ["sm_acs"])
            tt("dve", dE2, PS[3][:, 272:288], acs_sb, ALU.subtract, ["ps3", "sm_acs"], ["sm_d"])
            act(dE2, dE2, AF.Exp, ["sm_d"], ["sm_d"])
            act(E1, acs_sb, AF.Exp, ["sm_acs"], ["sm_E1"])
            act(CD, PS[3][:, 272:288], AF.Exp, ["ps3"], ["sm_CD"])
            tt("dve", dtE2, dtt[:, c, :], dE2, ALU.mult, ["dtt%d" % c, "sm_d"], ["sm_dtE2"])

            if stop == 'p3a':
                continue
            def f(e, cs_=cs_):
                ins = None
                for blk in range(8):
                    ins = e.transpose(psb(2)[:, blk * 128:(blk + 1) * 128], xbcT[:, blk, cs_], ident_b[:])
                ins = e.transpose(psb(3)[:, 640:768], xbcT[:, 8, cs_], ident_b[:])
                return ins
            pe(f, ["xbcT%d" % b for b in range(9)] + ["ident_b"], ["ps2", "ps3"], cost=1.1)
            psxb = psb(2).rearrange("p (h q) -> p h q", h=16)

            def bc16(base_ap_tensor, col0):
                return mk(base_ap_tensor, 0, 128, col0, [[1, 16], [0, 64]])
            tt("dve", x_dt[:].rearrange("p (h q) -> p h q", h=16), psxb, mk(dtt, 0, 128, c * 16, [[1, 16], [0, 64]]), ALU.mult, ["ps2", "dtt%d" % c], ["x_dt"])
            tt("dve", xw[:].rearrange("p (h q) -> p h q", h=16), psxb, bc16(sm, 80), ALU.mult, ["ps2", "sm_dtE2"], ["xw"])
            act(Btok[:], psb(3)[:, 640:768], AF.Copy, ["ps3"], ["Btok"])

            if stop == 'p3b':
                continue
            def f(e, cs_=cs_):
                ins = None
                for g in range(2):
                    ins = e.matmul(PS[g][:, 0:128], lhsT=xbcT[:, 8, cs_], rhs=BCz[:, 0, g, cs_], start=True, stop=True)
                return ins
            pe(f, ["xbcT8", "Cz"], ["ps0", "ps1"], cost=0.2)
            for g in range(2):
                tt("dve", Gm[:, g, :], PS[g][:, 0:128], U_f, ALU.mult, ["ps%d" % g, "cst"], ["Gm%d" % g])

            for g in range(2):
                tt(OFFE("rhsA"), rhsA[:, 0, :, :], mk(cst, 0, 128, K_U, [[0, 8], [1, 128]]), mk(sm, 0, 128, g * 8, [[1, 8], [0, 128]]), ALU.mult, ["cst", "sm_a"], ["rhsA"])

                def f(e, g=g):
                    e.matmul(PS[4][:], lhsT=L_f, rhs=rhsA[:, 0, 0:4, :].rearrange("p h i -> p (h i)"), start=True, stop=True)
                    return e.matmul(PS[5][:], lhsT=L_f, rhs=rhsA[:, 0, 4:8, :].rearrange("p h i -> p (h i)"), start=True, stop=True)
                pe(f, ["rhsA", "cst"], ["ps4", "ps5"], cost=1.8)
                act(expseg[:, g, 0:4, :].rearrange("p h i -> p (h i)"), PS[4][:], AF.Exp, ["ps4"], ["expseg%da" % g])
                act(expseg[:, g, 4:8, :].rearrange("p h i -> p (h i)"), PS[5][:], AF.Exp, ["ps5"], ["expseg%db" % g])
                tt(OFFE("Wt"), Wt[:, g * 8:(g + 1) * 8, :], expseg[:, g, :, :], mk(Gm, 0, 128, g * 128, [[0, 8], [1, 128]]), ALU.mult,
                   ["expseg%da" % g, "expseg%db" % g, "Gm%d" % g], ["Wt%d" % g])

            if stop == 'p3c':
                continue
            for g in range(2):
                def f(e, g=g, cs_=cs_):
                    for j in range(4):
                        e.matmul(PS[6 + g][:, j * 128:(j + 1) * 128], lhsT=xbcT[:, g * 4 + j, cs_], rhs=DD[:, g * 4 + j, :], start=(j == 0), stop=False)
                    ins = None
                    for hl in range(8):
                        h = g * 8 + hl
                        ins = e.matmul(PS[6 + g][:, hl * 64:(hl + 1) * 64], lhsT=Wt[:, h, :], rhs=x_dt[:, h * 64:(h + 1) * 64], start=False, stop=(hl == 7))
                    return ins
                pe(f, ["DD", "x_dt", "Wt%d" % g] + ["xbcT%d" % (g * 4 + j) for j in range(4)], ["ps%d" % (6 + g)], cost=1.1)

                def f(e, g=g, cs_=cs_):
                    return e.matmul(PS[g][:], lhsT=BCz[:, 0, g, cs_], rhs=Sbf[:, :], start=True, stop=True)
                pe(f, ["Cz", "Sbf"], ["ps%d" % g], cost=0.27)
                tt("dve", t1[:, g * 512:(g + 1) * 512].rearrange("p (h q) -> p h q", h=8), PS[g][:].rearrange("p (h q) -> p h q", h=8),
                   mk(sm, 0, 128, 48 + g * 8, [[1, 8], [0, 64]]), ALU.mult, ["ps%d" % g, "sm_E1"], ["t1a" if g == 0 else "t1b"])
                tt("dve", y[:, c, g * 512:(g + 1) * 512], PS[6 + g][:], t1[:, g * 512:(g + 1) * 512], ALU.add,
                   ["ps%d" % (6 + g), "t1a" if g == 0 else "t1b"], ["y%d" % c])
            if stop == 'p3d':
                continue
            for g in range(2):
                def f(e, g=g):
                    return e.matmul(PS[4 + g][:], lhsT=Btok[:], rhs=xw[:, g * 512:(g + 1) * 512], start=True, stop=True)
                pe(f, ["Btok", "xw"], ["ps%d" % (4 + g)], cost=0.27)
                r0 = g * 64
                tt(OFFE("SstCD"), mk(Sst, r0, 64, 0, [[64, 8], [1, 64]]), mk(Sst, r0, 64, 0, [[64, 8], [1, 64]]), mk(sm, r0, 64, 64 + g * 8, [[1, 8], [0, 64]]),
                   ALU.mult, ["Sst", "sm_CD"], ["Sst"])
                tt("dve", Sst[r0:r0 + 64, :], Sst[r0:r0 + 64, :], PS[4 + g][r0:r0 + 64, :], ALU.add, ["Sst", "ps%d" % (4 + g)], ["Sst"])
                act(Sbf[r0:r0 + 64, :], Sst[r0:r0 + 64, :], AF.Copy, ["Sst"], ["Sbf"])

            if stop == 'p3e':
                continue
            def f(e, c=c):
                ins = None
                for blk in range(4):
                    e.transpose(psb(2)[:, blk * 128:(blk + 1) * 128], qr[:, c, blk * 128:(blk + 1) * 128], ident_b[:])
                    ins = e.transpose(psb(2)[:, 512 + blk * 128:512 + (blk + 1) * 128], kr[:, c, blk * 128:(blk + 1) * 128], ident_b[:])
                return ins
            pe(f, ["qr", "kr", "ident_b"], ["ps2"], cost=1.0)
            act(qT[:].rearrange("p b i -> p (b i)"), psb(2)[:, 0:512], AF.Copy, ["ps2"], ["qT"])
            for hh in range(2):
                act(kT[:, hh, :, :].rearrange("p b i -> p (b i)"), psb(2)[:, 512:1024], AF.Copy, ["ps2", "cst"], ["kT"], scale=cst[:, K_MSK + hh:K_MSK + hh + 1])
            tt("dve", qwT[:].rearrange("p b i -> p (b i)"), psb(2)[:, 0:512], cst[:, K_GAM:K_GAM + 512], ALU.mult, ["ps2", "cst"], ["qwT"])
            tt(OFFE("kw"), kw[:].rearrange("p (h d) -> p h d", h=8), kr[:, c, :].rearrange("p (h d) -> p h d", h=8),
               mk(cst, 0, 128, K_KWV, [[1, 8], [0, 64]]), ALU.mult, ["kr", "cst"], ["kw"])

            def f(e):
                ins = None
                for h in range(8):
                    blk, hh = h // 2, h % 2
                    ins = e.matmul(PS[hh][:, blk * 128:(blk + 1) * 128], lhsT=kT[:, hh, blk, :],
                                   rhs=qT[:, blk, :], start=True, stop=True)
                return ins
            pe(f, ["qT", "kT"], ["ps0", "ps1"], cost=0.8)
            for bk in range(2):
                tt("dve", mk(PT, 0, 128, bk * 128, [[256, 4], [1, 128]]), PS[bk][:].rearrange("p (b i) -> p b i", b=4),
                   mk(cst, 0, 128, K_DMAT + bk * 128, [[256, 4], [1, 128]]), ALU.mult, ["ps%d" % bk, "cst"], ["PT%d" % bk])

            if stop == 'p3f':
                continue
            def f(e, c=c):
                ins = None
                for h in range(8):
                    blk, hh = h // 2, h % 2
                    o_ = PS[6 + hh][:, blk * 128:(blk + 1) * 128]
                    e.matmul(o_, lhsT=qwT[:, blk, :], rhs=Rbf[:, hh, blk, :], start=True, stop=False)
                    ins = e.matmul(o_, lhsT=PT[:, h, :], rhs=vv[:, c, h * 128:(h + 1) * 128], start=False, stop=True)
                return ins
            pe(f, ["qwT", "Rbf", "PT0", "PT1", "v"], ["ps6", "ps7"], cost=1.6)

            def f(e, c=c):
                ins = None
                for h in range(8):
                    blk = h // 2
                    ins = e.matmul(PS[4 + h // 4][:, (h % 4) * 128:(h % 4 + 1) * 128], lhsT=kw[:, blk * 128:(blk + 1) * 128],
                                   rhs=vv[:, c, h * 128:(h + 1) * 128], start=True, stop=True)
                return ins
            pe(f, ["kw", "v"], ["ps4", "ps5"], cost=0.8)
            tt(OFFE("RstRD"), Rst[:], Rst[:], mk(cst, 0, 128, K_RD, [[1, 4], [0, 128]]), ALU.mult, ["Rst", "cst"], ["Rst"])
            for bk in range(2):
                for hh in range(2):
                    r0 = hh * 64
                    tt("dve", Rst[r0:r0 + 64, 2 * bk:2 * bk + 2, :], Rst[r0:r0 + 64, 2 * bk:2 * bk + 2, :],
                       mk(PS[4 + bk], r0, 64, hh * 128, [[256, 2], [1, 128]]), ALU.add, ["Rst", "ps%d" % (4 + bk)], ["Rst"])
            for hh in range(2):
                act(Rbf[:, hh, :, :], Rst[:], AF.Copy, ["Rst", "cst"], ["Rbf"], scale=cst[:, K_MSK + hh:K_MSK + hh + 1])
            if stop == 'p3g':
                continue
            for h in range(8):
                blk, hh = h // 2, h % 2
                act(junk2[:], PS[6 + hh][:, blk * 128:(blk + 1) * 128], AF.Square, ["ps%d" % (6 + hh)], ["junk2", "st_ss8_%d" % h],
                    accum_out=st[:, 24 + h:25 + h])
            rstd_pool(st[:, 32:40], st[:, 24:32], 1.0 / 128, None, "st_rstd8", 40, reads=["st_ss8_%d" % h for h in range(8)])
            for bk in range(2):
                tt("dve", mk(on, 0, 128, c * 1024 + bk * 128, [[256, 4], [1, 128]]), PS[6 + bk][:].rearrange("p (b e) -> p b e", b=4),
                   mk(st, 0, 128, 32 + bk, [[2, 4], [0, 128]]), ALU.mult, ["ps%d" % (6 + bk), "st_rstd8"], ["on%d" % c])
        if sc == 0:
            dump("y", y[:], [128, NTC, D], ["y%d" % c for c in range(NTC)])
            dump("on", on[:], [128, NTC, D], ["on%d" % c for c in range(NTC)], BF16)

        if stop is not None and stop.startswith('p3'):
            continue
        for zb in range(2):
            W, wkey = wnext(win_d, C_Z + zb * 512)
            for c in range(NTC):
                bk = nextbank((0, 1, 4, 5))
                sb_ = c % 2
                cs_ = slice(c * 128, (c + 1) * 128)

                def f(e, W=W, c=c, bk=bk):
                    ins = None
                    for kc in range(8):
                        ins = e.matmul(PS[bk][:], lhsT=hT[:, kc, c * 128:(c + 1) * 128], rhs=W[:, kc, :], start=(kc == 0), stop=(kc == 7))
                    return ins
                pe(f, [wkey, "hT%d" % c], ["ps%d" % bk], cost=2.1)
                act(sz[:, sb_, :], PS[bk][:], AF.Silu, ["ps%d" % bk], ["sz%d" % sb_])
                ysl = y[:, c, zb * 512:(zb + 1) * 512]
                tt(OFFE("yz"), ysl, ysl, sz[:, sb_, :], ALU.mult, ["y%d" % c, "sz%d" % sb_], ["y%d" % c])
                act(junk[:, 0:512], ysl, AF.Square, ["y%d" % c], ["xs0", "st_ssg"], accum_out=st[:, 48:49])
                rstd_pool(st[:, 49:50], st[:, 48:49], 1.0 / 512, "st_ssg", "st_rstdg", 50)
                ts("dve", ymb[:, sb_, :], ysl, st[:, 49:50], None, ALU.mult, None, ["y%d" % c, "st_rstdg"], ["ymb%d" % sb_])

                def f(e, sb_=sb_):
                    ins = None
                    for j in range(4):
                        ins = e.transpose(psb(2 + sb_)[:, j * 128:(j + 1) * 128], ymb[:, sb_, j * 128:(j + 1) * 128], ident_b[:])
                    return ins
                pe(f, ["ymb%d" % sb_, "ident_b"], ["ps%d" % (2 + sb_)], cost=0.5)
                tt("dve", ymT[:, zb * 4:(zb + 1) * 4, cs_], psb(2 + sb_)[:, 0:512].rearrange("p (j t) -> p j t", j=4),
                   mk(vec, 0, 128, V_MNW + zb * 4, [[1, 4], [0, 128]]), ALU.mult, ["ps%d" % (2 + sb_), "vec"], ["ymT"])
        for gb in range(2):
            W, wkey = wnext(win_d, C_G + gb * 512)
            for c in range(NTC):
                bk = nextbank((0, 1, 4, 5))
                sb_ = c % 2
                cs_ = slice(c * 128, (c + 1) * 128)

                def f(e, W=W, c=c, bk=bk):
                    ins = None
                    for kc in range(8):
                        ins = e.matmul(PS[bk][:], lhsT=hT[:, kc, c * 128:(c + 1) * 128], rhs=W[:, kc, :], start=(kc == 0), stop=(kc == 7))
                    return ins
                pe(f, [wkey, "hT%d" % c], ["ps%d" % bk], cost=2.1)
                act(sz[:, sb_, :], PS[bk][:], AF.Silu, ["ps%d" % bk], ["sz%d" % sb_])
                tt(OFFE("yrb"), ymb[:, sb_, :], on[:, c, gb * 512:(gb + 1) * 512], sz[:, sb_, :], ALU.mult, ["on%d" % c, "sz%d" % sb_], ["ymb%d" % sb_])

                def f(e, sb_=sb_):
                    ins = None
                    for j in range(4):
                        ins = e.transpose(psb(2 + sb_)[:, j * 128:(j + 1) * 128], ymb[:, sb_, j * 128:(j + 1) * 128], ident_b[:])
                    return ins
                pe(f, ["ymb%d" % sb_, "ident_b"], ["ps%d" % (2 + sb_)], cost=0.5)
                act(yrT[:, gb * 4:(gb + 1) * 4, cs_], psb(2 + sb_)[:, 0:512].rearrange("p (j t) -> p j t", j=4), AF.Copy, ["ps%d" % (2 + sb_)], ["yrT"])
        if sc == 0:
            dump("ymT", ymT, [128, 8, T], ["ymT"], BF16)
            dump("yrT", yrT, [128, 8, T], ["yrT"], BF16)
        if stop == 'p4':
            continue
        if sc == 0:
            gate_setup()
        for hf in range(2):
            (Wm_, kWm), (Wgm, kWgm), (Wr_, kWr), (Wgr, kWgr) = wgroup(
                [(wm_d, hf * 512), (win_d, C_GAM + hf * 512), (wr_d, hf * 512), (win_d, C_GAR + hf * 512)])
            for c in range(NTC):
                cs_ = slice(c * 128, (c + 1) * 128)
                B5 = (4, 5, 6, 7) if c % 2 == 0 else (0, 1, 2, 3)
                for (bk, lh, lkey, W, wkey) in ((B5[0], ymT, "ymT", Wm_, kWm), (B5[1], hT, "hT%d" % c, Wgm, kWgm), (B5[2], yrT, "yrT", Wr_, kWr), (B5[3], hT, "hT%d" % c, Wgr, kWgr)):
                    def f(e, bk=bk, lh=lh, W=W, cs_=cs_):
                        ins = None
                        for kc in range(8):
                            ins = e.matmul(PS[bk][:], lhsT=lh[:, kc, cs_], rhs=W[:, kc, :], start=(kc == 0), stop=(kc == 7))
                        return ins
                    pe(f, [lkey, wkey], ["ps%d" % bk], cost=2.1)
                act(tmg[:, 0, :], PS[B5[1]][:], AF.Tanh, ["ps%d" % B5[1]], ["sz0"], scale=0.5)
                act(tmg[:, 1, :], PS[B5[3]][:], AF.Tanh, ["ps%d" % B5[3]], ["sz1"], scale=0.5)
                stt(amr[:, 0, :], tmg[:, 0, :], 1.0, PS[B5[0]][:], ALU.add, ALU.mult, ["sz0", "ps%d" % B5[0]], ["amr0"])
                stt(amr[:, 1, :], tmg[:, 1, :], 1.0, PS[B5[2]][:], ALU.add, ALU.mult, ["sz1", "ps%d" % B5[2]], ["amr1"])
                tt(OFFE("mrg"), mrg[:, c, hf * 512:(hf + 1) * 512], amr[:, 0, :], amr[:, 1, :], ALU.add, ["amr0", "amr1"], ["mrg"])
        for c in range(NTC):
            def f(e, c=c):
                ins = None
                for kc in range(8):
                    ins = e.transpose(psb(2)[:, kc * 128:(kc + 1) * 128], mrg[:, c, kc * 128:(kc + 1) * 128], ident_b[:])
                return ins
            pe(f, ["mrg", "ident_b"], ["ps2"], cost=1.0)
            act(mT[:, :, c * 128:(c + 1) * 128], psb(2).rearrange("p (k t) -> p k t", k=8), AF.Copy, ["ps2"], ["mT"])
        if sc == 0:
            dump("mrg", mrg, [128, NTC, D], ["mrg"], BF16)
        if stop == 'p5':
            continue
        (Wo0, kWo0), (Wo1, kWo1) = wgroup([(wo_d, 0), (wo_d, 512)])
        for c in range(NTC):
            gc = sc * NTC + c
            b = gc % 2
            cs_ = slice(c * 128, (c + 1) * 128)
            dma("sp", xo[:, 0, :], x_d[gc * 128:(gc + 1) * 128, :], [], ["xo0"], "xo0")
            for hf, (W, wkey) in enumerate(((Wo0, kWo0), (Wo1, kWo1))):
                bk = nextbank((0, 1, 4, 5))

                def f(e, W=W, cs_=cs_, bk=bk):
                    ins = None
                    for kc in range(8):
                        ins = e.matmul(PS[bk][:], lhsT=mT[:, kc, cs_], rhs=W[:, kc, :], start=(kc == 0), stop=(kc == 7))
                    return ins
                pe(f, ["mT", wkey], ["ps%d" % bk], cost=2.1)
                tt("dve", amr[:, hf, :], PS[bk][:], gate_bc[:, hf * 512:(hf + 1) * 512], ALU.mult, ["ps%d" % bk, "gate_bc"], ["amr%d" % hf])
                tt(OFFE("xoadd"), xo[:, 0, hf * 512:(hf + 1) * 512], xo[:, 0, hf * 512:(hf + 1) * 512], amr[:, hf, :], ALU.add, ["xo0", "amr%d" % hf], ["xo0"])
            act(junk, xo[:, 0, :], AF.Square, ["xo0"], ["xs0", "st_ssf"], accum_out=st[:, 52:53])
            rstd_pool(st[:, 53:54], st[:, 52:53], 1.0 / D, "st_ssf", "st_rstdf", 54)
            stt(xo[:, 0, :], xo[:, 0, :], st[:, 53:54], fnw_bc[:], ALU.mult, ALU.mult, ["xo0", "st_rstdf", "fnw_bc"], ["xo0"])
            dma("sp", out_d[gc * 128:(gc + 1) * 128, :], xo[:, 0, :], ["xo0"], ["out%d" % gc], "out%d" % b)
    allout = ["out%d" % gc for gc in range(NCH)] + ["dbgout_" + k for k in dbg_outs] + ["ring%d" % i for i in range(NSLOT)] + ["vec", "rowv", "gate_bc", "fnw_bc", "cst", "wdt", "cs_t", "x1_0", "x1_1"]
    S.add("sp", None, allout, [])

    S.reorder(REORDER)
    with nc.Block() as block:
        @block.sync
        def _(e):
            S.emit("sp", e, engsem, dmasem)

        @block.gpsimd
        def _(e):
            S.emit("pool", e, engsem, dmasem)

        @block.vector
        def _(e):
            S.emit("dve", e, engsem, dmasem)

        @block.scalar
        def _(e):
            S.emit("act", e, engsem, dmasem)

        @block.tensor
        def _(e):
            S.emit("pe", e, engsem, dmasem)
    es.close()
    return nc, dbg_outs


def _consts():
    H, Q = 8, 128
    log_g = np.log1p(-np.exp2(-5.0 - np.arange(H, dtype=np.float64)))
    idx = np.arange(Q, dtype=np.float64)
    cst = np.zeros((128, NCST), np.float64)
    rel = idx[None, :] - idx[:, None]
    for h in range(H):
        m = np.where(rel >= 0, np.exp(rel * log_g[h]), 0.0) * (64 ** -0.5)
        cst[:, K_DMAT + h * 128:K_DMAT + (h + 1) * 128] = m
    p = np.arange(128)
    hh = p // 64
    for blk in range(4):
        h = 2 * blk + hh
        cst[:, K_GAM + blk * 128:K_GAM + (blk + 1) * 128] = np.exp((idx[None, :] + 1) * log_g[h][:, None])
        cst[:, K_RD + blk] = np.exp(Q * log_g[h])
    for h in range(H):
        cst[:, K_KWV + h] = np.exp((Q - 1 - idx) * log_g[h]) * (64 ** -0.5)
    cst[:, K_U:K_U + 128] = (idx[:, None] <= idx[None, :])
    cst[:, K_L:K_L + 128] = (idx[:, None] > idx[None, :])
    cst[:, K_ONES:K_ONES + 128] = 1.0
    cst[:, K_ID:K_ID + 128] = np.eye(128)
    cst[:64, K_MSK] = 1.0
    cst[64:, K_MSK + 1] = 1.0
    half = 32
    inv = 10000.0 ** (-np.arange(half, dtype=np.float64) / half)
    ang = np.arange(L, dtype=np.float64)[:, None] * inv[None, :]
    cosT = np.cos(ang).astype(np.float32)
    sinT = np.sin(ang).astype(np.float32)
    return cst.astype(np.float32), cosT, sinT


_CACHE = {}


def _prep_inputs(inputs):
    f = lambda a: np.ascontiguousarray(np.asarray(a, dtype=np.float32))
    x = f(inputs["x"]); c = f(inputs["c"])
    cst, cosT, sinT = _consts()
    w_ada = f(inputs["w_ada"][0]); b_ada = f(inputs["b_ada"][0])
    shared = {
        "w_ada": w_ada,
        "b_adaT": np.ascontiguousarray(b_ada.reshape(24, 128).T),
        "b_gate": np.ascontiguousarray(b_ada[2048:3072].reshape(1, D)),
        "norm_wT": np.ascontiguousarray(f(inputs["norm_w"][0]).reshape(8, 128).T),
        "w_in": f(inputs["w_in"][0]),
        "conv_wT": np.ascontiguousarray(f(inputs["conv_w"][0]).reshape(4, 10, 128).transpose(2, 1, 0).reshape(128, 40)),
        "conv_bT": np.ascontiguousarray(f(inputs["conv_b"][0]).reshape(10, 128).T),
        "dt_bias": f(inputs["dt_bias"][0]).reshape(1, 16),
        "a_log": f(inputs["a_log"][0]).reshape(1, 16),
        "d_skip": f(inputs["d_skip"][0]).reshape(1, 16),
        "dskT": np.ascontiguousarray(np.repeat(f(inputs["d_skip"][0]), 64).reshape(8, 128).T),
        "m_norm_wT": np.ascontiguousarray(f(inputs["m_norm_w"][0]).reshape(8, 128).T),
        "w_proj_m": f(inputs["w_proj_m"][0]),
        "w_proj_r": f(inputs["w_proj_r"][0]),
        "w_out": f(inputs["w_out"][0]),
        "fnw": f(inputs["final_norm_w"]).reshape(1, D),
        "cosT": cosT, "sinT": sinT, "cst": cst,
    }
    in_maps = []
    for b in range(8):
        m = dict(shared)
        m["x"] = np.ascontiguousarray(x[b])
        cT = np.zeros((128, 16), np.float32)
        cT[:, 0:8] = c[b].reshape(8, 128).T
        m["cT"] = cT
        in_maps.append(m)
    return in_maps


def kernel(**inputs):
    in_maps = _prep_inputs(inputs)
    if "nc" not in _CACHE:
        _CACHE["nc"] = build(False)[0]
    nc = _CACHE["nc"]
    res = run_bass_kernel_spmd(nc, in_maps, core_ids=list(range(8)))
    out = np.stack([np.asarray(r["out"], dtype=np.float32) for r in res.results], axis=0)
    return out
```

```python
import numpy as np
from contextlib import ExitStack
import concourse.bass as bass
import concourse.mybir as mybir
from concourse.bass_utils import run_bass_kernel_spmd

F32, BF16 = mybir.dt.float32, mybir.dt.bfloat16
AF = mybir.ActivationFunctionType
ALU = mybir.AluOpType
AX = mybir.AxisListType

L, D = 2048, 1024
NCH = 16
NTC = 4
T = NTC * 128
NSC = NCH // NTC
DIN = 7440
NSLOT = 6
REORDER = True
SLACK = 0.0
XLAT = 0.55
ATTACH_WAIT = True
ACT_FREE = True
PESCALE = 0.85
FREEZE = set()
OFF = {'rhsA', 'Wt'}
PSX = True


def OFFE(name):
    return 'pool' if name in OFF else 'dve'
EPS = 1e-6
C_Z, C_XBC, C_DT, C_Q, C_K, C_V, C_G, C_GAM, C_GAR = 0, 1024, 2304, 2320, 2832, 3344, 4368, 5392, 6416
K_DMAT, K_GAM, K_KWV, K_RD, K_U, K_L, K_ONES, K_ID = 0, 1024, 1536, 1544, 1548, 1676, 1804, 1932
K_MSK = 2060
NCST = 2062


ALIAS = {}
for _b in range(10):
    ALIAS["xbcT%d" % _b] = ["u%d" % _b]
ALIAS["qr"] = ["u%d" % i for i in range(10, 14)]
ALIAS["kr"] = ["u%d" % i for i in range(14, 18)]
ALIAS["v"] = ["u%d" % i for i in range(18, 26)]
ALIAS["ymT"] = ["u%d" % i for i in range(0, 8)]
ALIAS["yrT"] = ["u%d" % i for i in range(8, 16)]
ALIAS["mrg"] = ["u%d" % i for i in range(16, 24)]
ALIAS["mT"] = ["u%d" % i for i in range(26, 34)]


class _Op:
    __slots__ = ("eng", "fn", "deps", "idx", "sig", "sigidx", "dmakey", "dmaval", "alldeps", "cidx", "cost", "lat",
                 "aseg", "pos", "fin", "nrem", "users", "ready", "tag", "st", "bl", "ks", "vc")


class _FirstWait:
    def __init__(self, e, sem, val):
        self._e, self._sem, self._val, self._done = e, sem, val, False

    def __getattr__(self, name):
        attr = getattr(self._e, name)
        if not callable(attr):
            return attr

        def call(*a, **k):
            ins = attr(*a, **k)
            if not self._done:
                ins._wait_ge(self._sem, self._val)
                self._done = True
            return ins
        return call


class Sched:
    ENGS = ("pe", "act", "dve", "pool", "sp")

    def __init__(self):
        self.ops = {e: [] for e in self.ENGS}
        self.all = []
        self.lastw = {}
        self.readers = {}
        self.dmacount = {}
        self.aseg = 0
        self.agroup = None

    def add(self, eng, fn, reads=(), writes=(), dma=None, cost=0.1, lat=0.0, agroup=None):
        op = _Op()
        op.eng, op.fn, op.sig, op.sigidx = eng, fn, False, 0
        op.cidx = len(self.all)
        op.cost, op.lat = cost, lat
        import sys as _sys
        fr = _sys._getframe(1)
        while fr.f_code.co_name in ("dma", "act", "tt", "ts", "stt", "cp", "pe", "rstd_pool"):
            fr = fr.f_back
        op.tag = fr.f_lineno
        op.dmakey = dma
        if dma is not None:
            self.dmacount[dma] = self.dmacount.get(dma, 0) + 1
            op.dmaval = 16 * self.dmacount[dma]
        if eng == "act" and agroup is not None and agroup != self.agroup:
            self.agroup = agroup
            self.aseg += 1
        op.aseg = self.aseg if (agroup is not None or not ACT_FREE) else -1
        deps = {}
        reads = [kk for k in reads for kk in ALIAS.get(k, (k,))]
        writes = [kk for k in writes for kk in ALIAS.get(k, (k,))]
        if PSX and eng in ("act", "dve"):
            extra = ["rd_" + k for k in reads if k.startswith("ps")]
            if extra:
                writes = list(writes) + extra

        def consider(d):
            if d is not None and d is not op:
                deps[d.cidx] = d

        for k in reads:
            consider(self.lastw.get(k))
        for k in writes:
            consider(self.lastw.get(k))
            for r in self.readers.get(k, ()):
                consider(r)
        op.alldeps = list(deps.values())
        for k in reads:
            self.readers.setdefault(k, []).append(op)
        for k in writes:
            self.lastw[k] = op
            self.readers[k] = []
        self.ops[eng].append(op)
        self.all.append(op)
        return op

    def reorder(self, enable=True):
        for op in self.all:
            op.users = []
            op.nrem = 0
            op.ready = 0.0
        for op in self.all:
            for d in op.alldeps:
                d.users.append(op)
                op.nrem += 1
        if not enable:
            neword = {e: list(self.ops[e]) for e in self.ENGS}
        else:
            avail = {e: [] for e in self.ENGS}
            for op in self.all:
                if op.nrem == 0:
                    avail[op.eng].append(op)
            tE = {e: 0.0 for e in self.ENGS}
            neword = {e: [] for e in self.ENGS}
            act_rem = {}
            for op in self.ops["act"]:
                if op.aseg >= 0:
                    act_rem[op.aseg] = act_rem.get(op.aseg, 0) + 1
            act_cur = min(act_rem) if act_rem else 0
            nleft = len(self.all)
            ptr = {e: 0 for e in self.ENGS}
            orig = {e: list(self.ops[e]) for e in self.ENGS}
            nxt = {e: (orig[e][0].cidx if orig[e] else -1) for e in self.ENGS}
            for op in reversed(self.all):
                m = 0.0
                for u in op.users:
                    if u.bl > m:
                        m = u.bl
                op.bl = op.cost + op.lat + m
            while nleft:
                best = None
                for e in self.ENGS:
                    te = tE[e]
                    cands = []
                    t0 = None
                    for op in avail[e]:
                        if e == "act" and op.aseg >= 0 and op.aseg != act_cur:
                            continue
                        if e in FREEZE and op.cidx != nxt[e]:
                            continue
                        st = op.ready if op.ready > te else te
                        cands.append((st, op))
                        if t0 is None or st < t0:
                            t0 = st
                    if not cands:
                        continue
                    pick = None
                    for st, op in cands:
                        if st <= t0 + SLACK:
                            if pick is None or (op.bl, -op.cidx) > (pick[1].bl, -pick[1].cidx):
                                pick = (st, op)
                    key = (pick[0], pick[1].cidx)
                    if best is None or key < best[0]:
                        best = (key, pick[1])
                assert best is not None, "scheduler stuck"
                (st, _), op = best
                e = op.eng
                avail[e].remove(op)
                tE[e] = st + op.cost
                op.st = st
                op.fin = st + op.cost + op.lat
                neword[e].append(op)
                nleft -= 1
                if e in FREEZE:
                    ptr[e] += 1
                    nxt[e] = orig[e][ptr[e]].cidx if ptr[e] < len(orig[e]) else -1
                if e == "act" and op.aseg >= 0:
                    act_rem[op.aseg] -= 1
                    while act_rem.get(act_cur, 0) == 0 and act_rem:
                        act_rem.pop(act_cur, None)
                        if not act_rem:
                            break
                        act_cur = min(act_rem)
                for u in op.users:
                    f_ = op.fin + (XLAT if u.eng != op.eng else 0.0)
                    if f_ > u.ready:
                        u.ready = f_
                    u.nrem -= 1
                    if u.nrem == 0:
                        avail[u.eng].append(u)
            self.est = max(tE.values())
        self.ops = neword
        for e in self.ENGS:
            for i, op in enumerate(self.ops[e]):
                op.pos = i
        prev = {}
        for e in self.ENGS:
            p = None
            for op in self.ops[e]:
                prev[id(op)] = p
                p = op
        order = sorted(self.all, key=lambda o: (o.st, o.cidx)) if enable else list(self.all)
        for op in order:
            p = prev[id(op)]
            k = dict(p.ks) if p is not None else {}
            waits = []
            deps = sorted(op.alldeps, key=lambda d: -d.fin) if enable else list(op.alldeps)
            for d in deps:
                if d.dmakey is not None:
                    if k.get("dma:" + d.dmakey, 0) >= d.dmaval:
                        continue
                    waits.append(d)
                    k["dma:" + d.dmakey] = d.dmaval
                    continue
                if d.eng == "pe" and op.eng == "pe":
                    continue
                if k.get(d.eng, -1) >= d.pos:
                    continue
                waits.append(d)
                for kk, vv in d.vc.items():
                    if k.get(kk, -1) < vv:
                        k[kk] = vv
            op.ks = k
            op.deps = waits
            vc = dict(k)
            vc[op.eng] = op.pos
            if op.dmakey is not None:
                vc = {"dma:" + op.dmakey: op.dmaval}
                vc.update({kk: vv for kk, vv in k.items()})
                vc.pop(op.eng, None) if False else None
            op.vc = vc
            for d in waits:
                if d.dmakey is None:
                    d.sig = True
        for e in self.ENGS:
            n = 0
            for op in self.ops[e]:
                if op.sig and op.dmakey is None:
                    n += 1
                    op.sigidx = n

    def emit(self, eng_name, e, engsem, dmasem):
        waited = {}
        for op in self.ops[eng_name]:
            need = []
            for d in op.deps:
                if d.dmakey is not None:
                    sem, val = dmasem[d.dmakey], d.dmaval
                else:
                    if d.eng == "pe" and eng_name == "pe":
                        continue
                    sem, val = engsem[d.eng], d.sigidx
                key = id(sem)
                if waited.get(key, 0) >= val:
                    continue
                waited[key] = val
                need.append((sem, val))
            attach = None
            if ATTACH_WAIT and need and op.fn is not None:
                attach = need.pop()
            for sem, val in need:
                e.wait_ge(sem, val)
            if attach is not None:
                ins = op.fn(_FirstWait(e, attach[0], attach[1]))
            else:
                ins = op.fn(e) if op.fn is not None else None
            if op.dmakey is not None:
                ins.then_inc(dmasem[op.dmakey], 16)
            elif op.sig:
                assert ins is not None
                ins.then_inc(engsem[eng_name], 1)


def build(dbg=False, stop=None, nsc=NSC):
    nc = bass.Bass("TRN2", target_bir_lowering=False)
    S = Sched()

    def din(name, shape):
        return nc.dram_tensor(name, list(shape), F32, kind="ExternalInput").ap()

    x_d = din("x", [L, D])
    cT_d = din("cT", [128, 16])
    wada_d = din("w_ada", [D, 3 * D])
    badaT_d = din("b_adaT", [128, 24])
    bgate_d = din("b_gate", [1, D])
    nwT_d = din("norm_wT", [128, 8])
    win_d = din("w_in", [D, DIN])
    cwT_d = din("conv_wT", [128, 40])
    cbT_d = din("conv_bT", [128, 10])
    dtb_d = din("dt_bias", [1, 16])
    alog_d = din("a_log", [1, 16])
    dsk_d = din("d_skip", [1, 16])
    dskT_d = din("dskT", [128, 8])
    mnwT_d = din("m_norm_wT", [128, 8])
    wm_d = din("w_proj_m", [D, D])
    wr_d = din("w_proj_r", [D, D])
    wo_d = din("w_out", [D, D])
    fnw_d = din("fnw", [1, D])
    cos_d = din("cosT", [L, 32])
    sin_d = din("sinT", [L, 32])
    cst_d = din("cst", [128, NCST])
    out_d = nc.dram_tensor("out", [L, D], F32, kind="ExternalOutput").ap()
    dbg_outs = {}

    es = ExitStack()

    def sb(name, shape, dt):
        return es.enter_context(nc.sbuf_tensor("s_" + name, list(shape), dt))

    def rowsize(h):
        r = 1
        for s in h.shape[1:]:
            r *= s
        return r

    def mk(h, p0, npart, off, dims):
        return bass.AP(h, p0 * rowsize(h) + off, [[rowsize(h), npart]] + [list(d) for d in dims])

    cst = sb("cst", [128, NCST], F32)
    ident_b = sb("ident_b", [128, 128], BF16)
    vec = sb("vec", [128, 144], F32)
    V_CT, V_BADA, V_NW, V_CW, V_CB, V_MNW, V_G, V_SH, V_NH, V_DSK = 0, 16, 40, 48, 88, 98, 106, 114, 122, 128
    rowv = sb("rowv", [128, 64], F32)
    gate_bc = sb("gate_bc", [128, D], F32)
    fnw_bc = sb("fnw_bc", [128, D], F32)
    wdt = sb("wdt", [128, 8, 16], BF16)
    ring = sb("ring", [128, NSLOT * 2048], F32)
    x1 = sb("x1", [128, 2, D], F32)
    xo = x1
    xs2 = sb("xs", [128, 2, D], BF16)
    junk = xs2[:, 0, :]
    hT = sb("hT", [128, 8, T], BF16)
    halo = sb("halo", [128, 10, 3], F32)
    rawt = sb("rawt", [128, 2, T + 3], F32)
    acc = sb("acc", [128, 2, T], F32)
    cs_t = sb("cs_t", [128, 2, NTC, 32], F32)
    dtt = sb("dtt", [128, NTC, 16], F32)
    uni = sb("uni", [128, 17408], BF16)
    y = sb("y", [128, NTC, D], F32)
    on = sb("on", [128, NTC, D], BF16)
    st = sb("st", [128, 64], F32)
    sm = sb("sm", [128, 128], F32)
    rhsA = sb("rhsA", [128, 1, 8, 128], F32)
    expseg = sb("expseg", [128, 2, 8, 128], BF16)
    Gm = sb("Gm", [128, 2, 128], F32)
    Wt = sb("Wt", [128, 16, 128], BF16)
    x_dt = sb("x_dt", [128, D], BF16)
    xw = sb("xw", [128, D], BF16)
    DD = sb("DD", [128, 8, 128], BF16)
    junk2 = sb("junk2", [128, 128], BF16)
    cTb = sb("cTb", [128, 16], BF16)
    Btok = sb("Btok", [128, 128], BF16)
    t1 = sb("t1", [128, D], F32)
    Sst = sb("Sst", [128, 512], F32)
    Sbf = sb("Sbf", [128, 512], BF16)
    qT = sb("qT", [128, 4, 128], BF16)
    kT = sb("kT", [128, 2, 4, 128], BF16)
    BCz = sb("BCz", [128, 1, 2, T], BF16)
    qwT = sb("qwT", [128, 4, 128], BF16)
    kw = sb("kw", [128, 512], BF16)
    PT = sb("PT", [128, 8, 128], BF16)
    Rst = sb("Rst", [128, 4, 128], F32)
    Rbf = sb("Rbf", [128, 2, 4, 128], BF16)
    sz = sb("sz", [128, 2, 512], F32)
    ymb = sb("ymb", [128, 2, 512], BF16)
    tmg = sz
    amr = sb("amr", [128, 2, 512], F32)

    def uview(off, dims):
        return mk(uni, 0, 128, off, dims)

    O_XBC, O_QR, O_KR, O_V = 0, 10 * T, 10 * T + NTC * 512, 10 * T + 2 * NTC * 512
    O_YMT, O_YRT, O_MRG, O_MT = 0, 8 * T, 16 * T, 26 * 512
    assert O_V + NTC * 1024 <= O_MT and O_MRG + NTC * 1024 <= O_MT and O_MT + 8 * T <= 17408

    PS = [es.enter_context(nc.psum_tensor(f"ps{i}", [128, 512], F32)) for i in range(8)]

    def psb(i):
        return PS[i][:].bitcast(BF16)

    engsem = {e: es.enter_context(nc.semaphore("sem_" + e)) for e in ("pe", "act", "dve", "pool")}
    dmasem = {}

    def dsem(key):
        if key not in dmasem:
            dmasem[key] = es.enter_context(nc.semaphore("d_" + key))
        return key

    def fsz(ap):
        n = 1
        for d in ap.shape[1:]:
            n *= d
        return n

    def dma(eng, out, in_, reads, writes, key, nbytes=None):
        dsem(key)
        if nbytes is None:
            nbytes = 128 * fsz(out) * 4
        return S.add(eng, lambda e, o=out, i=in_: e.dma_start(out=o, in_=i), reads, writes, dma=key,
                     cost=(1.2 if eng == "pool" else 0.15), lat=2.0 + nbytes / 120e3)

    AGROUP = {AF.Silu: "g18", AF.Tanh: "g18", AF.Exp: "g6", AF.Ln: "g6"}

    def act(out, in_, func, reads, writes, **kw):
        return S.add("act", lambda e: e.activation(out=out, in_=in_, func=func, **kw), reads, writes,
                     cost=0.22 + fsz(out) / 1200.0, agroup=AGROUP.get(func))

    def ecost(eng, out):
        n = fsz(out)
        return (0.07 + n / 960.0) if eng == "dve" else (0.15 + n / 450.0)

    def tt(eng, out, in0, in1, op, reads, writes):
        return S.add(eng, lambda e: e.tensor_tensor(out=out, in0=in0, in1=in1, op=op), reads, writes, cost=ecost(eng, out))

    def ts(eng, out, in0, s1, s2, op0, op1, reads, writes):
        c = ecost(eng, out)
        if s2 is None:
            return S.add(eng, lambda e: e.tensor_scalar(out=out, in0=in0, scalar1=s1, scalar2=None, op0=op0), reads, writes, cost=c)
        return S.add(eng, lambda e: e.tensor_scalar(out=out, in0=in0, scalar1=s1, scalar2=s2, op0=op0, op1=op1), reads, writes, cost=c)

    def stt(out, in0, scalar, in1, op0, op1, reads, writes):
        return S.add("dve", lambda e: e.scalar_tensor_tensor(out=out, in0=in0, scalar=scalar, in1=in1, op0=op0, op1=op1), reads, writes,
                     cost=ecost("dve", out))

    def cp(eng, out, in_, reads, writes):
        return S.add(eng, lambda e: e.tensor_copy(out=out, in_=in_), reads, writes, cost=ecost(eng, out))

    def pe(fn, reads, writes, cost=1.0):
        return S.add("pe", fn, reads, writes, cost=cost * PESCALE)

    def rstd_pool(dst, src, inv_n, reads_key, write_key, tmpcol, reads=None):
        tmp = st[:, tmpcol:tmpcol + src.shape[1]]
        ts("pool", tmp, src, inv_n, EPS, ALU.mult, ALU.add, reads if reads is not None else [reads_key], ["st_tmp%d" % tmpcol])
        nh = vec[:, V_NH:V_NH + 1].to_broadcast([128, src.shape[1]]) if src.shape[1] > 1 else vec[:, V_NH:V_NH + 1]
        tt("pool", dst, tmp, nh, ALU.pow, ["st_tmp%d" % tmpcol, "vec"], [write_key])

    def dump(name, src, shape, reads, dt=F32):
        if not dbg:
            return
        d = nc.dram_tensor("dbg_" + name, list(shape), dt, kind="ExternalOutput").ap()
        dbg_outs[name] = d
        dma("sp", d, src, reads, ["dbgout_" + name], "dbg_" + name)

    wreq = []
    for cb in range(4):
        wreq.append((wada_d, cb * 512, 512))
    for sc in range(NSC):
        for (c0, n) in ((C_XBC, 512), (C_XBC + 512, 512), (C_XBC + 1024, 256), (C_Q, 512), (C_K, 512), (C_V, 512), (C_V + 512, 512),
                        (C_Z, 512), (C_Z + 512, 512), (C_G, 512), (C_G + 512, 512)):
            wreq.append((win_d, c0, n))
        if sc == 0:
            wreq.append((wada_d, 4 * 512, 512))
            wreq.append((wada_d, 5 * 512, 512))
        for hf in range(2):
            wreq.append((wm_d, hf * 512, 512))
            wreq.append((win_d, C_GAM + hf * 512, 512))
            wreq.append((wr_d, hf * 512, 512))
            wreq.append((win_d, C_GAR + hf * 512, 512))
        wreq.append((wo_d, 0, 512))
        wreq.append((wo_d, 512, 512))
    wstate = {"issued": 0, "next": 0}

    def slot_ap(s):
        return ring[:, s * 2048:(s + 1) * 2048].bitcast(BF16).rearrange("p (k n) -> p k n", k=8)

    def wissue_upto(r):
        while wstate["issued"] <= min(r, len(wreq) - 1):
            i = wstate["issued"]
            src, c0, n = wreq[i]
            s = i % NSLOT
            srcap = src.rearrange("(kc p) n -> p kc n", p=128)[:, :, c0:c0 + n]
            dma("pool", slot_ap(s)[:, :, 0:n], srcap, [], ["ring%d" % s], "ring%d" % s)
            wstate["issued"] += 1

    def wgroup(specs):
        r0 = wstate["next"]
        wissue_upto(r0 + NSLOT - 1)
        outl = []
        for (expect_src, expect_c0) in specs:
            r = wstate["next"]
            src, c0, n = wreq[r]
            assert src is expect_src and c0 == expect_c0, (r, c0, expect_c0)
            assert r <= r0 + NSLOT - 1
            wstate["next"] += 1
            s = r % NSLOT
            outl.append((slot_ap(s), "ring%d" % s))
        return outl

    def wnext(expect_src, expect_c0):
        return wgroup([(expect_src, expect_c0)])[0]

    dma("sp", cst[:], cst_d[:, :], [], ["cst"], "cst")
    dma("sp", vec[:, V_CT:V_CT + 16], cT_d[:, :], [], ["vec"], "v0")
    dma("sp", vec[:, V_BADA:V_BADA + 24], badaT_d[:, :], [], ["vec"], "v1")
    dma("sp", vec[:, V_NW:V_NW + 8], nwT_d[:, :], [], ["vec"], "v2")
    dma("sp", vec[:, V_CW:V_CW + 40], cwT_d[:, :], [], ["vec"], "v3")
    dma("sp", vec[:, V_CB:V_CB + 10], cbT_d[:, :], [], ["vec"], "v4")
    dma("sp", vec[:, V_MNW:V_MNW + 8], mnwT_d[:, :], [], ["vec"], "v5")
    dma("sp", vec[:, V_DSK:V_DSK + 8], dskT_d[:, :], [], ["vec"], "v6")
    dma("sp", rowv[:, 0:16], bass.AP(dtb_d.tensor, 0, [[0, 128], [1, 16]]), [], ["rowv"], "r0")
    dma("sp", rowv[:, 48:64], bass.AP(alog_d.tensor, 0, [[0, 128], [1, 16]]), [], ["rowv"], "r1")
    dma("sp", rowv[:, 32:48], bass.AP(dsk_d.tensor, 0, [[0, 128], [1, 16]]), [], ["rowv"], "r2")
    dma("sp", gate_bc[:], bass.AP(bgate_d.tensor, 0, [[0, 128], [1, D]]), [], ["gate_bc"], "gb")
    dma("sp", fnw_bc[:], bass.AP(fnw_d.tensor, 0, [[0, 128], [1, D]]), [], ["fnw_bc"], "fb")
    dma("pool", wdt[:], win_d.rearrange("(kc p) n -> p kc n", p=128)[:, :, C_DT:C_DT + 16], [], ["wdt"], "wdt")
    S.add("pool", lambda e: e.memset(vec[:, V_NH:V_NH + 1], -0.5), [], ["vec"])
    S.add("pool", lambda e: e.memset(halo[:], 0.0), [], ["halo"])
    S.add("pool", lambda e: e.memset(Sst[:], 0.0), [], ["Sst"])
    S.add("pool", lambda e: e.memset(Sbf[:], 0.0), [], ["Sbf"])
    S.add("pool", lambda e: e.memset(Rst[:], 0.0), [], ["Rst"])
    S.add("pool", lambda e: e.memset(Rbf[:], 0.0), [], ["Rbf"])
    cp("dve", ident_b[:], cst[:, K_ID:K_ID + 128], ["cst"], ["ident_b"])
    act(rowv[:, 16:32], rowv[:, 48:64], AF.Exp, ["rowv"], ["rowv"])
    ts("dve", rowv[:, 16:32], rowv[:, 16:32], -1.0, None, ALU.mult, None, ["rowv"], ["rowv"])

    U_f = cst[:, K_U:K_U + 128]
    L_f = cst[:, K_L:K_L + 128]
    ones_f = cst[:, K_ONES:K_ONES + 128]

    cp("dve", cTb[:], vec[:, V_CT:V_CT + 16], ["vec"], ["cTb"])
    cbc = amr[:, 0, :].bitcast(BF16).rearrange("p (k m) -> p k m", k=8)
    for kc in range(8):
        cp("dve", cbc[:, kc, :], vec[:, V_CT + kc:V_CT + kc + 1].to_broadcast([128, 128]), ["vec"], ["amr0"])
    for blk in range(8):
        ts("dve", DD[:, blk, :], ident_b[:], vec[:, V_DSK + blk:V_DSK + blk + 1], None, ALU.mult, None, ["ident_b", "vec"], ["DD"])
    ps_mod = PS[3]
    for cb in range(4):
        W, wkey = wnext(wada_d, cb * 512)

        def f(e, W=W, cb=cb):
            ins = None
            for sub in range(4):
                j = cb * 4 + sub
                for kc in range(8):
                    ins = e.matmul(ps_mod[:, 2 * j:2 * j + 2], lhsT=W[:, kc, sub * 128:(sub + 1) * 128],
                                   rhs=cTb[:, kc:kc + 2], start=(kc == 0), stop=(kc == 7))
            return ins
        pe(f, [wkey, "cTb"], ["ps3"], cost=2.6)

    def gate_setup():
        for hf in range(2):
            W, wkey = wnext(wada_d, (4 + hf) * 512)

            def f(e, W=W, hf=hf):
                ins = None
                for kc in range(8):
                    ins = e.matmul(PS[hf][:], lhsT=cbc[:, kc, :], rhs=W[:, kc, :], start=(kc == 0), stop=(kc == 7))
                return ins
            pe(f, [wkey, "amr0"], ["ps%d" % hf], cost=2.1)
            tt("dve", gate_bc[:, hf * 512:(hf + 1) * 512], PS[hf][:], gate_bc[:, hf * 512:(hf + 1) * 512], ALU.add, ["ps%d" % hf, "gate_bc"], ["gate_bc"])
        ts("dve", gate_bc[:], gate_bc[:], 0.5, None, ALU.mult, None, ["gate_bc"], ["gate_bc"])
    modv = mk(ps_mod, 0, 128, 0, [[2, 16]])
    tt("dve", st[:, 0:16], modv, vec[:, V_BADA:V_BADA + 16], ALU.add, ["ps3", "vec"], ["st_mod"])
    cp("dve", vec[:, V_SH:V_SH + 8], st[:, 0:8], ["st_mod"], ["vec"])
    stt(vec[:, V_G:V_G + 8], st[:, 8:16], 1.0, vec[:, V_NW:V_NW + 8], ALU.add, ALU.mult, ["st_mod", "vec"], ["vec"])

    xbcT = uview(O_XBC, [[T, 10], [1, T]])
    qr = uview(O_QR, [[512, NTC], [1, 512]])
    kr = uview(O_KR, [[512, NTC], [1, 512]])
    vv = uview(O_V, [[1024, NTC], [1, 1024]])
    ymT = uview(O_YMT, [[T, 8], [1, T]])
    yrT = uview(O_YRT, [[T, 8], [1, T]])
    mrg = uview(O_MRG, [[1024, NTC], [1, 1024]])
    mT = uview(O_MT, [[T, 8], [1, T]])
    A_KEYS = ["xbcT%d" % b for b in range(10)] + ["qr", "kr", "v"]
    B_KEYS = ["ymT", "yrT", "mrg", "mT"]

    mmbank = {"i": 0}

    def nextbank(banks=(0, 1)):
        b = banks[mmbank["i"] % len(banks)]
        mmbank["i"] += 1
        return b

    for sc in range(nsc if stop != 'setup' else 0):
        dma("sp", cs_t[:, 0, :, :], cos_d.rearrange("(c p) f -> p c f", p=128)[:, sc * NTC:(sc + 1) * NTC, :], [], ["cs_t"], "cs0")
        dma("sp", cs_t[:, 1, :, :], sin_d.rearrange("(c p) f -> p c f", p=128)[:, sc * NTC:(sc + 1) * NTC, :], [], ["cs_t"], "cs1")
        for c in range(NTC):
            gc = sc * NTC + c
            b = gc % 2
            dma("sp", x1[:, b, :], x_d[gc * 128:(gc + 1) * 128, :], [], ["x1_%d" % b], "x1_%d" % b)
            xs = xs2[:, b, :]
            xk = "xs%d" % b
            sc0 = 16 + 4 * b
            act(xs, x1[:, b, :], AF.Square, ["x1_%d" % b], [xk, "st_ss%d" % b], accum_out=st[:, sc0:sc0 + 1])
            rstd_pool(st[:, sc0 + 1:sc0 + 2], st[:, sc0:sc0 + 1], 1.0 / D, "st_ss%d" % b, "st_rstd%d" % b, sc0 + 2)
            act(xs, x1[:, b, :], AF.Copy, ["x1_%d" % b, "st_rstd%d" % b], [xk], scale=st[:, sc0 + 1:sc0 + 2])
            pb = 2 + b

            def f(e, xs=xs, pb=pb):
                ins = None
                for kc in range(8):
                    ins = e.transpose(psb(pb)[:, kc * 128:(kc + 1) * 128], xs[:, kc * 128:(kc + 1) * 128], ident_b[:])
                return ins
            pe(f, [xk, "ident_b"], ["ps%d" % pb], cost=1.0)
            for kc in range(8):
                if kc % 2 == 0:
                    act(hT[:, kc, c * 128:(c + 1) * 128], psb(pb)[:, kc * 128:(kc + 1) * 128], AF.Identity, ["ps%d" % pb, "vec"], ["hT%d" % c],
                        scale=vec[:, V_G + kc:V_G + kc + 1], bias=vec[:, V_SH + kc:V_SH + kc + 1])
                else:
                    ts("dve", hT[:, kc, c * 128:(c + 1) * 128], psb(pb)[:, kc * 128:(kc + 1) * 128],
                       vec[:, V_G + kc:V_G + kc + 1], vec[:, V_SH + kc:V_SH + kc + 1], ALU.mult, ALU.add,
                       ["ps%d" % pb, "vec"], ["hT%d" % c])
        HT_KEYS = ["hT%d" % c for c in range(NTC)]
        if stop == 'p1':
            continue
        if sc == 0:
            dump("hT", hT[:], [128, 8, T], HT_KEYS, BF16)

        wslots = wgroup([(win_d, C_XBC), (win_d, C_XBC + 512), (win_d, C_XBC + 1024)])
        for blk in range(10):
            W, wkey = wslots[blk // 4]
            sub = blk % 4
            bk = nextbank((4, 5, 6, 7))
            rb = blk % 2

            def f(e, W=W, sub=sub, bk=bk):
                ins = None
                for kc in range(8):
                    ins = e.matmul(PS[bk][:], lhsT=W[:, kc, sub * 128:(sub + 1) * 128], rhs=hT[:, kc, :], start=(kc == 0), stop=(kc == 7))
                return ins
            pe(f, [wkey] + HT_KEYS, ["ps%d" % bk], cost=2.1)
            cp("dve", rawt[:, rb, 0:3], halo[:, blk, :], ["halo"], ["rawt%d" % rb])
            act(rawt[:, rb, 3:3 + T], PS[bk][:], AF.Copy, ["ps%d" % bk], ["rawt%d" % rb])
            cp("dve", halo[:, blk, :], rawt[:, rb, T:T + 3], ["rawt%d" % rb], ["halo"])
            act(acc[:, rb, :], PS[bk][:], AF.Identity, ["ps%d" % bk, "vec"], ["acc%d" % rb],
                scale=vec[:, V_CW + blk * 4 + 3:V_CW + blk * 4 + 4], bias=vec[:, V_CB + blk:V_CB + blk + 1])
            for s_ in (1, 2, 3):
                stt(acc[:, rb, :], rawt[:, rb, 3 - s_:3 - s_ + T], vec[:, V_CW + blk * 4 + 3 - s_:V_CW + blk * 4 + 4 - s_],
                    acc[:, rb, :], ALU.mult, ALU.add, ["rawt%d" % rb, "acc%d" % rb, "vec"], ["acc%d" % rb])
            act(xbcT[:, blk, :], acc[:, rb, :], AF.Silu, ["acc%d" % rb], ["xbcT%d" % blk])
        if sc == 0:
            dump("xbcT", xbcT, [128, 10, T], ["xbcT%d" % b for b in range(10)], BF16)
        for g in range(2):
            act(BCz[:, 0, g, :], xbcT[:, 9, :], AF.Copy, ["xbcT9", "cst"], ["Cz"], scale=cst[:, K_MSK + g:K_MSK + g + 1])
        if stop == 'p2a':
            continue
        for c in range(NTC):
            def f(e, c=c):
                ins = None
                for kc in range(8):
                    ins = e.matmul(PS[3][:, 256:272], lhsT=hT[:, kc, c * 128:(c + 1) * 128], rhs=wdt[:, kc, :], start=(kc == 0), stop=(kc == 7))
                return ins
            pe(f, ["hT%d" % c, "wdt"], ["ps3"], cost=0.6)
            tt("dve", sm[:, 96:112], PS[3][:, 256:272], rowv[:, 0:16], ALU.add, ["ps3", "rowv"], ["sm_dtpre"])
            act(sm[:, 112:128], sm[:, 96:112], AF.Exp, ["sm_dtpre"], ["sm_e"])
            act(dtt[:, c, :], sm[:, 112:128], AF.Ln, ["sm_e"], ["dtt%d" % c], bias=1.0)
        if sc == 0:
            dump("dt", dtt[:], [128, NTC, 16], ["dtt%d" % c for c in range(NTC)])
        for (dst, dkey, c0) in ((qr, "qr", C_Q), (kr, "kr", C_K)):
            W, wkey = wnext(win_d, c0)
            for c in range(NTC):
                bk = nextbank()

                def f(e, W=W, c=c, bk=bk):
                    ins = None
                    for kc in range(8):
                        ins = e.matmul(PS[bk][:], lhsT=hT[:, kc, c * 128:(c + 1) * 128], rhs=W[:, kc, :], start=(kc == 0), stop=(kc == 7))
                    return ins
                pe(f, [wkey, "hT%d" % c], ["ps%d" % bk], cost=2.1)
                psv = mk(PS[bk], 0, 128, 0, [[64, 8], [32, 2], [1, 32]])
                cosb = mk(cs_t, 0, 128, c * 32, [[0, 8], [0, 2], [1, 32]])
                sinb = mk(cs_t, 0, 128, NTC * 32 + c * 32, [[0, 8], [0, 2], [1, 32]])
                tc_ = mk(t1, 0, 128, 0, [[64, 8], [32, 2], [1, 32]])
                ts_ = mk(t1, 0, 128, 512, [[64, 8], [32, 2], [1, 32]])
                tt("dve", tc_, psv, cosb, ALU.mult, ["ps%d" % bk, "cs_t"], ["t1a"])
                tt("dve", ts_, psv, sinb, ALU.mult, ["ps%d" % bk, "cs_t"], ["t1b"])
                dv = mk(uni, 0, 128, (O_QR if dkey == "qr" else O_KR) + c * 512, [[64, 8], [32, 2], [1, 32]])
                tt("dve", dv[:, :, 0, :], tc_[:, :, 0, :], ts_[:, :, 1, :], ALU.subtract, ["t1a", "t1b"], [dkey])
                tt("dve", dv[:, :, 1, :], ts_[:, :, 0, :], tc_[:, :, 1, :], ALU.add, ["t1a", "t1b"], [dkey])
        for half in range(2):
            W, wkey = wnext(win_d, C_V + half * 512)
            for c in range(NTC):
                bk = nextbank((4, 5, 6, 7))

                def f(e, W=W, c=c, bk=bk):
                    ins = None
                    for kc in range(8):
                        ins = e.matmul(PS[bk][:], lhsT=hT[:, kc, c * 128:(c + 1) * 128], rhs=W[:, kc, :], start=(kc == 0), stop=(kc == 7))
                    return ins
                pe(f, [wkey, "hT%d" % c], ["ps%d" % bk], cost=2.1)
                act(vv[:, c, half * 512:(half + 1) * 512], PS[bk][:], AF.Copy, ["ps%d" % bk], ["v"])
        if sc == 0:
            dump("qr", qr, [128, NTC, 512], ["qr"], BF16)
            dump("kr", kr, [128, NTC, 512], ["kr"], BF16)

        if stop == 'p2':
            continue
        for c in range(NTC):
            cs_ = slice(c * 128, (c + 1) * 128)
            a_sb, acs_sb, dE2, E1, CD, dtE2 = sm[:, 0:16], sm[:, 16:32], sm[:, 32:48], sm[:, 48:64], sm[:, 64:80], sm[:, 80:96]
            tt("dve", a_sb, dtt[:, c, :], rowv[:, 16:32], ALU.mult, ["dtt%d" % c, "rowv"], ["sm_a"])

            def f(e):
                e.matmul(PS[3][:, 256:272], lhsT=U_f, rhs=a_sb, start=True, stop=True)
                return e.matmul(PS[3][:, 272:288], lhsT=ones_f, rhs=a_sb, start=True, stop=True)
            pe(f, ["sm_a", "cst"], ["ps3"], cost=0.3)
            act(acs_sb, PS[3][:, 256:272], AF.Copy, ["ps3"], ["sm_acs"])
            tt("dve", dE2, PS[3][:, 272:288], acs_sb, ALU.subtract, ["ps3", "sm_acs"], ["sm_d"])
            act(dE2, dE2, AF.Exp, ["sm_d"], ["sm_d"])
            act(E1, acs_sb, AF.Exp, ["sm_acs"], ["sm_E1"])
            act(CD, PS[3][:, 272:288], AF.Exp, ["ps3"], ["sm_CD"])
            tt("dve", dtE2, dtt[:, c, :], dE2, ALU.mult, ["dtt%d" % c, "sm_d"], ["sm_dtE2"])

            if stop == 'p3a':
                continue
            def f(e, cs_=cs_):
                ins = None
                for blk in range(8):
                    ins = e.transpose(psb(2)[:, blk * 128:(blk + 1) * 128], xbcT[:, blk, cs_], ident_b[:])
                ins = e.transpose(psb(3)[:, 640:768], xbcT[:, 8, cs_], ident_b[:])
                return ins
            pe(f, ["xbcT%d" % b for b in range(9)] + ["ident_b"], ["ps2", "ps3"], cost=1.1)
            psxb = psb(2).rearrange("p (h q) -> p h q", h=16)

            def bc16(base_ap_tensor, col0):
                return mk(base_ap_tensor, 0, 128, col0, [[1, 16], [0, 64]])
            tt("dve", x_dt[:].rearrange("p (h q) -> p h q", h=16), psxb, mk(dtt, 0, 128, c * 16, [[1, 16], [0, 64]]), ALU.mult, ["ps2", "dtt%d" % c], ["x_dt"])
            tt("dve", xw[:].rearrange("p (h q) -> p h q", h=16), psxb, bc16(sm, 80), ALU.mult, ["ps2", "sm_dtE2"], ["xw"])
            act(Btok[:], psb(3)[:, 640:768], AF.Copy, ["ps3"], ["Btok"])

            if stop == 'p3b':
                continue
            def f(e, cs_=cs_):
                ins = None
                for g in range(2):
                    ins = e.matmul(PS[g][:, 0:128], lhsT=xbcT[:, 8, cs_], rhs=BCz[:, 0, g, cs_], start=True, stop=True)
                return ins
            pe(f, ["xbcT8", "Cz"], ["ps0", "ps1"], cost=0.2)
            for g in range(2):
                tt("dve", Gm[:, g, :], PS[g][:, 0:128], U_f, ALU.mult, ["ps%d" % g, "cst"], ["Gm%d" % g])

            for g in range(2):
                tt(OFFE("rhsA"), rhsA[:, 0, :, :], mk(cst, 0, 128, K_U, [[0, 8], [1, 128]]), mk(sm, 0, 128, g * 8, [[1, 8], [0, 128]]), ALU.mult, ["cst", "sm_a"], ["rhsA"])

                def f(e, g=g):
                    e.matmul(PS[4][:], lhsT=L_f, rhs=rhsA[:, 0, 0:4, :].rearrange("p h i -> p (h i)"), start=True, stop=True)
                    return e.matmul(PS[5][:], lhsT=L_f, rhs=rhsA[:, 0, 4:8, :].rearrange("p h i -> p (h i)"), start=True, stop=True)
                pe(f, ["rhsA", "cst"], ["ps4", "ps5"], cost=1.8)
                act(expseg[:, g, 0:4, :].rearrange("p h i -> p (h i)"), PS[4][:], AF.Exp, ["ps4"], ["expseg%da" % g])
                act(expseg[:, g, 4:8, :].rearrange("p h i -> p (h i)"), PS[5][:], AF.Exp, ["ps5"], ["expseg%db" % g])
                tt(OFFE("Wt"), Wt[:, g * 8:(g + 1) * 8, :], expseg[:, g, :, :], mk(Gm, 0, 128, g * 128, [[0, 8], [1, 128]]), ALU.mult,
                   ["expseg%da" % g, "expseg%db" % g, "Gm%d" % g], ["Wt%d" % g])

            if stop == 'p3c':
                continue
            for g in range(2):
                def f(e, g=g, cs_=cs_):
                    for j in range(4):
                        e.matmul(PS[6 + g][:, j * 128:(j + 1) * 128], lhsT=xbcT[:, g * 4 + j, cs_], rhs=DD[:, g * 4 + j, :], start=(j == 0), stop=False)
                    ins = None
                    for hl in range(8):
                        h = g * 8 + hl
                        ins = e.matmul(PS[6 + g][:, hl * 64:(hl + 1) * 64], lhsT=Wt[:, h, :], rhs=x_dt[:, h * 64:(h + 1) * 64], start=False, stop=(hl == 7))
                    return ins
                pe(f, ["DD", "x_dt", "Wt%d" % g] + ["xbcT%d" % (g * 4 + j) for j in range(4)], ["ps%d" % (6 + g)], cost=1.1)

                def f(e, g=g, cs_=cs_):
                    return e.matmul(PS[g][:], lhsT=BCz[:, 0, g, cs_], rhs=Sbf[:, :], start=True, stop=True)
                pe(f, ["Cz", "Sbf"], ["ps%d" % g], cost=0.27)
                tt("dve", t1[:, g * 512:(g + 1) * 512].rearrange("p (h q) -> p h q", h=8), PS[g][:].rearrange("p (h q) -> p h q", h=8),
                   mk(sm, 0, 128, 48 + g * 8, [[1, 8], [0, 64]]), ALU.mult, ["ps%d" % g, "sm_E1"], ["t1a" if g == 0 else "t1b"])
                tt("dve", y[:, c, g * 512:(g + 1) * 512], PS[6 + g][:], t1[:, g * 512:(g + 1) * 512], ALU.add,
                   ["ps%d" % (6 + g), "t1a" if g == 0 else "t1b"], ["y%d" % c])
            if stop == 'p3d':
                continue
            for g in range(2):
                def f(e, g=g):
                    return e.matmul(PS[4 + g][:], lhsT=Btok[:], rhs=xw[:, g * 512:(g + 1) * 512], start=True, stop=True)
                pe(f, ["Btok", "xw"], ["ps%d" % (4 + g)], cost=0.27)
                r0 = g * 64
                tt(OFFE("SstCD"), mk(Sst, r0, 64, 0, [[64, 8], [1, 64]]), mk(Sst, r0, 64, 0, [[64, 8], [1, 64]]), mk(sm, r0, 64, 64 + g * 8, [[1, 8], [0, 64]]),
                   ALU.mult, ["Sst", "sm_CD"], ["Sst"])
                tt("dve", Sst[r0:r0 + 64, :], Sst[r0:r0 + 64, :], PS[4 + g][r0:r0 + 64, :], ALU.add, ["Sst", "ps%d" % (4 + g)], ["Sst"])
                act(Sbf[r0:r0 + 64, :], Sst[r0:r0 + 64, :], AF.Copy, ["Sst"], ["Sbf"])

            if stop == 'p3e':
                continue
            def f(e, c=c):
                ins = None
                for blk in range(4):
                    e.transpose(psb(2)[:, blk * 128:(blk + 1) * 128], qr[:, c, blk * 128:(blk + 1) * 128], ident_b[:])
                    ins = e.transpose(psb(2)[:, 512 + blk * 128:512 + (blk + 1) * 128], kr[:, c, blk * 128:(blk + 1) * 128], ident_b[:])
                return ins
            pe(f, ["qr", "kr", "ident_b"], ["ps2"], cost=1.0)
            act(qT[:].rearrange("p b i -> p (b i)"), psb(2)[:, 0:512], AF.Copy, ["ps2"], ["qT"])
            for hh in range(2):
                act(kT[:, hh, :, :].rearrange("p b i -> p (b i)"), psb(2)[:, 512:1024], AF.Copy, ["ps2", "cst"], ["kT"], scale=cst[:, K_MSK + hh:K_MSK + hh + 1])
            tt("dve", qwT[:].rearrange("p b i -> p (b i)"), psb(2)[:, 0:512], cst[:, K_GAM:K_GAM + 512], ALU.mult, ["ps2", "cst"], ["qwT"])
            tt(OFFE("kw"), kw[:].rearrange("p (h d) -> p h d", h=8), kr[:, c, :].rearrange("p (h d) -> p h d", h=8),
               mk(cst, 0, 128, K_KWV, [[1, 8], [0, 64]]), ALU.mult, ["kr", "cst"], ["kw"])

            def f(e):
                ins = None
                for h in range(8):
                    blk, hh = h // 2, h % 2
                    ins = e.matmul(PS[hh][:, blk * 128:(blk + 1) * 128], lhsT=kT[:, hh, blk, :],
                                   rhs=qT[:, blk, :], start=True, stop=True)
                return ins
            pe(f, ["qT", "kT"], ["ps0", "ps1"], cost=0.8)
            for bk in range(2):
                tt("dve", mk(PT, 0, 128, bk * 128, [[256, 4], [1, 128]]), PS[bk][:].rearrange("p (b i) -> p b i", b=4),
                   mk(cst, 0, 128, K_DMAT + bk * 128, [[256, 4], [1, 128]]), ALU.mult, ["ps%d" % bk, "cst"], ["PT%d" % bk])

            if stop == 'p3f':
                continue
            def f(e, c=c):
                ins = None
                for h in range(8):
                    blk, hh = h // 2, h % 2
                    o_ = PS[6 + hh][:, blk * 128:(blk + 1) * 128]
                    e.matmul(o_, lhsT=qwT[:, blk, :], rhs=Rbf[:, hh, blk, :], start=True, stop=False)
                    ins = e.matmul(o_, lhsT=PT[:, h, :], rhs=vv[:, c, h * 128:(h + 1) * 128], start=False, stop=True)
                return ins
            pe(f, ["qwT", "Rbf", "PT0", "PT1", "v"], ["ps6", "ps7"], cost=1.6)

            def f(e, c=c):
                ins = None
                for h in range(8):
                    blk = h // 2
                    ins = e.matmul(PS[4 + h // 4][:, (h % 4) * 128:(h % 4 + 1) * 128], lhsT=kw[:, blk * 128:(blk + 1) * 128],
                                   rhs=vv[:, c, h * 128:(h + 1) * 128], start=True, stop=True)
                return ins
            pe(f, ["kw", "v"], ["ps4", "ps5"], cost=0.8)
            tt(OFFE("RstRD"), Rst[:], Rst[:], mk(cst, 0, 128, K_RD, [[1, 4], [0, 128]]), ALU.mult, ["Rst", "cst"], ["Rst"])
            for bk in range(2):
                for hh in range(2):
                    r0 = hh * 64
                    tt("dve", Rst[r0:r0 + 64, 2 * bk:2 * bk + 2, :], Rst[r0:r0 + 64, 2 * bk:2 * bk + 2, :],
                       mk(PS[4 + bk], r0, 64, hh * 128, [[256, 2], [1, 128]]), ALU.add, ["Rst", "ps%d" % (4 + bk)], ["Rst"])
            for hh in range(2):
                act(Rbf[:, hh, :, :], Rst[:], AF.Copy, ["Rst", "cst"], ["Rbf"], scale=cst[:, K_MSK + hh:K_MSK + hh + 1])
            if stop == 'p3g':
                continue
            for h in range(8):
                blk, hh = h // 2, h % 2
                act(junk2[:], PS[6 + hh][:, blk * 128:(blk + 1) * 128], AF.Square, ["ps%d" % (6 + hh)], ["junk2", "st_ss8_%d" % h],
                    accum_out=st[:, 24 + h:25 + h])
            rstd_pool(st[:, 32:40], st[:, 24:32], 1.0 / 128, None, "st_rstd8", 40, reads=["st_ss8_%d" % h for h in range(8)])
            for bk in range(2):
                tt("dve", mk(on, 0, 128, c * 1024 + bk * 128, [[256, 4], [1, 128]]), PS[6 + bk][:].rearrange("p (b e) -> p b e", b=4),
                   mk(st, 0, 128, 32 + bk, [[2, 4], [0, 128]]), ALU.mult, ["ps%d" % (6 + bk), "st_rstd8"], ["on%d" % c])
        if sc == 0:
            dump("y", y[:], [128, NTC, D], ["y%d" % c for c in range(NTC)])
            dump("on", on[:], [128, NTC, D], ["on%d" % c for c in range(NTC)], BF16)

        if stop is not None and stop.startswith('p3'):
            continue
        for zb in range(2):
            W, wkey = wnext(win_d, C_Z + zb * 512)
            for c in range(NTC):
                bk = nextbank((0, 1, 4, 5))
                sb_ = c % 2
                cs_ = slice(c * 128, (c + 1) * 128)

                def f(e, W=W, c=c, bk=bk):
                    ins = None
                    for kc in range(8):
                        ins = e.matmul(PS[bk][:], lhsT=hT[:, kc, c * 128:(c + 1) * 128], rhs=W[:, kc, :], start=(kc == 0), stop=(kc == 7))
                    return ins
                pe(f, [wkey, "hT%d" % c], ["ps%d" % bk], cost=2.1)
                act(sz[:, sb_, :], PS[bk][:], AF.Silu, ["ps%d" % bk], ["sz%d" % sb_])
                ysl = y[:, c, zb * 512:(zb + 1) * 512]
                tt(OFFE("yz"), ysl, ysl, sz[:, sb_, :], ALU.mult, ["y%d" % c, "sz%d" % sb_], ["y%d" % c])
                act(junk[:, 0:512], ysl, AF.Square, ["y%d" % c], ["xs0", "st_ssg"], accum_out=st[:, 48:49])
                rstd_pool(st[:, 49:50], st[:, 48:49], 1.0 / 512, "st_ssg", "st_rstdg", 50)
                ts("dve", ymb[:, sb_, :], ysl, st[:, 49:50], None, ALU.mult, None, ["y%d" % c, "st_rstdg"], ["ymb%d" % sb_])

                def f(e, sb_=sb_):
                    ins = None
                    for j in range(4):
                        ins = e.transpose(psb(2 + sb_)[:, j * 128:(j + 1) * 128], ymb[:, sb_, j * 128:(j + 1) * 128], ident_b[:])
                    return ins
                pe(f, ["ymb%d" % sb_, "ident_b"], ["ps%d" % (2 + sb_)], cost=0.5)
                tt("dve", ymT[:, zb * 4:(zb + 1) * 4, cs_], psb(2 + sb_)[:, 0:512].rearrange("p (j t) -> p j t", j=4),
                   mk(vec, 0, 128, V_MNW + zb * 4, [[1, 4], [0, 128]]), ALU.mult, ["ps%d" % (2 + sb_), "vec"], ["ymT"])
        for gb in range(2):
            W, wkey = wnext(win_d, C_G + gb * 512)
            for c in range(NTC):
                bk = nextbank((0, 1, 4, 5))
                sb_ = c % 2
                cs_ = slice(c * 128, (c + 1) * 128)

                def f(e, W=W, c=c, bk=bk):
                    ins = None
                    for kc in range(8):
                        ins = e.matmul(PS[bk][:], lhsT=hT[:, kc, c * 128:(c + 1) * 128], rhs=W[:, kc, :], start=(kc == 0), stop=(kc == 7))
                    return ins
                pe(f, [wkey, "hT%d" % c], ["ps%d" % bk], cost=2.1)
                act(sz[:, sb_, :], PS[bk][:], AF.Silu, ["ps%d" % bk], ["sz%d" % sb_])
                tt(OFFE("yrb"), ymb[:, sb_, :], on[:, c, gb * 512:(gb + 1) * 512], sz[:, sb_, :], ALU.mult, ["on%d" % c, "sz%d" % sb_], ["ymb%d" % sb_])

                def f(e, sb_=sb_):
                    ins = None
                    for j in range(4):
                        ins = e.transpose(psb(2 + sb_)[:, j * 128:(j + 1) * 128], ymb[:, sb_, j * 128:(j + 1) * 128], ident_b[:])
                    return ins
                pe(f, ["ymb%d" % sb_, "ident_b"], ["ps%d" % (2 + sb_)], cost=0.5)
                act(yrT[:, gb * 4:(gb + 1) * 4, cs_], psb(2 + sb_)[:, 0:512].rearrange("p (j t) -> p j t", j=4), AF.Copy, ["ps%d" % (2 + sb_)], ["yrT"])
        if sc == 0:
            dump("ymT", ymT, [128, 8, T], ["ymT"], BF16)
            dump("yrT", yrT, [128, 8, T], ["yrT"], BF16)
        if stop == 'p4':
            continue
        if sc == 0:
            gate_setup()
        for hf in range(2):
            (Wm_, kWm), (Wgm, kWgm), (Wr_, kWr), (Wgr, kWgr) = wgroup(
                [(wm_d, hf * 512), (win_d, C_GAM + hf * 512), (wr_d, hf * 512), (win_d, C_GAR + hf * 512)])
            for c in range(NTC):
                cs_ = slice(c * 128, (c + 1) * 128)
                B5 = (4, 5, 6, 7) if c % 2 == 0 else (0, 1, 2, 3)
                for (bk, lh, lkey, W, wkey) in ((B5[0], ymT, "ymT", Wm_, kWm), (B5[1], hT, "hT%d" % c, Wgm, kWgm), (B5[2], yrT, "yrT", Wr_, kWr), (B5[3], hT, "hT%d" % c, Wgr, kWgr)):
                    def f(e, bk=bk, lh=lh, W=W, cs_=cs_):
                        ins = None
                        for kc in range(8):
                            ins = e.matmul(PS[bk][:], lhsT=lh[:, kc, cs_], rhs=W[:, kc, :], start=(kc == 0), stop=(kc == 7))
                        return ins
                    pe(f, [lkey, wkey], ["ps%d" % bk], cost=2.1)
                act(tmg[:, 0, :], PS[B5[1]][:], AF.Tanh, ["ps%d" % B5[1]], ["sz0"], scale=0.5)
                act(tmg[:, 1, :], PS[B5[3]][:], AF.Tanh, ["ps%d" % B5[3]], ["sz1"], scale=0.5)
                stt(amr[:, 0, :], tmg[:, 0, :], 1.0, PS[B5[0]][:], ALU.add, ALU.mult, ["sz0", "ps%d" % B5[0]], ["amr0"])
                stt(amr[:, 1, :], tmg[:, 1, :], 1.0, PS[B5[2]][:], ALU.add, ALU.mult, ["sz1", "ps%d" % B5[2]], ["amr1"])
                tt(OFFE("mrg"), mrg[:, c, hf * 512:(hf + 1) * 512], amr[:, 0, :], amr[:, 1, :], ALU.add, ["amr0", "amr1"], ["mrg"])
        for c in range(NTC):
            def f(e, c=c):
                ins = None
                for kc in range(8):
                    ins = e.transpose(psb(2)[:, kc * 128:(kc + 1) * 128], mrg[:, c, kc * 128:(kc + 1) * 128], ident_b[:])
                return ins
            pe(f, ["mrg", "ident_b"], ["ps2"], cost=1.0)
            act(mT[:, :, c * 128:(c + 1) * 128], psb(2).rearrange("p (k t) -> p k t", k=8), AF.Copy, ["ps2"], ["mT"])
        if sc == 0:
            dump("mrg", mrg, [128, NTC, D], ["mrg"], BF16)
        if stop == 'p5':
            continue
        (Wo0, kWo0), (Wo1, kWo1) = wgroup([(wo_d, 0), (wo_d, 512)])
        for c in range(NTC):
            gc = sc * NTC + c
            b = gc % 2
            cs_ = slice(c * 128, (c + 1) * 128)
            dma("sp", xo[:, b, :], x_d[gc * 128:(gc + 1) * 128, :], [], ["x1_%d" % b], "x1_%d" % b)
            for hf, (W, wkey) in enumerate(((Wo0, kWo0), (Wo1, kWo1))):
                bk = nextbank((0, 1, 4, 5))

                def f(e, W=W, cs_=cs_, bk=bk):
                    ins = None
                    for kc in range(8):
                        ins = e.matmul(PS[bk][:], lhsT=mT[:, kc, cs_], rhs=W[:, kc, :], start=(kc == 0), stop=(kc == 7))
                    return ins
                pe(f, ["mT", wkey], ["ps%d" % bk], cost=2.1)
                tt("dve", amr[:, hf, :], PS[bk][:], gate_bc[:, hf * 512:(hf + 1) * 512], ALU.mult, ["ps%d" % bk, "gate_bc"], ["amr%d" % hf])
                tt(OFFE("xoadd"), xo[:, b, hf * 512:(hf + 1) * 512], xo[:, b, hf * 512:(hf + 1) * 512], amr[:, hf, :], ALU.add, ["x1_%d" % b, "amr%d" % hf], ["x1_%d" % b])
            act(junk, xo[:, b, :], AF.Square, ["x1_%d" % b], ["xs0", "st_ssf"], accum_out=st[:, 52:53])
            rstd_pool(st[:, 53:54], st[:, 52:53], 1.0 / D, "st_ssf", "st_rstdf", 54)
            stt(xo[:, b, :], xo[:, b, :], st[:, 53:54], fnw_bc[:], ALU.mult, ALU.mult, ["x1_%d" % b, "st_rstdf", "fnw_bc"], ["x1_%d" % b])
            dma("sp", out_d[gc * 128:(gc + 1) * 128, :], xo[:, b, :], ["x1_%d" % b], ["out%d" % gc], "out%d" % b)
    allout = ["out%d" % gc for gc in range(NCH)] + ["dbgout_" + k for k in dbg_outs] + ["ring%d" % i for i in range(NSLOT)] + ["vec", "rowv", "gate_bc", "fnw_bc", "cst", "wdt", "cs_t", "x1_0", "x1_1"]
    S.add("sp", None, allout, [])

    S.reorder(REORDER)
    with nc.Block() as block:
        @block.sync
        def _(e):
            S.emit("sp", e, engsem, dmasem)

        @block.gpsimd
        def _(e):
            S.emit("pool", e, engsem, dmasem)

        @block.vector
        def _(e):
            S.emit("dve", e, engsem, dmasem)

        @block.scalar
        def _(e):
            S.emit("act", e, engsem, dmasem)

        @block.tensor
        def _(e):
            S.emit("pe", e, engsem, dmasem)
    es.close()
    return nc, dbg_outs


def _consts():
    H, Q = 8, 128
    log_g = np.log1p(-np.exp2(-5.0 - np.arange(H, dtype=np.float64)))
    idx = np.arange(Q, dtype=np.float64)
    cst = np.zeros((128, NCST), np.float64)
    rel = idx[None, :] - idx[:, None]
    for h in range(H):
        m = np.where(rel >= 0, np.exp(rel * log_g[h]), 0.0) * (64 ** -0.5)
        cst[:, K_DMAT + h * 128:K_DMAT + (h + 1) * 128] = m
    p = np.arange(128)
    hh = p // 64
    for blk in range(4):
        h = 2 * blk + hh
        cst[:, K_GAM + blk * 128:K_GAM + (blk + 1) * 128] = np.exp((idx[None, :] + 1) * log_g[h][:, None])
        cst[:, K_RD + blk] = np.exp(Q * log_g[h])
    for h in range(H):
        cst[:, K_KWV + h] = np.exp((Q - 1 - idx) * log_g[h]) * (64 ** -0.5)
    cst[:, K_U:K_U + 128] = (idx[:, None] <= idx[None, :])
    cst[:, K_L:K_L + 128] = (idx[:, None] > idx[None, :])
    cst[:, K_ONES:K_ONES + 128] = 1.0
    cst[:, K_ID:K_ID + 128] = np.eye(128)
    cst[:64, K_MSK] = 1.0
    cst[64:, K_MSK + 1] = 1.0
    half = 32
    inv = 10000.0 ** (-np.arange(half, dtype=np.float64) / half)
    ang = np.arange(L, dtype=np.float64)[:, None] * inv[None, :]
    cosT = np.cos(ang).astype(np.float32)
    sinT = np.sin(ang).astype(np.float32)
    return cst.astype(np.float32), cosT, sinT


_CACHE = {}


def _prep_inputs(inputs):
    f = lambda a: np.ascontiguousarray(np.asarray(a, dtype=np.float32))
    x = f(inputs["x"]); c = f(inputs["c"])
    cst, cosT, sinT = _consts()
    w_ada = f(inputs["w_ada"][0]); b_ada = f(inputs["b_ada"][0])
    shared = {
        "w_ada": w_ada,
        "b_adaT": np.ascontiguousarray(b_ada.reshape(24, 128).T),
        "b_gate": np.ascontiguousarray(b_ada[2048:3072].reshape(1, D)),
        "norm_wT": np.ascontiguousarray(f(inputs["norm_w"][0]).reshape(8, 128).T),
        "w_in": f(inputs["w_in"][0]),
        "conv_wT": np.ascontiguousarray(f(inputs["conv_w"][0]).reshape(4, 10, 128).transpose(2, 1, 0).reshape(128, 40)),
        "conv_bT": np.ascontiguousarray(f(inputs["conv_b"][0]).reshape(10, 128).T),
        "dt_bias": f(inputs["dt_bias"][0]).reshape(1, 16),
        "a_log": f(inputs["a_log"][0]).reshape(1, 16),
        "d_skip": f(inputs["d_skip"][0]).reshape(1, 16),
        "dskT": np.ascontiguousarray(np.repeat(f(inputs["d_skip"][0]), 64).reshape(8, 128).T),
        "m_norm_wT": np.ascontiguousarray(f(inputs["m_norm_w"][0]).reshape(8, 128).T),
        "w_proj_m": f(inputs["w_proj_m"][0]),
        "w_proj_r": f(inputs["w_proj_r"][0]),
        "w_out": f(inputs["w_out"][0]),
        "fnw": f(inputs["final_norm_w"]).reshape(1, D),
        "cosT": cosT, "sinT": sinT, "cst": cst,
    }
    in_maps = []
    for b in range(8):
        m = dict(shared)
        m["x"] = np.ascontiguousarray(x[b])
        cT = np.zeros((128, 16), np.float32)
        cT[:, 0:8] = c[b].reshape(8, 128).T
        m["cT"] = cT
        in_maps.append(m)
    return in_maps


def kernel(**inputs):
    in_maps = _prep_inputs(inputs)
    if "nc" not in _CACHE:
        _CACHE["nc"] = build(False)[0]
    nc = _CACHE["nc"]
    res = run_bass_kernel_spmd(nc, in_maps, core_ids=list(range(8)))
    out = np.stack([np.asarray(r["out"], dtype=np.float32) for r in res.results], axis=0)
    return out
```

```python
import numpy as np
from contextlib import ExitStack
import concourse.bass as bass
import concourse.mybir as mybir
from concourse.bass_utils import run_bass_kernel_spmd

F32, BF16 = mybir.dt.float32, mybir.dt.bfloat16
AF = mybir.ActivationFunctionType
ALU = mybir.AluOpType
AX = mybir.AxisListType

L, D = 2048, 1024
NCH = 16
NTC = 4
T = NTC * 128
NSC = NCH // NTC
DIN = 7440
NSLOT = 6
REORDER = True
SLACK = 0.0
XLAT = 0.55
ATTACH_WAIT = True
ACT_FREE = True
PESCALE = 0.85
FREEZE = set()
OFF = {'rhsA', 'Wt'}
PSX = True


def OFFE(name):
    return 'pool' if name in OFF else 'dve'
EPS = 1e-6
C_Z, C_XBC, C_DT, C_Q, C_K, C_V, C_G, C_GAM, C_GAR = 0, 1024, 2304, 2320, 2832, 3344, 4368, 5392, 6416
K_DMAT, K_GAM, K_KWV, K_RD, K_U, K_L, K_ONES, K_ID = 0, 1024, 1536, 1544, 1548, 1676, 1804, 1932
K_MSK = 2060
NCST = 2062


ALIAS = {}
for _b in range(10):
    ALIAS["xbcT%d" % _b] = ["u%d" % _b]
ALIAS["qr"] = ["u%d" % i for i in range(10, 14)]
ALIAS["kr"] = ["u%d" % i for i in range(14, 18)]
ALIAS["v"] = ["u%d" % i for i in range(18, 26)]
ALIAS["ymT"] = ["u%d" % i for i in range(0, 8)]
ALIAS["yrT"] = ["u%d" % i for i in range(8, 16)]
ALIAS["mrg"] = ["u%d" % i for i in range(16, 24)]
ALIAS["mT"] = ["u%d" % i for i in range(26, 34)]


class _Op:
    __slots__ = ("eng", "fn", "deps", "idx", "sig", "sigidx", "dmakey", "dmaval", "alldeps", "cidx", "cost", "lat",
                 "aseg", "pos", "fin", "nrem", "users", "ready", "tag", "st", "bl", "ks", "vc")


class _FirstWait:
    def __init__(self, e, sem, val):
        self._e, self._sem, self._val, self._done = e, sem, val, False

    def __getattr__(self, name):
        attr = getattr(self._e, name)
        if not callable(attr):
            return attr

        def call(*a, **k):
            ins = attr(*a, **k)
            if not self._done:
                ins._wait_ge(self._sem, self._val)
                self._done = True
            return ins
        return call


class Sched:
    ENGS = ("pe", "act", "dve", "pool", "sp")

    def __init__(self):
        self.ops = {e: [] for e in self.ENGS}
        self.all = []
        self.lastw = {}
        self.readers = {}
        self.dmacount = {}
        self.aseg = 0
        self.agroup = None

    def add(self, eng, fn, reads=(), writes=(), dma=None, cost=0.1, lat=0.0, agroup=None):
        op = _Op()
        op.eng, op.fn, op.sig, op.sigidx = eng, fn, False, 0
        op.cidx = len(self.all)
        op.cost, op.lat = cost, lat
        import sys as _sys
        fr = _sys._getframe(1)
        while fr.f_code.co_name in ("dma", "act", "tt", "ts", "stt", "cp", "pe", "rstd_pool"):
            fr = fr.f_back
        op.tag = fr.f_lineno
        op.dmakey = dma
        if dma is not None:
            self.dmacount[dma] = self.dmacount.get(dma, 0) + 1
            op.dmaval = 16 * self.dmacount[dma]
        if eng == "act" and agroup is not None and agroup != self.agroup:
            self.agroup = agroup
            self.aseg += 1
        op.aseg = self.aseg if (agroup is not None or not ACT_FREE) else -1
        deps = {}
        reads = [kk for k in reads for kk in ALIAS.get(k, (k,))]
        writes = [kk for k in writes for kk in ALIAS.get(k, (k,))]
        if PSX and eng in ("act", "dve"):
            extra = ["rd_" + k for k in reads if k.startswith("ps")]
            if extra:
                writes = list(writes) + extra

        def consider(d):
            if d is not None and d is not op:
                deps[d.cidx] = d

        for k in reads:
            consider(self.lastw.get(k))
        for k in writes:
            consider(self.lastw.get(k))
            for r in self.readers.get(k, ()):
                consider(r)
        op.alldeps = list(deps.values())
        for k in reads:
            self.readers.setdefault(k, []).append(op)
        for k in writes:
            self.lastw[k] = op
            self.readers[k] = []
        self.ops[eng].append(op)
        self.all.append(op)
        return op

    def reorder(self, enable=True):
        for op in self.all:
            op.users = []
            op.nrem = 0
            op.ready = 0.0
        for op in self.all:
            for d in op.alldeps:
                d.users.append(op)
                op.nrem += 1
        if not enable:
            neword = {e: list(self.ops[e]) for e in self.ENGS}
        else:
            avail = {e: [] for e in self.ENGS}
            for op in self.all:
                if op.nrem == 0:
                    avail[op.eng].append(op)
            tE = {e: 0.0 for e in self.ENGS}
            neword = {e: [] for e in self.ENGS}
            act_rem = {}
            for op in self.ops["act"]:
                if op.aseg >= 0:
                    act_rem[op.aseg] = act_rem.get(op.aseg, 0) + 1
            act_cur = min(act_rem) if act_rem else 0
            nleft = len(self.all)
            ptr = {e: 0 for e in self.ENGS}
            orig = {e: list(self.ops[e]) for e in self.ENGS}
            nxt = {e: (orig[e][0].cidx if orig[e] else -1) for e in self.ENGS}
            for op in reversed(self.all):
                m = 0.0
                for u in op.users:
                    if u.bl > m:
                        m = u.bl
                op.bl = op.cost + op.lat + m
            while nleft:
                best = None
                for e in self.ENGS:
                    te = tE[e]
                    cands = []
                    t0 = None
                    for op in avail[e]:
                        if e == "act" and op.aseg >= 0 and op.aseg != act_cur:
                            continue
                        if e in FREEZE and op.cidx != nxt[e]:
                            continue
                        st = op.ready if op.ready > te else te
                        cands.append((st, op))
                        if t0 is None or st < t0:
                            t0 = st
                    if not cands:
                        continue
                    pick = None
                    for st, op in cands:
                        if st <= t0 + SLACK:
                            if pick is None or (op.bl, -op.cidx) > (pick[1].bl, -pick[1].cidx):
                                pick = (st, op)
                    key = (pick[0], pick[1].cidx)
                    if best is None or key < best[0]:
                        best = (key, pick[1])
                assert best is not None, "scheduler stuck"
                (st, _), op = best
                e = op.eng
                avail[e].remove(op)
                tE[e] = st + op.cost
                op.st = st
                op.fin = st + op.cost + op.lat
                neword[e].append(op)
                nleft -= 1
                if e in FREEZE:
                    ptr[e] += 1
                    nxt[e] = orig[e][ptr[e]].cidx if ptr[e] < len(orig[e]) else -1
                if e == "act" and op.aseg >= 0:
                    act_rem[op.aseg] -= 1
                    while act_rem.get(act_cur, 0) == 0 and act_rem:
                        act_rem.pop(act_cur, None)
                        if not act_rem:
                            break
                        act_cur = min(act_rem)
                for u in op.users:
                    f_ = op.fin + (XLAT if u.eng != op.eng else 0.0)
                    if f_ > u.ready:
                        u.ready = f_
                    u.nrem -= 1
                    if u.nrem == 0:
                        avail[u.eng].append(u)
            self.est = max(tE.values())
        self.ops = neword
        for e in self.ENGS:
            for i, op in enumerate(self.ops[e]):
                op.pos = i
        prev = {}
        for e in self.ENGS:
            p = None
            for op in self.ops[e]:
                prev[id(op)] = p
                p = op
        order = sorted(self.all, key=lambda o: (o.st, o.cidx)) if enable else list(self.all)
        for op in order:
            p = prev[id(op)]
            k = dict(p.ks) if p is not None else {}
            waits = []
            deps = sorted(op.alldeps, key=lambda d: -d.fin) if enable else list(op.alldeps)
            for d in deps:
                if d.dmakey is not None:
                    if k.get("dma:" + d.dmakey, 0) >= d.dmaval:
                        continue
                    waits.append(d)
                    k["dma:" + d.dmakey] = d.dmaval
                    continue
                if d.eng == "pe" and op.eng == "pe":
                    continue
                if k.get(d.eng, -1) >= d.pos:
                    continue
                waits.append(d)
                for kk, vv in d.vc.items():
                    if k.get(kk, -1) < vv:
                        k[kk] = vv
            op.ks = k
            op.deps = waits
            vc = dict(k)
            vc[op.eng] = op.pos
            if op.dmakey is not None:
                vc = {"dma:" + op.dmakey: op.dmaval}
                vc.update({kk: vv for kk, vv in k.items()})
                vc.pop(op.eng, None) if False else None
            op.vc = vc
            for d in waits:
                if d.dmakey is None:
                    d.sig = True
        for e in self.ENGS:
            n = 0
            for op in self.ops[e]:
                if op.sig and op.dmakey is None:
                    n += 1
                    op.sigidx = n

    def emit(self, eng_name, e, engsem, dmasem):
        waited = {}
        for op in self.ops[eng_name]:
            need = []
            for d in op.deps:
                if d.dmakey is not None:
                    sem, val = dmasem[d.dmakey], d.dmaval
                else:
                    if d.eng == "pe" and eng_name == "pe":
                        continue
                    sem, val = engsem[d.eng], d.sigidx
                key = id(sem)
                if waited.get(key, 0) >= val:
                    continue
                waited[key] = val
                need.append((sem, val))
            attach = None
            if ATTACH_WAIT and need and op.fn is not None:
                attach = need.pop()
            for sem, val in need:
                e.wait_ge(sem, val)
            if attach is not None:
                ins = op.fn(_FirstWait(e, attach[0], attach[1]))
            else:
                ins = op.fn(e) if op.fn is not None else None
            if op.dmakey is not None:
                ins.then_inc(dmasem[op.dmakey], 16)
            elif op.sig:
                assert ins is not None
                ins.then_inc(engsem[eng_name], 1)


def build(dbg=False, stop=None, nsc=NSC):
    nc = bass.Bass("TRN2", target_bir_lowering=False)
    S = Sched()

    def din(name, shape):
        return nc.dram_tensor(name, list(shape), F32, kind="ExternalInput").ap()

    x_d = din("x", [L, D])
    cT_d = din("cT", [128, 16])
    wada_d = din("w_ada", [D, 3 * D])
    badaT_d = din("b_adaT", [128, 24])
    bgate_d = din("b_gate", [1, D])
    nwT_d = din("norm_wT", [128, 8])
    win_d = din("w_in", [D, DIN])
    cwT_d = din("conv_wT", [128, 40])
    cbT_d = din("conv_bT", [128, 10])
    dtb_d = din("dt_bias", [1, 16])
    alog_d = din("a_log", [1, 16])
    dsk_d = din("d_skip", [1, 16])
    dskT_d = din("dskT", [128, 8])
    mnwT_d = din("m_norm_wT", [128, 8])
    wm_d = din("w_proj_m", [D, D])
    wr_d = din("w_proj_r", [D, D])
    wo_d = din("w_out", [D, D])
    fnw_d = din("fnw", [1, D])
    cos_d = din("cosT", [L, 32])
    sin_d = din("sinT", [L, 32])
    cst_d = din("cst", [128, NCST])
    out_d = nc.dram_tensor("out", [L, D], F32, kind="ExternalOutput").ap()
    dbg_outs = {}

    es = ExitStack()

    def sb(name, shape, dt):
        return es.enter_context(nc.sbuf_tensor("s_" + name, list(shape), dt))

    def rowsize(h):
        r = 1
        for s in h.shape[1:]:
            r *= s
        return r

    def mk(h, p0, npart, off, dims):
        return bass.AP(h, p0 * rowsize(h) + off, [[rowsize(h), npart]] + [list(d) for d in dims])

    cst = sb("cst", [128, NCST], F32)
    ident_b = sb("ident_b", [128, 128], BF16)
    vec = sb("vec", [128, 144], F32)
    V_CT, V_BADA, V_NW, V_CW, V_CB, V_MNW, V_G, V_SH, V_NH, V_DSK = 0, 16, 40, 48, 88, 98, 106, 114, 122, 128
    rowv = sb("rowv", [128, 64], F32)
    gate_bc = sb("gate_bc", [128, D], F32)
    fnw_bc = sb("fnw_bc", [128, D], F32)
    wdt = sb("wdt", [128, 8, 16], BF16)
    ring = sb("ring", [128, NSLOT * 2048], F32)
    x1 = sb("x1", [128, 2, D], F32)
    xo = x1
    xs2 = sb("xs", [128, 2, D], BF16)
    junk = xs2[:, 0, :]
    hT = sb("hT", [128, 8, T], BF16)
    halo = sb("halo", [128, 10, 3], F32)
    rawt = sb("rawt", [128, 2, T + 3], F32)
    acc = sb("acc", [128, 2, T], F32)
    cs_t = sb("cs_t", [128, 2, NTC, 32], F32)
    dtt = sb("dtt", [128, NTC, 16], F32)
    uni = sb("uni", [128, 17408], BF16)
    y = sb("y", [128, NTC, D], F32)
    on = sb("on", [128, NTC, D], BF16)
    st = sb("st", [128, 64], F32)
    sm = sb("sm", [128, 128], F32)
    rhsA = sb("rhsA", [128, 1, 8, 128], F32)
    expseg = sb("expseg", [128, 2, 8, 128], BF16)
    Gm = sb("Gm", [128, 2, 128], F32)
    Wt = sb("Wt", [128, 16, 128], BF16)
    x_dt2 = sb("x_dt", [128, 2, D], BF16)
    xw2 = sb("xw", [128, 2, D], BF16)
    DD = sb("DD", [128, 8, 128], BF16)
    junk2 = sb("junk2", [128, 128], BF16)
    cTb = sb("cTb", [128, 16], BF16)
    Btok = sb("Btok", [128, 128], BF16)
    t1 = sb("t1", [128, D], F32)
    Sst = sb("Sst", [128, 512], F32)
    Sbf = sb("Sbf", [128, 512], BF16)
    qT = sb("qT", [128, 4, 128], BF16)
    kT = sb("kT", [128, 2, 4, 128], BF16)
    BCz = sb("BCz", [128, 1, 2, T], BF16)
    qwT = sb("qwT", [128, 4, 128], BF16)
    kw = sb("kw", [128, 512], BF16)
    PT = sb("PT", [128, 8, 128], BF16)
    Rst = sb("Rst", [128, 4, 128], F32)
    Rbf = sb("Rbf", [128, 2, 4, 128], BF16)
    sz = sb("sz", [128, 2, 512], F32)
    ymb = sb("ymb", [128, 2, 512], BF16)
    tmg = sz
    amr = sb("amr", [128, 2, 512], F32)

    def uview(off, dims):
        return mk(uni, 0, 128, off, dims)

    O_XBC, O_QR, O_KR, O_V = 0, 10 * T, 10 * T + NTC * 512, 10 * T + 2 * NTC * 512
    O_YMT, O_YRT, O_MRG, O_MT = 0, 8 * T, 16 * T, 26 * 512
    assert O_V + NTC * 1024 <= O_MT and O_MRG + NTC * 1024 <= O_MT and O_MT + 8 * T <= 17408

    PS = [es.enter_context(nc.psum_tensor(f"ps{i}", [128, 512], F32)) for i in range(8)]

    def psb(i):
        return PS[i][:].bitcast(BF16)

    engsem = {e: es.enter_context(nc.semaphore("sem_" + e)) for e in ("pe", "act", "dve", "pool")}
    dmasem = {}

    def dsem(key):
        if key not in dmasem:
            dmasem[key] = es.enter_context(nc.semaphore("d_" + key))
        return key

    def fsz(ap):
        n = 1
        for d in ap.shape[1:]:
            n *= d
        return n

    def dma(eng, out, in_, reads, writes, key, nbytes=None):
        dsem(key)
        if nbytes is None:
            nbytes = 128 * fsz(out) * 4
        return S.add(eng, lambda e, o=out, i=in_: e.dma_start(out=o, in_=i), reads, writes, dma=key,
                     cost=(1.2 if eng == "pool" else 0.15), lat=2.0 + nbytes / 120e3)

    AGROUP = {AF.Silu: "g18", AF.Tanh: "g18", AF.Exp: "g6", AF.Ln: "g6"}

    def act(out, in_, func, reads, writes, **kw):
        return S.add("act", lambda e: e.activation(out=out, in_=in_, func=func, **kw), reads, writes,
                     cost=0.22 + fsz(out) / 1200.0, agroup=AGROUP.get(func))

    def ecost(eng, out):
        n = fsz(out)
        return (0.07 + n / 960.0) if eng == "dve" else (0.15 + n / 450.0)

    def tt(eng, out, in0, in1, op, reads, writes):
        return S.add(eng, lambda e: e.tensor_tensor(out=out, in0=in0, in1=in1, op=op), reads, writes, cost=ecost(eng, out))

    def ts(eng, out, in0, s1, s2, op0, op1, reads, writes):
        c = ecost(eng, out)
        if s2 is None:
            return S.add(eng, lambda e: e.tensor_scalar(out=out, in0=in0, scalar1=s1, scalar2=None, op0=op0), reads, writes, cost=c)
        return S.add(eng, lambda e: e.tensor_scalar(out=out, in0=in0, scalar1=s1, scalar2=s2, op0=op0, op1=op1), reads, writes, cost=c)

    def stt(out, in0, scalar, in1, op0, op1, reads, writes):
        return S.add("dve", lambda e: e.scalar_tensor_tensor(out=out, in0=in0, scalar=scalar, in1=in1, op0=op0, op1=op1), reads, writes,
                     cost=ecost("dve", out))

    def cp(eng, out, in_, reads, writes):
        return S.add(eng, lambda e: e.tensor_copy(out=out, in_=in_), reads, writes, cost=ecost(eng, out))

    def pe(fn, reads, writes, cost=1.0):
        return S.add("pe", fn, reads, writes, cost=cost * PESCALE)

    def rstd_pool(dst, src, inv_n, reads_key, write_key, tmpcol, reads=None):
        tmp = st[:, tmpcol:tmpcol + src.shape[1]]
        ts("pool", tmp, src, inv_n, EPS, ALU.mult, ALU.add, reads if reads is not None else [reads_key], ["st_tmp%d" % tmpcol])
        nh = vec[:, V_NH:V_NH + 1].to_broadcast([128, src.shape[1]]) if src.shape[1] > 1 else vec[:, V_NH:V_NH + 1]
        tt("pool", dst, tmp, nh, ALU.pow, ["st_tmp%d" % tmpcol, "vec"], [write_key])

    def dump(name, src, shape, reads, dt=F32):
        if not dbg:
            return
        d = nc.dram_tensor("dbg_" + name, list(shape), dt, kind="ExternalOutput").ap()
        dbg_outs[name] = d
        dma("sp", d, src, reads, ["dbgout_" + name], "dbg_" + name)

    wreq = []
    for cb in range(4):
        wreq.append((wada_d, cb * 512, 512))
    for sc in range(NSC):
        for (c0, n) in ((C_XBC, 512), (C_XBC + 512, 512), (C_XBC + 1024, 256), (C_Q, 512), (C_K, 512), (C_V, 512), (C_V + 512, 512),
                        (C_Z, 512), (C_Z + 512, 512), (C_G, 512), (C_G + 512, 512)):
            wreq.append((win_d, c0, n))
        if sc == 0:
            wreq.append((wada_d, 4 * 512, 512))
            wreq.append((wada_d, 5 * 512, 512))
        for hf in range(2):
            wreq.append((wm_d, hf * 512, 512))
            wreq.append((win_d, C_GAM + hf * 512, 512))
            wreq.append((wr_d, hf * 512, 512))
            wreq.append((win_d, C_GAR + hf * 512, 512))
        wreq.append((wo_d, 0, 512))
        wreq.append((wo_d, 512, 512))
    wstate = {"issued": 0, "next": 0}

    def slot_ap(s):
        return ring[:, s * 2048:(s + 1) * 2048].bitcast(BF16).rearrange("p (k n) -> p k n", k=8)

    def wissue_upto(r):
        while wstate["issued"] <= min(r, len(wreq) - 1):
            i = wstate["issued"]
            src, c0, n = wreq[i]
            s = i % NSLOT
            srcap = src.rearrange("(kc p) n -> p kc n", p=128)[:, :, c0:c0 + n]
            dma("pool", slot_ap(s)[:, :, 0:n], srcap, [], ["ring%d" % s], "ring%d" % s)
            wstate["issued"] += 1

    def wgroup(specs):
        r0 = wstate["next"]
        wissue_upto(r0 + NSLOT - 1)
        outl = []
        for (expect_src, expect_c0) in specs:
            r = wstate["next"]
            src, c0, n = wreq[r]
            assert src is expect_src and c0 == expect_c0, (r, c0, expect_c0)
            assert r <= r0 + NSLOT - 1
            wstate["next"] += 1
            s = r % NSLOT
            outl.append((slot_ap(s), "ring%d" % s))
        return outl

    def wnext(expect_src, expect_c0):
        return wgroup([(expect_src, expect_c0)])[0]

    dma("sp", cst[:], cst_d[:, :], [], ["cst"], "cst")
    dma("sp", vec[:, V_CT:V_CT + 16], cT_d[:, :], [], ["vec"], "v0")
    dma("sp", vec[:, V_BADA:V_BADA + 24], badaT_d[:, :], [], ["vec"], "v1")
    dma("sp", vec[:, V_NW:V_NW + 8], nwT_d[:, :], [], ["vec"], "v2")
    dma("sp", vec[:, V_CW:V_CW + 40], cwT_d[:, :], [], ["vec"], "v3")
    dma("sp", vec[:, V_CB:V_CB + 10], cbT_d[:, :], [], ["vec"], "v4")
    dma("sp", vec[:, V_MNW:V_MNW + 8], mnwT_d[:, :], [], ["vec"], "v5")
    dma("sp", vec[:, V_DSK:V_DSK + 8], dskT_d[:, :], [], ["vec"], "v6")
    dma("sp", rowv[:, 0:16], bass.AP(dtb_d.tensor, 0, [[0, 128], [1, 16]]), [], ["rowv"], "r0")
    dma("sp", rowv[:, 48:64], bass.AP(alog_d.tensor, 0, [[0, 128], [1, 16]]), [], ["rowv"], "r1")
    dma("sp", rowv[:, 32:48], bass.AP(dsk_d.tensor, 0, [[0, 128], [1, 16]]), [], ["rowv"], "r2")
    dma("sp", gate_bc[:], bass.AP(bgate_d.tensor, 0, [[0, 128], [1, D]]), [], ["gate_bc"], "gb")
    dma("sp", fnw_bc[:], bass.AP(fnw_d.tensor, 0, [[0, 128], [1, D]]), [], ["fnw_bc"], "fb")
    dma("pool", wdt[:], win_d.rearrange("(kc p) n -> p kc n", p=128)[:, :, C_DT:C_DT + 16], [], ["wdt"], "wdt")
    S.add("pool", lambda e: e.memset(vec[:, V_NH:V_NH + 1], -0.5), [], ["vec"])
    S.add("pool", lambda e: e.memset(halo[:], 0.0), [], ["halo"])
    S.add("pool", lambda e: e.memset(Sst[:], 0.0), [], ["Sst"])
    S.add("pool", lambda e: e.memset(Sbf[:], 0.0), [], ["Sbf"])
    S.add("pool", lambda e: e.memset(Rst[:], 0.0), [], ["Rst"])
    S.add("pool", lambda e: e.memset(Rbf[:], 0.0), [], ["Rbf"])
    cp("dve", ident_b[:], cst[:, K_ID:K_ID + 128], ["cst"], ["ident_b"])
    act(rowv[:, 16:32], rowv[:, 48:64], AF.Exp, ["rowv"], ["rowv"])
    ts("dve", rowv[:, 16:32], rowv[:, 16:32], -1.0, None, ALU.mult, None, ["rowv"], ["rowv"])

    U_f = cst[:, K_U:K_U + 128]
    L_f = cst[:, K_L:K_L + 128]
    ones_f = cst[:, K_ONES:K_ONES + 128]

    cp("dve", cTb[:], vec[:, V_CT:V_CT + 16], ["vec"], ["cTb"])
    cbc = amr[:, 0, :].bitcast(BF16).rearrange("p (k m) -> p k m", k=8)
    for kc in range(8):
        cp("dve", cbc[:, kc, :], vec[:, V_CT + kc:V_CT + kc + 1].to_broadcast([128, 128]), ["vec"], ["amr0"])
    for blk in range(8):
        ts("dve", DD[:, blk, :], ident_b[:], vec[:, V_DSK + blk:V_DSK + blk + 1], None, ALU.mult, None, ["ident_b", "vec"], ["DD"])
    ps_mod = PS[3]
    for cb in range(4):
        W, wkey = wnext(wada_d, cb * 512)

        def f(e, W=W, cb=cb):
            ins = None
            for sub in range(4):
                j = cb * 4 + sub
                for kc in range(8):
                    ins = e.matmul(ps_mod[:, 2 * j:2 * j + 2], lhsT=W[:, kc, sub * 128:(sub + 1) * 128],
                                   rhs=cTb[:, kc:kc + 2], start=(kc == 0), stop=(kc == 7))
            return ins
        pe(f, [wkey, "cTb"], ["ps3"], cost=2.6)

    def gate_setup():
        for hf in range(2):
            W, wkey = wnext(wada_d, (4 + hf) * 512)

            def f(e, W=W, hf=hf):
                ins = None
                for kc in range(8):
                    ins = e.matmul(PS[hf][:], lhsT=cbc[:, kc, :], rhs=W[:, kc, :], start=(kc == 0), stop=(kc == 7))
                return ins
            pe(f, [wkey, "amr0"], ["ps%d" % hf], cost=2.1)
            tt("dve", gate_bc[:, hf * 512:(hf + 1) * 512], PS[hf][:], gate_bc[:, hf * 512:(hf + 1) * 512], ALU.add, ["ps%d" % hf, "gate_bc"], ["gate_bc"])
        ts("dve", gate_bc[:], gate_bc[:], 0.5, None, ALU.mult, None, ["gate_bc"], ["gate_bc"])
    modv = mk(ps_mod, 0, 128, 0, [[2, 16]])
    tt("dve", st[:, 0:16], modv, vec[:, V_BADA:V_BADA + 16], ALU.add, ["ps3", "vec"], ["st_mod"])
    cp("dve", vec[:, V_SH:V_SH + 8], st[:, 0:8], ["st_mod"], ["vec"])
    stt(vec[:, V_G:V_G + 8], st[:, 8:16], 1.0, vec[:, V_NW:V_NW + 8], ALU.add, ALU.mult, ["st_mod", "vec"], ["vec"])

    xbcT = uview(O_XBC, [[T, 10], [1, T]])
    qr = uview(O_QR, [[512, NTC], [1, 512]])
    kr = uview(O_KR, [[512, NTC], [1, 512]])
    vv = uview(O_V, [[1024, NTC], [1, 1024]])
    ymT = uview(O_YMT, [[T, 8], [1, T]])
    yrT = uview(O_YRT, [[T, 8], [1, T]])
    mrg = uview(O_MRG, [[1024, NTC], [1, 1024]])
    mT = uview(O_MT, [[T, 8], [1, T]])
    A_KEYS = ["xbcT%d" % b for b in range(10)] + ["qr", "kr", "v"]
    B_KEYS = ["ymT", "yrT", "mrg", "mT"]

    mmbank = {"i": 0}

    def nextbank(banks=(0, 1)):
        b = banks[mmbank["i"] % len(banks)]
        mmbank["i"] += 1
        return b

    for sc in range(nsc if stop != 'setup' else 0):
        dma("sp", cs_t[:, 0, :, :], cos_d.rearrange("(c p) f -> p c f", p=128)[:, sc * NTC:(sc + 1) * NTC, :], [], ["cs_t"], "cs0")
        dma("sp", cs_t[:, 1, :, :], sin_d.rearrange("(c p) f -> p c f", p=128)[:, sc * NTC:(sc + 1) * NTC, :], [], ["cs_t"], "cs1")
        for c in range(NTC):
            gc = sc * NTC + c
            b = gc % 2
            dma("sp", x1[:, b, :], x_d[gc * 128:(gc + 1) * 128, :], [], ["x1_%d" % b], "x1_%d" % b)
            xs = xs2[:, b, :]
            xk = "xs%d" % b
            sc0 = 16 + 4 * b
            act(xs, x1[:, b, :], AF.Square, ["x1_%d" % b], [xk, "st_ss%d" % b], accum_out=st[:, sc0:sc0 + 1])
            rstd_pool(st[:, sc0 + 1:sc0 + 2], st[:, sc0:sc0 + 1], 1.0 / D, "st_ss%d" % b, "st_rstd%d" % b, sc0 + 2)
            act(xs, x1[:, b, :], AF.Copy, ["x1_%d" % b, "st_rstd%d" % b], [xk], scale=st[:, sc0 + 1:sc0 + 2])
            pb = 2 + b

            def f(e, xs=xs, pb=pb):
                ins = None
                for kc in range(8):
                    ins = e.transpose(psb(pb)[:, kc * 128:(kc + 1) * 128], xs[:, kc * 128:(kc + 1) * 128], ident_b[:])
                return ins
            pe(f, [xk, "ident_b"], ["ps%d" % pb], cost=1.0)
            for kc in range(8):
                if kc % 2 == 0:
                    act(hT[:, kc, c * 128:(c + 1) * 128], psb(pb)[:, kc * 128:(kc + 1) * 128], AF.Identity, ["ps%d" % pb, "vec"], ["hT%d" % c],
                        scale=vec[:, V_G + kc:V_G + kc + 1], bias=vec[:, V_SH + kc:V_SH + kc + 1])
                else:
                    ts("dve", hT[:, kc, c * 128:(c + 1) * 128], psb(pb)[:, kc * 128:(kc + 1) * 128],
                       vec[:, V_G + kc:V_G + kc + 1], vec[:, V_SH + kc:V_SH + kc + 1], ALU.mult, ALU.add,
                       ["ps%d" % pb, "vec"], ["hT%d" % c])
        HT_KEYS = ["hT%d" % c for c in range(NTC)]
        if stop == 'p1':
            continue
        if sc == 0:
            dump("hT", hT[:], [128, 8, T], HT_KEYS, BF16)

        wslots = wgroup([(win_d, C_XBC), (win_d, C_XBC + 512), (win_d, C_XBC + 1024)])
        for blk in range(10):
            W, wkey = wslots[blk // 4]
            sub = blk % 4
            bk = nextbank((4, 5, 6, 7))
            rb = blk % 2

            def f(e, W=W, sub=sub, bk=bk):
                ins = None
                for kc in range(8):
                    ins = e.matmul(PS[bk][:], lhsT=W[:, kc, sub * 128:(sub + 1) * 128], rhs=hT[:, kc, :], start=(kc == 0), stop=(kc == 7))
                return ins
            pe(f, [wkey] + HT_KEYS, ["ps%d" % bk], cost=2.1)
            cp("dve", rawt[:, rb, 0:3], halo[:, blk, :], ["halo"], ["rawt%d" % rb])
            act(rawt[:, rb, 3:3 + T], PS[bk][:], AF.Copy, ["ps%d" % bk], ["rawt%d" % rb])
            cp("dve", halo[:, blk, :], rawt[:, rb, T:T + 3], ["rawt%d" % rb], ["halo"])
            act(acc[:, rb, :], PS[bk][:], AF.Identity, ["ps%d" % bk, "vec"], ["acc%d" % rb],
                scale=vec[:, V_CW + blk * 4 + 3:V_CW + blk * 4 + 4], bias=vec[:, V_CB + blk:V_CB + blk + 1])
            for s_ in (1, 2, 3):
                stt(acc[:, rb, :], rawt[:, rb, 3 - s_:3 - s_ + T], vec[:, V_CW + blk * 4 + 3 - s_:V_CW + blk * 4 + 4 - s_],
                    acc[:, rb, :], ALU.mult, ALU.add, ["rawt%d" % rb, "acc%d" % rb, "vec"], ["acc%d" % rb])
            act(xbcT[:, blk, :], acc[:, rb, :], AF.Silu, ["acc%d" % rb], ["xbcT%d" % blk])
        if sc == 0:
            dump("xbcT", xbcT, [128, 10, T], ["xbcT%d" % b for b in range(10)], BF16)
        for g in range(2):
            act(BCz[:, 0, g, :], xbcT[:, 9, :], AF.Copy, ["xbcT9", "cst"], ["Cz"], scale=cst[:, K_MSK + g:K_MSK + g + 1])
        if stop == 'p2a':
            continue
        for c in range(NTC):
            def f(e, c=c):
                ins = None
                for kc in range(8):
                    ins = e.matmul(PS[3][:, 256:272], lhsT=hT[:, kc, c * 128:(c + 1) * 128], rhs=wdt[:, kc, :], start=(kc == 0), stop=(kc == 7))
                return ins
            pe(f, ["hT%d" % c, "wdt"], ["ps3"], cost=0.6)
            tt("dve", sm[:, 96:112], PS[3][:, 256:272], rowv[:, 0:16], ALU.add, ["ps3", "rowv"], ["sm_dtpre"])
            act(sm[:, 112:128], sm[:, 96:112], AF.Exp, ["sm_dtpre"], ["sm_e"])
            act(dtt[:, c, :], sm[:, 112:128], AF.Ln, ["sm_e"], ["dtt%d" % c], bias=1.0)
        if sc == 0:
            dump("dt", dtt[:], [128, NTC, 16], ["dtt%d" % c for c in range(NTC)])
        for (dst, dkey, c0) in ((qr, "qr", C_Q), (kr, "kr", C_K)):
            W, wkey = wnext(win_d, c0)
            for c in range(NTC):
                bk = nextbank()

                def f(e, W=W, c=c, bk=bk):
                    ins = None
                    for kc in range(8):
                        ins = e.matmul(PS[bk][:], lhsT=hT[:, kc, c * 128:(c + 1) * 128], rhs=W[:, kc, :], start=(kc == 0), stop=(kc == 7))
                    return ins
                pe(f, [wkey, "hT%d" % c], ["ps%d" % bk], cost=2.1)
                psv = mk(PS[bk], 0, 128, 0, [[64, 8], [32, 2], [1, 32]])
                cosb = mk(cs_t, 0, 128, c * 32, [[0, 8], [0, 2], [1, 32]])
                sinb = mk(cs_t, 0, 128, NTC * 32 + c * 32, [[0, 8], [0, 2], [1, 32]])
                tc_ = mk(t1, 0, 128, 0, [[64, 8], [32, 2], [1, 32]])
                ts_ = mk(t1, 0, 128, 512, [[64, 8], [32, 2], [1, 32]])
                tt("dve", tc_, psv, cosb, ALU.mult, ["ps%d" % bk, "cs_t"], ["t1a"])
                tt("dve", ts_, psv, sinb, ALU.mult, ["ps%d" % bk, "cs_t"], ["t1b"])
                dv = mk(uni, 0, 128, (O_QR if dkey == "qr" else O_KR) + c * 512, [[64, 8], [32, 2], [1, 32]])
                tt("dve", dv[:, :, 0, :], tc_[:, :, 0, :], ts_[:, :, 1, :], ALU.subtract, ["t1a", "t1b"], [dkey])
                tt("dve", dv[:, :, 1, :], ts_[:, :, 0, :], tc_[:, :, 1, :], ALU.add, ["t1a", "t1b"], [dkey])
        for half in range(2):
            W, wkey = wnext(win_d, C_V + half * 512)
            for c in range(NTC):
                bk = nextbank((4, 5, 6, 7))

                def f(e, W=W, c=c, bk=bk):
                    ins = None
                    for kc in range(8):
                        ins = e.matmul(PS[bk][:], lhsT=hT[:, kc, c * 128:(c + 1) * 128], rhs=W[:, kc, :], start=(kc == 0), stop=(kc == 7))
                    return ins
                pe(f, [wkey, "hT%d" % c], ["ps%d" % bk], cost=2.1)
                act(vv[:, c, half * 512:(half + 1) * 512], PS[bk][:], AF.Copy, ["ps%d" % bk], ["v"])
        if sc == 0:
            dump("qr", qr, [128, NTC, 512], ["qr"], BF16)
            dump("kr", kr, [128, NTC, 512], ["kr"], BF16)

        if stop == 'p2':
            continue
        for c in range(NTC):
            cs_ = slice(c * 128, (c + 1) * 128)
            a_sb, acs_sb, dE2, E1, CD, dtE2 = sm[:, 0:16], sm[:, 16:32], sm[:, 32:48], sm[:, 48:64], sm[:, 64:80], sm[:, 80:96]
            tt("dve", a_sb, dtt[:, c, :], rowv[:, 16:32], ALU.mult, ["dtt%d" % c, "rowv"], ["sm_a"])

            def f(e):
                e.matmul(PS[3][:, 256:272], lhsT=U_f, rhs=a_sb, start=True, stop=True)
                return e.matmul(PS[3][:, 272:288], lhsT=ones_f, rhs=a_sb, start=True, stop=True)
            pe(f, ["sm_a", "cst"], ["ps3"], cost=0.3)
            act(acs_sb, PS[3][:, 256:272], AF.Copy, ["ps3"], ["sm_acs"])
            tt("dve", dE2, PS[3][:, 272:288], acs_sb, ALU.subtract, ["ps3", "sm_acs"], ["sm_d"])
            act(dE2, dE2, AF.Exp, ["sm_d"], ["sm_d"])
            act(E1, acs_sb, AF.Exp, ["sm_acs"], ["sm_E1"])
            act(CD, PS[3][:, 272:288], AF.Exp, ["ps3"], ["sm_CD"])
            tt("dve", dtE2, dtt[:, c, :], dE2, ALU.mult, ["dtt%d" % c, "sm_d"], ["sm_dtE2"])

            if stop == 'p3a':
                continue
            def f(e, cs_=cs_):
                ins = None
                for blk in range(8):
                    ins = e.transpose(psb(2)[:, blk * 128:(blk + 1) * 128], xbcT[:, blk, cs_], ident_b[:])
                ins = e.transpose(psb(3)[:, 640:768], xbcT[:, 8, cs_], ident_b[:])
                return ins
            pe(f, ["xbcT%d" % b for b in range(9)] + ["ident_b"], ["ps2", "ps3"], cost=1.1)
            psxb = psb(2).rearrange("p (h q) -> p h q", h=16)

            def bc16(base_ap_tensor, col0):
                return mk(base_ap_tensor, 0, 128, col0, [[1, 16], [0, 64]])
            x_dt = x_dt2[:, c % 2, :]
            xw = xw2[:, c % 2, :]
            kxd, kxw = "x_dt%d" % (c % 2), "xw%d" % (c % 2)
            tt("dve", x_dt.rearrange("p (h q) -> p h q", h=16), psxb, mk(dtt, 0, 128, c * 16, [[1, 16], [0, 64]]), ALU.mult, ["ps2", "dtt%d" % c], [kxd])
            tt("dve", xw.rearrange("p (h q) -> p h q", h=16), psxb, bc16(sm, 80), ALU.mult, ["ps2", "sm_dtE2"], [kxw])
            act(Btok[:], psb(3)[:, 640:768], AF.Copy, ["ps3"], ["Btok"])

            if stop == 'p3b':
                continue
            def f(e, cs_=cs_):
                ins = None
                for g in range(2):
                    ins = e.matmul(PS[g][:, 0:128], lhsT=xbcT[:, 8, cs_], rhs=BCz[:, 0, g, cs_], start=True, stop=True)
                return ins
            pe(f, ["xbcT8", "Cz"], ["ps0", "ps1"], cost=0.2)
            for g in range(2):
                tt("dve", Gm[:, g, :], PS[g][:, 0:128], U_f, ALU.mult, ["ps%d" % g, "cst"], ["Gm%d" % g])

            for g in range(2):
                tt(OFFE("rhsA"), rhsA[:, 0, :, :], mk(cst, 0, 128, K_U, [[0, 8], [1, 128]]), mk(sm, 0, 128, g * 8, [[1, 8], [0, 128]]), ALU.mult, ["cst", "sm_a"], ["rhsA"])

                def f(e, g=g):
                    e.matmul(PS[4][:], lhsT=L_f, rhs=rhsA[:, 0, 0:4, :].rearrange("p h i -> p (h i)"), start=True, stop=True)
                    return e.matmul(PS[5][:], lhsT=L_f, rhs=rhsA[:, 0, 4:8, :].rearrange("p h i -> p (h i)"), start=True, stop=True)
                pe(f, ["rhsA", "cst"], ["ps4", "ps5"], cost=1.8)
                act(expseg[:, g, 0:4, :].rearrange("p h i -> p (h i)"), PS[4][:], AF.Exp, ["ps4"], ["expseg%da" % g])
                act(expseg[:, g, 4:8, :].rearrange("p h i -> p (h i)"), PS[5][:], AF.Exp, ["ps5"], ["expseg%db" % g])
                tt(OFFE("Wt"), Wt[:, g * 8:(g + 1) * 8, :], expseg[:, g, :, :], mk(Gm, 0, 128, g * 128, [[0, 8], [1, 128]]), ALU.mult,
                   ["expseg%da" % g, "expseg%db" % g, "Gm%d" % g], ["Wt%d" % g])

            if stop == 'p3c':
                continue
            for g in range(2):
                def f(e, g=g, cs_=cs_, x_dt=x_dt):
                    for j in range(4):
                        e.matmul(PS[6 + g][:, j * 128:(j + 1) * 128], lhsT=xbcT[:, g * 4 + j, cs_], rhs=DD[:, g * 4 + j, :], start=(j == 0), stop=False)
                    ins = None
                    for hl in range(8):
                        h = g * 8 + hl
                        ins = e.matmul(PS[6 + g][:, hl * 64:(hl + 1) * 64], lhsT=Wt[:, h, :], rhs=x_dt[:, h * 64:(h + 1) * 64], start=False, stop=(hl == 7))
                    return ins
                pe(f, ["DD", kxd, "Wt%d" % g] + ["xbcT%d" % (g * 4 + j) for j in range(4)], ["ps%d" % (6 + g)], cost=1.1)

                def f(e, g=g, cs_=cs_):
                    return e.matmul(PS[g][:], lhsT=BCz[:, 0, g, cs_], rhs=Sbf[:, :], start=True, stop=True)
                pe(f, ["Cz", "Sbf"], ["ps%d" % g], cost=0.27)
                tt("dve", t1[:, g * 512:(g + 1) * 512].rearrange("p (h q) -> p h q", h=8), PS[g][:].rearrange("p (h q) -> p h q", h=8),
                   mk(sm, 0, 128, 48 + g * 8, [[1, 8], [0, 64]]), ALU.mult, ["ps%d" % g, "sm_E1"], ["t1a" if g == 0 else "t1b"])
                tt("dve", y[:, c, g * 512:(g + 1) * 512], PS[6 + g][:], t1[:, g * 512:(g + 1) * 512], ALU.add,
                   ["ps%d" % (6 + g), "t1a" if g == 0 else "t1b"], ["y%d" % c])
            if stop == 'p3d':
                continue
            for g in range(2):
                def f(e, g=g, xw=xw):
                    return e.matmul(PS[4 + g][:], lhsT=Btok[:], rhs=xw[:, g * 512:(g + 1) * 512], start=True, stop=True)
                pe(f, ["Btok", kxw], ["ps%d" % (4 + g)], cost=0.27)
                r0 = g * 64
                tt(OFFE("SstCD"), mk(Sst, r0, 64, 0, [[64, 8], [1, 64]]), mk(Sst, r0, 64, 0, [[64, 8], [1, 64]]), mk(sm, r0, 64, 64 + g * 8, [[1, 8], [0, 64]]),
                   ALU.mult, ["Sst", "sm_CD"], ["Sst"])
                tt("dve", Sst[r0:r0 + 64, :], Sst[r0:r0 + 64, :], PS[4 + g][r0:r0 + 64, :], ALU.add, ["Sst", "ps%d" % (4 + g)], ["Sst"])
                act(Sbf[r0:r0 + 64, :], Sst[r0:r0 + 64, :], AF.Copy, ["Sst"], ["Sbf"])

            if stop == 'p3e':
                continue
            def f(e, c=c):
                ins = None
                for blk in range(4):
                    e.transpose(psb(2)[:, blk * 128:(blk + 1) * 128], qr[:, c, blk * 128:(blk + 1) * 128], ident_b[:])
                    ins = e.transpose(psb(2)[:, 512 + blk * 128:512 + (blk + 1) * 128], kr[:, c, blk * 128:(blk + 1) * 128], ident_b[:])
                return ins
            pe(f, ["qr", "kr", "ident_b"], ["ps2"], cost=1.0)
            act(qT[:].rearrange("p b i -> p (b i)"), psb(2)[:, 0:512], AF.Copy, ["ps2"], ["qT"])
            for hh in range(2):
                act(kT[:, hh, :, :].rearrange("p b i -> p (b i)"), psb(2)[:, 512:1024], AF.Copy, ["ps2", "cst"], ["kT"], scale=cst[:, K_MSK + hh:K_MSK + hh + 1])
            tt("dve", qwT[:].rearrange("p b i -> p (b i)"), psb(2)[:, 0:512], cst[:, K_GAM:K_GAM + 512], ALU.mult, ["ps2", "cst"], ["qwT"])
            tt(OFFE("kw"), kw[:].rearrange("p (h d) -> p h d", h=8), kr[:, c, :].rearrange("p (h d) -> p h d", h=8),
               mk(cst, 0, 128, K_KWV, [[1, 8], [0, 64]]), ALU.mult, ["kr", "cst"], ["kw"])

            def f(e):
                ins = None
                for h in range(8):
                    blk, hh = h // 2, h % 2
                    ins = e.matmul(PS[hh][:, blk * 128:(blk + 1) * 128], lhsT=kT[:, hh, blk, :],
                                   rhs=qT[:, blk, :], start=True, stop=True)
                return ins
            pe(f, ["qT", "kT"], ["ps0", "ps1"], cost=0.8)
            for bk in range(2):
                tt("dve", mk(PT, 0, 128, bk * 128, [[256, 4], [1, 128]]), PS[bk][:].rearrange("p (b i) -> p b i", b=4),
                   mk(cst, 0, 128, K_DMAT + bk * 128, [[256, 4], [1, 128]]), ALU.mult, ["ps%d" % bk, "cst"], ["PT%d" % bk])

            if stop == 'p3f':
                continue
            def f(e, c=c):
                ins = None
                for h in range(8):
                    blk, hh = h // 2, h % 2
                    o_ = PS[6 + hh][:, blk * 128:(blk + 1) * 128]
                    e.matmul(o_, lhsT=qwT[:, blk, :], rhs=Rbf[:, hh, blk, :], start=True, stop=False)
                    ins = e.matmul(o_, lhsT=PT[:, h, :], rhs=vv[:, c, h * 128:(h + 1) * 128], start=False, stop=True)
                return ins
            pe(f, ["qwT", "Rbf", "PT0", "PT1", "v"], ["ps6", "ps7"], cost=1.6)

            def f(e, c=c):
                ins = None
                for h in range(8):
                    blk = h // 2
                    ins = e.matmul(PS[4 + h // 4][:, (h % 4) * 128:(h % 4 + 1) * 128], lhsT=kw[:, blk * 128:(blk + 1) * 128],
                                   rhs=vv[:, c, h * 128:(h + 1) * 128], start=True, stop=True)
                return ins
            pe(f, ["kw", "v"], ["ps4", "ps5"], cost=0.8)
            tt(OFFE("RstRD"), Rst[:], Rst[:], mk(cst, 0, 128, K_RD, [[1, 4], [0, 128]]), ALU.mult, ["Rst", "cst"], ["Rst"])
            for bk in range(2):
                for hh in range(2):
                    r0 = hh * 64
                    tt("dve", Rst[r0:r0 + 64, 2 * bk:2 * bk + 2, :], Rst[r0:r0 + 64, 2 * bk:2 * bk + 2, :],
                       mk(PS[4 + bk], r0, 64, hh * 128, [[256, 2], [1, 128]]), ALU.add, ["Rst", "ps%d" % (4 + bk)], ["Rst"])
            for hh in range(2):
                act(Rbf[:, hh, :, :], Rst[:], AF.Copy, ["Rst", "cst"], ["Rbf"], scale=cst[:, K_MSK + hh:K_MSK + hh + 1])
            if stop == 'p3g':
                continue
            for h in range(8):
                blk, hh = h // 2, h % 2
                act(junk2[:], PS[6 + hh][:, blk * 128:(blk + 1) * 128], AF.Square, ["ps%d" % (6 + hh)], ["junk2", "st_ss8_%d" % h],
                    accum_out=st[:, 24 + h:25 + h])
            rstd_pool(st[:, 32:40], st[:, 24:32], 1.0 / 128, None, "st_rstd8", 40, reads=["st_ss8_%d" % h for h in range(8)])
            for bk in range(2):
                tt("dve", mk(on, 0, 128, c * 1024 + bk * 128, [[256, 4], [1, 128]]), PS[6 + bk][:].rearrange("p (b e) -> p b e", b=4),
                   mk(st, 0, 128, 32 + bk, [[2, 4], [0, 128]]), ALU.mult, ["ps%d" % (6 + bk), "st_rstd8"], ["on%d" % c])
        if sc == 0:
            dump("y", y[:], [128, NTC, D], ["y%d" % c for c in range(NTC)])
            dump("on", on[:], [128, NTC, D], ["on%d" % c for c in range(NTC)], BF16)

        if stop is not None and stop.startswith('p3'):
            continue
        for zb in range(2):
            W, wkey = wnext(win_d, C_Z + zb * 512)
            for c in range(NTC):
                bk = nextbank((0, 1, 4, 5))
                sb_ = c % 2
                cs_ = slice(c * 128, (c + 1) * 128)

                def f(e, W=W, c=c, bk=bk):
                    ins = None
                    for kc in range(8):
                        ins = e.matmul(PS[bk][:], lhsT=hT[:, kc, c * 128:(c + 1) * 128], rhs=W[:, kc, :], start=(kc == 0), stop=(kc == 7))
                    return ins
                pe(f, [wkey, "hT%d" % c], ["ps%d" % bk], cost=2.1)
                act(sz[:, sb_, :], PS[bk][:], AF.Silu, ["ps%d" % bk], ["sz%d" % sb_])
                ysl = y[:, c, zb * 512:(zb + 1) * 512]
                tt(OFFE("yz"), ysl, ysl, sz[:, sb_, :], ALU.mult, ["y%d" % c, "sz%d" % sb_], ["y%d" % c])
                act(junk[:, 0:512], ysl, AF.Square, ["y%d" % c], ["xs0", "st_ssg"], accum_out=st[:, 48:49])
                rstd_pool(st[:, 49:50], st[:, 48:49], 1.0 / 512, "st_ssg", "st_rstdg", 50)
                ts("dve", ymb[:, sb_, :], ysl, st[:, 49:50], None, ALU.mult, None, ["y%d" % c, "st_rstdg"], ["ymb%d" % sb_])

                def f(e, sb_=sb_):
                    ins = None
                    for j in range(4):
                        ins = e.transpose(psb(2 + sb_)[:, j * 128:(j + 1) * 128], ymb[:, sb_, j * 128:(j + 1) * 128], ident_b[:])
                    return ins
                pe(f, ["ymb%d" % sb_, "ident_b"], ["ps%d" % (2 + sb_)], cost=0.5)
                tt("dve", ymT[:, zb * 4:(zb + 1) * 4, cs_], psb(2 + sb_)[:, 0:512].rearrange("p (j t) -> p j t", j=4),
                   mk(vec, 0, 128, V_MNW + zb * 4, [[1, 4], [0, 128]]), ALU.mult, ["ps%d" % (2 + sb_), "vec"], ["ymT"])
        for gb in range(2):
            W, wkey = wnext(win_d, C_G + gb * 512)
            for c in range(NTC):
                bk = nextbank((0, 1, 4, 5))
                sb_ = c % 2
                cs_ = slice(c * 128, (c + 1) * 128)

                def f(e, W=W, c=c, bk=bk):
                    ins = None
                    for kc in range(8):
                        ins = e.matmul(PS[bk][:], lhsT=hT[:, kc, c * 128:(c + 1) * 128], rhs=W[:, kc, :], start=(kc == 0), stop=(kc == 7))
                    return ins
                pe(f, [wkey, "hT%d" % c], ["ps%d" % bk], cost=2.1)
                act(sz[:, sb_, :], PS[bk][:], AF.Silu, ["ps%d" % bk], ["sz%d" % sb_])
                tt(OFFE("yrb"), ymb[:, sb_, :], on[:, c, gb * 512:(gb + 1) * 512], sz[:, sb_, :], ALU.mult, ["on%d" % c, "sz%d" % sb_], ["ymb%d" % sb_])

                def f(e, sb_=sb_):
                    ins = None
                    for j in range(4):
                        ins = e.transpose(psb(2 + sb_)[:, j * 128:(j + 1) * 128], ymb[:, sb_, j * 128:(j + 1) * 128], ident_b[:])
                    return ins
                pe(f, ["ymb%d" % sb_, "ident_b"], ["ps%d" % (2 + sb_)], cost=0.5)
                act(yrT[:, gb * 4:(gb + 1) * 4, cs_], psb(2 + sb_)[:, 0:512].rearrange("p (j t) -> p j t", j=4), AF.Copy, ["ps%d" % (2 + sb_)], ["yrT"])
        if sc == 0:
            dump("ymT", ymT, [128, 8, T], ["ymT"], BF16)
            dump("yrT", yrT, [128, 8, T], ["yrT"], BF16)
        if stop == 'p4':
            continue
        if sc == 0:
            gate_setup()
        for hf in range(2):
            (Wm_, kWm), (Wgm, kWgm), (Wr_, kWr), (Wgr, kWgr) = wgroup(
                [(wm_d, hf * 512), (win_d, C_GAM + hf * 512), (wr_d, hf * 512), (win_d, C_GAR + hf * 512)])
            for c in range(NTC):
                cs_ = slice(c * 128, (c + 1) * 128)
                B5 = (4, 5, 6, 7) if c % 2 == 0 else (0, 1, 2, 3)
                for (bk, lh, lkey, W, wkey) in ((B5[0], ymT, "ymT", Wm_, kWm), (B5[1], hT, "hT%d" % c, Wgm, kWgm), (B5[2], yrT, "yrT", Wr_, kWr), (B5[3], hT, "hT%d" % c, Wgr, kWgr)):
                    def f(e, bk=bk, lh=lh, W=W, cs_=cs_):
                        ins = None
                        for kc in range(8):
                            ins = e.matmul(PS[bk][:], lhsT=lh[:, kc, cs_], rhs=W[:, kc, :], start=(kc == 0), stop=(kc == 7))
                        return ins
                    pe(f, [lkey, wkey], ["ps%d" % bk], cost=2.1)
                act(tmg[:, 0, :], PS[B5[1]][:], AF.Tanh, ["ps%d" % B5[1]], ["sz0"], scale=0.5)
                act(tmg[:, 1, :], PS[B5[3]][:], AF.Tanh, ["ps%d" % B5[3]], ["sz1"], scale=0.5)
                stt(amr[:, 0, :], tmg[:, 0, :], 1.0, PS[B5[0]][:], ALU.add, ALU.mult, ["sz0", "ps%d" % B5[0]], ["amr0"])
                stt(amr[:, 1, :], tmg[:, 1, :], 1.0, PS[B5[2]][:], ALU.add, ALU.mult, ["sz1", "ps%d" % B5[2]], ["amr1"])
                tt(OFFE("mrg"), mrg[:, c, hf * 512:(hf + 1) * 512], amr[:, 0, :], amr[:, 1, :], ALU.add, ["amr0", "amr1"], ["mrg"])
        for c in range(NTC):
            def f(e, c=c):
                ins = None
                for kc in range(8):
                    ins = e.transpose(psb(2)[:, kc * 128:(kc + 1) * 128], mrg[:, c, kc * 128:(kc + 1) * 128], ident_b[:])
                return ins
            pe(f, ["mrg", "ident_b"], ["ps2"], cost=1.0)
            act(mT[:, :, c * 128:(c + 1) * 128], psb(2).rearrange("p (k t) -> p k t", k=8), AF.Copy, ["ps2"], ["mT"])
        if sc == 0:
            dump("mrg", mrg, [128, NTC, D], ["mrg"], BF16)
        if stop == 'p5':
            continue
        (Wo0, kWo0), (Wo1, kWo1) = wgroup([(wo_d, 0), (wo_d, 512)])
        for c in range(NTC):
            gc = sc * NTC + c
            b = gc % 2
            cs_ = slice(c * 128, (c + 1) * 128)
            dma("sp", xo[:, b, :], x_d[gc * 128:(gc + 1) * 128, :], [], ["x1_%d" % b], "x1_%d" % b)
            for hf, (W, wkey) in enumerate(((Wo0, kWo0), (Wo1, kWo1))):
                bk = nextbank((0, 1, 4, 5))

                def f(e, W=W, cs_=cs_, bk=bk):
                    ins = None
                    for kc in range(8):
                        ins = e.matmul(PS[bk][:], lhsT=mT[:, kc, cs_], rhs=W[:, kc, :], start=(kc == 0), stop=(kc == 7))
                    return ins
                pe(f, ["mT", wkey], ["ps%d" % bk], cost=2.1)
                tt("dve", amr[:, hf, :], PS[bk][:], gate_bc[:, hf * 512:(hf + 1) * 512], ALU.mult, ["ps%d" % bk, "gate_bc"], ["amr%d" % hf])
                tt(OFFE("xoadd"), xo[:, b, hf * 512:(hf + 1) * 512], xo[:, b, hf * 512:(hf + 1) * 512], amr[:, hf, :], ALU.add, ["x1_%d" % b, "amr%d" % hf], ["x1_%d" % b])
            act(junk, xo[:, b, :], AF.Square, ["x1_%d" % b], ["xs0", "st_ssf"], accum_out=st[:, 52:53])
            rstd_pool(st[:, 53:54], st[:, 52:53], 1.0 / D, "st_ssf", "st_rstdf", 54)
            stt(xo[:, b, :], xo[:, b, :], st[:, 53:54], fnw_bc[:], ALU.mult, ALU.mult, ["x1_%d" % b, "st_rstdf", "fnw_bc"], ["x1_%d" % b])
            dma("sp", out_d[gc * 128:(gc + 1) * 128, :], xo[:, b, :], ["x1_%d" % b], ["out%d" % gc], "out%d" % b)
    allout = ["out%d" % gc for gc in range(NCH)] + ["dbgout_" + k for k in dbg_outs] + ["ring%d" % i for i in range(NSLOT)] + ["vec", "rowv", "gate_bc", "fnw_bc", "cst", "wdt", "cs_t", "x1_0", "x1_1"]
    S.add("sp", None, allout, [])

    S.reorder(REORDER)
    with nc.Block() as block:
        @block.sync
        def _(e):
            S.emit("sp", e, engsem, dmasem)

        @block.gpsimd
        def _(e):
            S.emit("pool", e, engsem, dmasem)

        @block.vector
        def _(e):
            S.emit("dve", e, engsem, dmasem)

        @block.scalar
        def _(e):
            S.emit("act", e, engsem, dmasem)

        @block.tensor
        def _(e):
            S.emit("pe", e, engsem, dmasem)
    es.close()
    return nc, dbg_outs


def _consts():
    H, Q = 8, 128
    log_g = np.log1p(-np.exp2(-5.0 - np.arange(H, dtype=np.float64)))
    idx = np.arange(Q, dtype=np.float64)
    cst = np.zeros((128, NCST), np.float64)
    rel = idx[None, :] - idx[:, None]
    for h in range(H):
        m = np.where(rel >= 0, np.exp(rel * log_g[h]), 0.0) * (64 ** -0.5)
        cst[:, K_DMAT + h * 128:K_DMAT + (h + 1) * 128] = m
    p = np.arange(128)
    hh = p // 64
    for blk in range(4):
        h = 2 * blk + hh
        cst[:, K_GAM + blk * 128:K_GAM + (blk + 1) * 128] = np.exp((idx[None, :] + 1) * log_g[h][:, None])
        cst[:, K_RD + blk] = np.exp(Q * log_g[h])
    for h in range(H):
        cst[:, K_KWV + h] = np.exp((Q - 1 - idx) * log_g[h]) * (64 ** -0.5)
    cst[:, K_U:K_U + 128] = (idx[:, None] <= idx[None, :])
    cst[:, K_L:K_L + 128] = (idx[:, None] > idx[None, :])
    cst[:, K_ONES:K_ONES + 128] = 1.0
    cst[:, K_ID:K_ID + 128] = np.eye(128)
    cst[:64, K_MSK] = 1.0
    cst[64:, K_MSK + 1] = 1.0
    half = 32
    inv = 10000.0 ** (-np.arange(half, dtype=np.float64) / half)
    ang = np.arange(L, dtype=np.float64)[:, None] * inv[None, :]
    cosT = np.cos(ang).astype(np.float32)
    sinT = np.sin(ang).astype(np.float32)
    return cst.astype(np.float32), cosT, sinT


_CACHE = {}


def _prep_inputs(inputs):
    f = lambda a: np.ascontiguousarray(np.asarray(a, dtype=np.float32))
    x = f(inputs["x"]); c = f(inputs["c"])
    cst, cosT, sinT = _consts()
    w_ada = f(inputs["w_ada"][0]); b_ada = f(inputs["b_ada"][0])
    shared = {
        "w_ada": w_ada,
        "b_adaT": np.ascontiguousarray(b_ada.reshape(24, 128).T),
        "b_gate": np.ascontiguousarray(b_ada[2048:3072].reshape(1, D)),
        "norm_wT": np.ascontiguousarray(f(inputs["norm_w"][0]).reshape(8, 128).T),
        "w_in": f(inputs["w_in"][0]),
        "conv_wT": np.ascontiguousarray(f(inputs["conv_w"][0]).reshape(4, 10, 128).transpose(2, 1, 0).reshape(128, 40)),
        "conv_bT": np.ascontiguousarray(f(inputs["conv_b"][0]).reshape(10, 128).T),
        "dt_bias": f(inputs["dt_bias"][0]).reshape(1, 16),
        "a_log": f(inputs["a_log"][0]).reshape(1, 16),
        "d_skip": f(inputs["d_skip"][0]).reshape(1, 16),
        "dskT": np.ascontiguousarray(np.repeat(f(inputs["d_skip"][0]), 64).reshape(8, 128).T),
        "m_norm_wT": np.ascontiguousarray(f(inputs["m_norm_w"][0]).reshape(8, 128).T),
        "w_proj_m": f(inputs["w_proj_m"][0]),
        "w_proj_r": f(inputs["w_proj_r"][0]),
        "w_out": f(inputs["w_out"][0]),
        "fnw": f(inputs["final_norm_w"]).reshape(1, D),
        "cosT": cosT, "sinT": sinT, "cst": cst,
    }
    in_maps = []
    for b in range(8):
        m = dict(shared)
        m["x"] = np.ascontiguousarray(x[b])
        cT = np.zeros((128, 16), np.float32)
        cT[:, 0:8] = c[b].reshape(8, 128).T
        m["cT"] = cT
        in_maps.append(m)
    return in_maps


def kernel(**inputs):
    in_maps = _prep_inputs(inputs)
    if "nc" not in _CACHE:
        _CACHE["nc"] = build(False)[0]
    nc = _CACHE["nc"]
    res = run_bass_kernel_spmd(nc, in_maps, core_ids=list(range(8)))
    out = np.stack([np.asarray(r["out"], dtype=np.float32) for r in res.results], axis=0)
    return out
```

```python
import numpy as np
from contextlib import ExitStack
import concourse.bass as bass
import concourse.mybir as mybir
from concourse.bass_utils import run_bass_kernel_spmd

F32, BF16 = mybir.dt.float32, mybir.dt.bfloat16
AF = mybir.ActivationFunctionType
ALU = mybir.AluOpType
AX = mybir.AxisListType

L, D = 2048, 1024
NCH = 16
NTC = 4
T = NTC * 128
NSC = NCH // NTC
DIN = 7440
NSLOT = 6
REORDER = True
SLACK = 0.0
XLAT = 0.7
ATTACH_WAIT = True
ACT_FREE = True
PESCALE = 0.85
FREEZE = set()
OFF = {'rhsA', 'Wt'}
PSX = True


def OFFE(name):
    return 'pool' if name in OFF else 'dve'
EPS = 1e-6
C_Z, C_XBC, C_DT, C_Q, C_K, C_V, C_G, C_GAM, C_GAR = 0, 1024, 2304, 2320, 2832, 3344, 4368, 5392, 6416
K_DMAT, K_GAM, K_KWV, K_RD, K_U, K_L, K_ONES, K_ID = 0, 1024, 1536, 1544, 1548, 1676, 1804, 1932
K_MSK = 2060
NCST = 2062


ALIAS = {}
for _b in range(10):
    ALIAS["xbcT%d" % _b] = ["u%d" % _b]
ALIAS["qr"] = ["u%d" % i for i in range(10, 14)]
ALIAS["kr"] = ["u%d" % i for i in range(14, 18)]
ALIAS["v"] = ["u%d" % i for i in range(18, 26)]
ALIAS["ymT"] = ["u%d" % i for i in range(0, 8)]
ALIAS["yrT"] = ["u%d" % i for i in range(8, 16)]
ALIAS["mrg"] = ["u%d" % i for i in range(16, 24)]
ALIAS["mT"] = ["u%d" % i for i in range(26, 34)]


class _Op:
    __slots__ = ("eng", "fn", "deps", "idx", "sig", "sigidx", "dmakey", "dmaval", "alldeps", "cidx", "cost", "lat",
                 "aseg", "pos", "fin", "nrem", "users", "ready", "tag", "st", "bl", "ks", "vc")


class _FirstWait:
    def __init__(self, e, sem, val):
        self._e, self._sem, self._val, self._done = e, sem, val, False

    def __getattr__(self, name):
        attr = getattr(self._e, name)
        if not callable(attr):
            return attr

        def call(*a, **k):
            ins = attr(*a, **k)
            if not self._done:
                ins._wait_ge(self._sem, self._val)
                self._done = True
            return ins
        return call


class Sched:
    ENGS = ("pe", "act", "dve", "pool", "sp")

    def __init__(self):
        self.ops = {e: [] for e in self.ENGS}
        self.all = []
        self.lastw = {}
        self.readers = {}
        self.dmacount = {}
        self.aseg = 0
        self.agroup = None

    def add(self, eng, fn, reads=(), writes=(), dma=None, cost=0.1, lat=0.0, agroup=None):
        op = _Op()
        op.eng, op.fn, op.sig, op.sigidx = eng, fn, False, 0
        op.cidx = len(self.all)
        op.cost, op.lat = cost, lat
        import sys as _sys
        fr = _sys._getframe(1)
        while fr.f_code.co_name in ("dma", "act", "tt", "ts", "stt", "cp", "pe", "rstd_pool"):
            fr = fr.f_back
        op.tag = fr.f_lineno
        op.dmakey = dma
        if dma is not None:
            self.dmacount[dma] = self.dmacount.get(dma, 0) + 1
            op.dmaval = 16 * self.dmacount[dma]
        if eng == "act" and agroup is not None and agroup != self.agroup:
            self.agroup = agroup
            self.aseg += 1
        op.aseg = self.aseg if (agroup is not None or not ACT_FREE) else -1
        deps = {}
        reads = [kk for k in reads for kk in ALIAS.get(k, (k,))]
        writes = [kk for k in writes for kk in ALIAS.get(k, (k,))]
        if PSX and eng in ("act", "dve"):
            extra = ["rd_" + k for k in reads if k.startswith("ps")]
            if extra:
                writes = list(writes) + extra

        def consider(d):
            if d is not None and d is not op:
                deps[d.cidx] = d

        for k in reads:
            consider(self.lastw.get(k))
        for k in writes:
            consider(self.lastw.get(k))
            for r in self.readers.get(k, ()):
                consider(r)
        op.alldeps = list(deps.values())
        for k in reads:
            self.readers.setdefault(k, []).append(op)
        for k in writes:
            self.lastw[k] = op
            self.readers[k] = []
        self.ops[eng].append(op)
        self.all.append(op)
        return op

    def reorder(self, enable=True):
        for op in self.all:
            op.users = []
            op.nrem = 0
            op.ready = 0.0
        for op in self.all:
            for d in op.alldeps:
                d.users.append(op)
                op.nrem += 1
        if not enable:
            neword = {e: list(self.ops[e]) for e in self.ENGS}
        else:
            avail = {e: [] for e in self.ENGS}
            for op in self.all:
                if op.nrem == 0:
                    avail[op.eng].append(op)
            tE = {e: 0.0 for e in self.ENGS}
            neword = {e: [] for e in self.ENGS}
            act_rem = {}
            for op in self.ops["act"]:
                if op.aseg >= 0:
                    act_rem[op.aseg] = act_rem.get(op.aseg, 0) + 1
            act_cur = min(act_rem) if act_rem else 0
            nleft = len(self.all)
            ptr = {e: 0 for e in self.ENGS}
            orig = {e: list(self.ops[e]) for e in self.ENGS}
            nxt = {e: (orig[e][0].cidx if orig[e] else -1) for e in self.ENGS}
            for op in reversed(self.all):
                m = 0.0
                for u in op.users:
                    if u.bl > m:
                        m = u.bl
                op.bl = op.cost + op.lat + m
            while nleft:
                best = None
                for e in self.ENGS:
                    te = tE[e]
                    cands = []
                    t0 = None
                    for op in avail[e]:
                        if e == "act" and op.aseg >= 0 and op.aseg != act_cur:
                            continue
                        if e in FREEZE and op.cidx != nxt[e]:
                            continue
                        st = op.ready if op.ready > te else te
                        cands.append((st, op))
                        if t0 is None or st < t0:
                            t0 = st
                    if not cands:
                        continue
                    pick = None
                    for st, op in cands:
                        if st <= t0 + SLACK:
                            if pick is None or (op.bl, -op.cidx) > (pick[1].bl, -pick[1].cidx):
                                pick = (st, op)
                    key = (pick[0], pick[1].cidx)
                    if best is None or key < best[0]:
                        best = (key, pick[1])
                assert best is not None, "scheduler stuck"
                (st, _), op = best
                e = op.eng
                avail[e].remove(op)
                tE[e] = st + op.cost
                op.st = st
                op.fin = st + op.cost + op.lat
                neword[e].append(op)
                nleft -= 1
                if e in FREEZE:
                    ptr[e] += 1
                    nxt[e] = orig[e][ptr[e]].cidx if ptr[e] < len(orig[e]) else -1
                if e == "act" and op.aseg >= 0:
                    act_rem[op.aseg] -= 1
                    while act_rem.get(act_cur, 0) == 0 and act_rem:
                        act_rem.pop(act_cur, None)
                        if not act_rem:
                            break
                        act_cur = min(act_rem)
                for u in op.users:
                    f_ = op.fin + (XLAT if u.eng != op.eng else 0.0)
                    if f_ > u.ready:
                        u.ready = f_
                    u.nrem -= 1
                    if u.nrem == 0:
                        avail[u.eng].append(u)
            self.est = max(tE.values())
        self.ops = neword
        for e in self.ENGS:
            for i, op in enumerate(self.ops[e]):
                op.pos = i
        prev = {}
        for e in self.ENGS:
            p = None
            for op in self.ops[e]:
                prev[id(op)] = p
                p = op
        order = sorted(self.all, key=lambda o: (o.st, o.cidx)) if enable else list(self.all)
        for op in order:
            p = prev[id(op)]
            k = dict(p.ks) if p is not None else {}
            waits = []
            deps = sorted(op.alldeps, key=lambda d: -d.fin) if enable else list(op.alldeps)
            for d in deps:
                if d.dmakey is not None:
                    if k.get("dma:" + d.dmakey, 0) >= d.dmaval:
                        continue
                    waits.append(d)
                    k["dma:" + d.dmakey] = d.dmaval
                    continue
                if d.eng == "pe" and op.eng == "pe":
                    continue
                if k.get(d.eng, -1) >= d.pos:
                    continue
                waits.append(d)
                for kk, vv in d.vc.items():
                    if k.get(kk, -1) < vv:
                        k[kk] = vv
            op.ks = k
            op.deps = waits
            vc = dict(k)
            vc[op.eng] = op.pos
            if op.dmakey is not None:
                vc = {"dma:" + op.dmakey: op.dmaval}
                vc.update({kk: vv for kk, vv in k.items()})
                vc.pop(op.eng, None) if False else None
            op.vc = vc
            for d in waits:
                if d.dmakey is None:
                    d.sig = True
        for e in self.ENGS:
            n = 0
            for op in self.ops[e]:
                if op.sig and op.dmakey is None:
                    n += 1
                    op.sigidx = n

    def emit(self, eng_name, e, engsem, dmasem):
        waited = {}
        for op in self.ops[eng_name]:
            need = []
            for d in op.deps:
                if d.dmakey is not None:
                    sem, val = dmasem[d.dmakey], d.dmaval
                else:
                    if d.eng == "pe" and eng_name == "pe":
                        continue
                    sem, val = engsem[d.eng], d.sigidx
                key = id(sem)
                if waited.get(key, 0) >= val:
                    continue
                waited[key] = val
                need.append((sem, val))
            attach = None
            if ATTACH_WAIT and need and op.fn is not None:
                attach = need.pop()
            for sem, val in need:
                e.wait_ge(sem, val)
            if attach is not None:
                ins = op.fn(_FirstWait(e, attach[0], attach[1]))
            else:
                ins = op.fn(e) if op.fn is not None else None
            if op.dmakey is not None:
                ins.then_inc(dmasem[op.dmakey], 16)
            elif op.sig:
                assert ins is not None
                ins.then_inc(engsem[eng_name], 1)


def build(dbg=False, stop=None, nsc=NSC):
    nc = bass.Bass("TRN2", target_bir_lowering=False)
    S = Sched()

    def din(name, shape):
        return nc.dram_tensor(name, list(shape), F32, kind="ExternalInput").ap()

    x_d = din("x", [L, D])
    cT_d = din("cT", [128, 16])
    wada_d = din("w_ada", [D, 3 * D])
    badaT_d = din("b_adaT", [128, 24])
    bgate_d = din("b_gate", [1, D])
    nwT_d = din("norm_wT", [128, 8])
    win_d = din("w_in", [D, DIN])
    cwT_d = din("conv_wT", [128, 40])
    cbT_d = din("conv_bT", [128, 10])
    dtb_d = din("dt_bias", [1, 16])
    alog_d = din("a_log", [1, 16])
    dsk_d = din("d_skip", [1, 16])
    dskT_d = din("dskT", [128, 8])
    mnwT_d = din("m_norm_wT", [128, 8])
    wm_d = din("w_proj_m", [D, D])
    wr_d = din("w_proj_r", [D, D])
    wo_d = din("w_out", [D, D])
    fnw_d = din("fnw", [1, D])
    cos_d = din("cosT", [L, 32])
    sin_d = din("sinT", [L, 32])
    cst_d = din("cst", [128, NCST])
    out_d = nc.dram_tensor("out", [L, D], F32, kind="ExternalOutput").ap()
    dbg_outs = {}

    es = ExitStack()

    def sb(name, shape, dt):
        return es.enter_context(nc.sbuf_tensor("s_" + name, list(shape), dt))

    def rowsize(h):
        r = 1
        for s in h.shape[1:]:
            r *= s
        return r

    def mk(h, p0, npart, off, dims):
        return bass.AP(h, p0 * rowsize(h) + off, [[rowsize(h), npart]] + [list(d) for d in dims])

    cst = sb("cst", [128, NCST], F32)
    ident_b = sb("ident_b", [128, 128], BF16)
    vec = sb("vec", [128, 144], F32)
    V_CT, V_BADA, V_NW, V_CW, V_CB, V_MNW, V_G, V_SH, V_NH, V_DSK = 0, 16, 40, 48, 88, 98, 106, 114, 122, 128
    rowv = sb("rowv", [128, 64], F32)
    gate_bc = sb("gate_bc", [128, D], F32)
    fnw_bc = sb("fnw_bc", [128, D], F32)
    wdt = sb("wdt", [128, 8, 16], BF16)
    ring = sb("ring", [128, NSLOT * 2048], F32)
    x1 = sb("x1", [128, 2, D], F32)
    xo = x1
    xs2 = sb("xs", [128, 2, D], BF16)
    junk = xs2[:, 0, :]
    hT = sb("hT", [128, 8, T], BF16)
    halo = sb("halo", [128, 10, 3], F32)
    rawt = sb("rawt", [128, 2, T + 3], F32)
    acc = sb("acc", [128, 2, T], F32)
    cs_t = sb("cs_t", [128, 2, NTC, 32], F32)
    dtt = sb("dtt", [128, NTC, 16], F32)
    uni = sb("uni", [128, 17408], BF16)
    y = sb("y", [128, NTC, D], F32)
    on = sb("on", [128, NTC, D], BF16)
    st = sb("st", [128, 64], F32)
    sm = sb("sm", [128, 128], F32)
    rhsA = sb("rhsA", [128, 1, 8, 128], F32)
    expseg = sb("expseg", [128, 2, 8, 128], BF16)
    Gm = sb("Gm", [128, 2, 128], F32)
    Wt = sb("Wt", [128, 16, 128], BF16)
    x_dt2 = sb("x_dt", [128, 2, D], BF16)
    xw2 = sb("xw", [128, 2, D], BF16)
    DD = sb("DD", [128, 8, 128], BF16)
    junk2 = sb("junk2", [128, 128], BF16)
    cTb = sb("cTb", [128, 16], BF16)
    Btok = sb("Btok", [128, 128], BF16)
    t1 = sb("t1", [128, D], F32)
    Sst = sb("Sst", [128, 512], F32)
    Sbf = sb("Sbf", [128, 512], BF16)
    qT = sb("qT", [128, 4, 128], BF16)
    kT = sb("kT", [128, 2, 4, 128], BF16)
    BCz = sb("BCz", [128, 1, 2, T], BF16)
    qwT = sb("qwT", [128, 4, 128], BF16)
    kw = sb("kw", [128, 512], BF16)
    PT = sb("PT", [128, 8, 128], BF16)
    Rst = sb("Rst", [128, 4, 128], F32)
    Rbf = sb("Rbf", [128, 2, 4, 128], BF16)
    sz = sb("sz", [128, 2, 512], F32)
    ymb = sb("ymb", [128, 2, 512], BF16)
    tmg = sz
    amr = sb("amr", [128, 2, 512], F32)

    def uview(off, dims):
        return mk(uni, 0, 128, off, dims)

    O_XBC, O_QR, O_KR, O_V = 0, 10 * T, 10 * T + NTC * 512, 10 * T + 2 * NTC * 512
    O_YMT, O_YRT, O_MRG, O_MT = 0, 8 * T, 16 * T, 26 * 512
    assert O_V + NTC * 1024 <= O_MT and O_MRG + NTC * 1024 <= O_MT and O_MT + 8 * T <= 17408

    PS = [es.enter_context(nc.psum_tensor(f"ps{i}", [128, 512], F32)) for i in range(8)]

    def psb(i):
        return PS[i][:].bitcast(BF16)

    engsem = {e: es.enter_context(nc.semaphore("sem_" + e)) for e in ("pe", "act", "dve", "pool")}
    dmasem = {}

    def dsem(key):
        if key not in dmasem:
            dmasem[key] = es.enter_context(nc.semaphore("d_" + key))
        return key

    def fsz(ap):
        n = 1
        for d in ap.shape[1:]:
            n *= d
        return n

    def dma(eng, out, in_, reads, writes, key, nbytes=None):
        dsem(key)
        if nbytes is None:
            nbytes = 128 * fsz(out) * 4
        return S.add(eng, lambda e, o=out, i=in_: e.dma_start(out=o, in_=i), reads, writes, dma=key,
                     cost=(1.2 if eng == "pool" else 0.15), lat=2.0 + nbytes / 120e3)

    AGROUP = {AF.Silu: "g18", AF.Tanh: "g18", AF.Exp: "g6", AF.Ln: "g6"}

    def act(out, in_, func, reads, writes, **kw):
        return S.add("act", lambda e: e.activation(out=out, in_=in_, func=func, **kw), reads, writes,
                     cost=0.22 + fsz(out) / 1200.0, agroup=AGROUP.get(func))

    def ecost(eng, out):
        n = fsz(out)
        return (0.07 + n / 960.0) if eng == "dve" else (0.15 + n / 450.0)

    def tt(eng, out, in0, in1, op, reads, writes):
        return S.add(eng, lambda e: e.tensor_tensor(out=out, in0=in0, in1=in1, op=op), reads, writes, cost=ecost(eng, out))

    def ts(eng, out, in0, s1, s2, op0, op1, reads, writes):
        c = ecost(eng, out)
        if s2 is None:
            return S.add(eng, lambda e: e.tensor_scalar(out=out, in0=in0, scalar1=s1, scalar2=None, op0=op0), reads, writes, cost=c)
        return S.add(eng, lambda e: e.tensor_scalar(out=out, in0=in0, scalar1=s1, scalar2=s2, op0=op0, op1=op1), reads, writes, cost=c)

    def stt(out, in0, scalar, in1, op0, op1, reads, writes):
        return S.add("dve", lambda e: e.scalar_tensor_tensor(out=out, in0=in0, scalar=scalar, in1=in1, op0=op0, op1=op1), reads, writes,
                     cost=ecost("dve", out))

    def cp(eng, out, in_, reads, writes):
        return S.add(eng, lambda e: e.tensor_copy(out=out, in_=in_), reads, writes, cost=ecost(eng, out))

    def pe(fn, reads, writes, cost=1.0):
        return S.add("pe", fn, reads, writes, cost=cost * PESCALE)

    def rstd_pool(dst, src, inv_n, reads_key, write_key, tmpcol, reads=None):
        tmp = st[:, tmpcol:tmpcol + src.shape[1]]
        ts("pool", tmp, src, inv_n, EPS, ALU.mult, ALU.add, reads if reads is not None else [reads_key], ["st_tmp%d" % tmpcol])
        nh = vec[:, V_NH:V_NH + 1].to_broadcast([128, src.shape[1]]) if src.shape[1] > 1 else vec[:, V_NH:V_NH + 1]
        tt("pool", dst, tmp, nh, ALU.pow, ["st_tmp%d" % tmpcol, "vec"], [write_key])

    def dump(name, src, shape, reads, dt=F32):
        if not dbg:
            return
        d = nc.dram_tensor("dbg_" + name, list(shape), dt, kind="ExternalOutput").ap()
        dbg_outs[name] = d
        dma("sp", d, src, reads, ["dbgout_" + name], "dbg_" + name)

    wreq = []
    for cb in range(4):
        wreq.append((wada_d, cb * 512, 512))
    for sc in range(NSC):
        for (c0, n) in ((C_XBC, 512), (C_XBC + 512, 512), (C_XBC + 1024, 256), (C_Q, 512), (C_K, 512), (C_V, 512), (C_V + 512, 512),
                        (C_Z, 512), (C_Z + 512, 512), (C_G, 512), (C_G + 512, 512)):
            wreq.append((win_d, c0, n))
        if sc == 0:
            wreq.append((wada_d, 4 * 512, 512))
            wreq.append((wada_d, 5 * 512, 512))
        for hf in range(2):
            wreq.append((wm_d, hf * 512, 512))
            wreq.append((win_d, C_GAM + hf * 512, 512))
            wreq.append((wr_d, hf * 512, 512))
            wreq.append((win_d, C_GAR + hf * 512, 512))
        wreq.append((wo_d, 0, 512))
        wreq.append((wo_d, 512, 512))
    wstate = {"issued": 0, "next": 0}

    def slot_ap(s):
        return ring[:, s * 2048:(s + 1) * 2048].bitcast(BF16).rearrange("p (k n) -> p k n", k=8)

    def wissue_upto(r):
        while wstate["issued"] <= min(r, len(wreq) - 1):
            i = wstate["issued"]
            src, c0, n = wreq[i]
            s = i % NSLOT
            srcap = src.rearrange("(kc p) n -> p kc n", p=128)[:, :, c0:c0 + n]
            dma("pool", slot_ap(s)[:, :, 0:n], srcap, [], ["ring%d" % s], "ring%d" % s)
            wstate["issued"] += 1

    def wgroup(specs):
        r0 = wstate["next"]
        wissue_upto(r0 + NSLOT - 1)
        outl = []
        for (expect_src, expect_c0) in specs:
            r = wstate["next"]
            src, c0, n = wreq[r]
            assert src is expect_src and c0 == expect_c0, (r, c0, expect_c0)
            assert r <= r0 + NSLOT - 1
            wstate["next"] += 1
            s = r % NSLOT
            outl.append((slot_ap(s), "ring%d" % s))
        return outl

    def wnext(expect_src, expect_c0):
        return wgroup([(expect_src, expect_c0)])[0]

    dma("sp", cst[:], cst_d[:, :], [], ["cst"], "cst")
    dma("sp", vec[:, V_CT:V_CT + 16], cT_d[:, :], [], ["vec"], "v0")
    dma("sp", vec[:, V_BADA:V_BADA + 24], badaT_d[:, :], [], ["vec"], "v1")
    dma("sp", vec[:, V_NW:V_NW + 8], nwT_d[:, :], [], ["vec"], "v2")
    dma("sp", vec[:, V_CW:V_CW + 40], cwT_d[:, :], [], ["vec"], "v3")
    dma("sp", vec[:, V_CB:V_CB + 10], cbT_d[:, :], [], ["vec"], "v4")
    dma("sp", vec[:, V_MNW:V_MNW + 8], mnwT_d[:, :], [], ["vec"], "v5")
    dma("sp", vec[:, V_DSK:V_DSK + 8], dskT_d[:, :], [], ["vec"], "v6")
    dma("sp", rowv[:, 0:16], bass.AP(dtb_d.tensor, 0, [[0, 128], [1, 16]]), [], ["rowv"], "r0")
    dma("sp", rowv[:, 48:64], bass.AP(alog_d.tensor, 0, [[0, 128], [1, 16]]), [], ["rowv"], "r1")
    dma("sp", rowv[:, 32:48], bass.AP(dsk_d.tensor, 0, [[0, 128], [1, 16]]), [], ["rowv"], "r2")
    dma("sp", gate_bc[:], bass.AP(bgate_d.tensor, 0, [[0, 128], [1, D]]), [], ["gate_bc"], "gb")
    dma("sp", fnw_bc[:], bass.AP(fnw_d.tensor, 0, [[0, 128], [1, D]]), [], ["fnw_bc"], "fb")
    dma("pool", wdt[:], win_d.rearrange("(kc p) n -> p kc n", p=128)[:, :, C_DT:C_DT + 16], [], ["wdt"], "wdt")
    S.add("pool", lambda e: e.memset(vec[:, V_NH:V_NH + 1], -0.5), [], ["vec"])
    S.add("pool", lambda e: e.memset(halo[:], 0.0), [], ["halo"])
    S.add("pool", lambda e: e.memset(Sst[:], 0.0), [], ["Sst"])
    S.add("pool", lambda e: e.memset(Sbf[:], 0.0), [], ["Sbf"])
    S.add("pool", lambda e: e.memset(Rst[:], 0.0), [], ["Rst"])
    S.add("pool", lambda e: e.memset(Rbf[:], 0.0), [], ["Rbf"])
    cp("dve", ident_b[:], cst[:, K_ID:K_ID + 128], ["cst"], ["ident_b"])
    act(rowv[:, 16:32], rowv[:, 48:64], AF.Exp, ["rowv"], ["rowv"])
    ts("dve", rowv[:, 16:32], rowv[:, 16:32], -1.0, None, ALU.mult, None, ["rowv"], ["rowv"])

    U_f = cst[:, K_U:K_U + 128]
    L_f = cst[:, K_L:K_L + 128]
    ones_f = cst[:, K_ONES:K_ONES + 128]

    cp("dve", cTb[:], vec[:, V_CT:V_CT + 16], ["vec"], ["cTb"])
    cbc = amr[:, 0, :].bitcast(BF16).rearrange("p (k m) -> p k m", k=8)
    for kc in range(8):
        cp("dve", cbc[:, kc, :], vec[:, V_CT + kc:V_CT + kc + 1].to_broadcast([128, 128]), ["vec"], ["amr0"])
    for blk in range(8):
        ts("dve", DD[:, blk, :], ident_b[:], vec[:, V_DSK + blk:V_DSK + blk + 1], None, ALU.mult, None, ["ident_b", "vec"], ["DD"])
    ps_mod = PS[3]
    for cb in range(4):
        W, wkey = wnext(wada_d, cb * 512)

        def f(e, W=W, cb=cb):
            ins = None
            for sub in range(4):
                j = cb * 4 + sub
                for kc in range(8):
                    ins = e.matmul(ps_mod[:, 2 * j:2 * j + 2], lhsT=W[:, kc, sub * 128:(sub + 1) * 128],
                                   rhs=cTb[:, kc:kc + 2], start=(kc == 0), stop=(kc == 7))
            return ins
        pe(f, [wkey, "cTb"], ["ps3"], cost=2.6)

    def gate_setup():
        for hf in range(2):
            W, wkey = wnext(wada_d, (4 + hf) * 512)

            def f(e, W=W, hf=hf):
                ins = None
                for kc in range(8):
                    ins = e.matmul(PS[hf][:], lhsT=cbc[:, kc, :], rhs=W[:, kc, :], start=(kc == 0), stop=(kc == 7))
                return ins
            pe(f, [wkey, "amr0"], ["ps%d" % hf], cost=2.1)
            tt("dve", gate_bc[:, hf * 512:(hf + 1) * 512], PS[hf][:], gate_bc[:, hf * 512:(hf + 1) * 512], ALU.add, ["ps%d" % hf, "gate_bc"], ["gate_bc"])
        ts("dve", gate_bc[:], gate_bc[:], 0.5, None, ALU.mult, None, ["gate_bc"], ["gate_bc"])
    modv = mk(ps_mod, 0, 128, 0, [[2, 16]])
    tt("dve", st[:, 0:16], modv, vec[:, V_BADA:V_BADA + 16], ALU.add, ["ps3", "vec"], ["st_mod"])
    cp("dve", vec[:, V_SH:V_SH + 8], st[:, 0:8], ["st_mod"], ["vec"])
    stt(vec[:, V_G:V_G + 8], st[:, 8:16], 1.0, vec[:, V_NW:V_NW + 8], ALU.add, ALU.mult, ["st_mod", "vec"], ["vec"])

    xbcT = uview(O_XBC, [[T, 10], [1, T]])
    qr = uview(O_QR, [[512, NTC], [1, 512]])
    kr = uview(O_KR, [[512, NTC], [1, 512]])
    vv = uview(O_V, [[1024, NTC], [1, 1024]])
    ymT = uview(O_YMT, [[T, 8], [1, T]])
    yrT = uview(O_YRT, [[T, 8], [1, T]])
    mrg = uview(O_MRG, [[1024, NTC], [1, 1024]])
    mT = uview(O_MT, [[T, 8], [1, T]])
    A_KEYS = ["xbcT%d" % b for b in range(10)] + ["qr", "kr", "v"]
    B_KEYS = ["ymT", "yrT", "mrg", "mT"]

    mmbank = {"i": 0}

    def nextbank(banks=(0, 1)):
        b = banks[mmbank["i"] % len(banks)]
        mmbank["i"] += 1
        return b

    for sc in range(nsc if stop != 'setup' else 0):
        dma("sp", cs_t[:, 0, :, :], cos_d.rearrange("(c p) f -> p c f", p=128)[:, sc * NTC:(sc + 1) * NTC, :], [], ["cs_t"], "cs0")
        dma("sp", cs_t[:, 1, :, :], sin_d.rearrange("(c p) f -> p c f", p=128)[:, sc * NTC:(sc + 1) * NTC, :], [], ["cs_t"], "cs1")
        for c in range(NTC):
            gc = sc * NTC + c
            b = gc % 2
            dma("sp", x1[:, b, :], x_d[gc * 128:(gc + 1) * 128, :], [], ["x1_%d" % b], "x1_%d" % b)
            xs = xs2[:, b, :]
            xk = "xs%d" % b
            sc0 = 16 + 4 * b
            act(xs, x1[:, b, :], AF.Square, ["x1_%d" % b], [xk, "st_ss%d" % b], accum_out=st[:, sc0:sc0 + 1])
            rstd_pool(st[:, sc0 + 1:sc0 + 2], st[:, sc0:sc0 + 1], 1.0 / D, "st_ss%d" % b, "st_rstd%d" % b, sc0 + 2)
            act(xs, x1[:, b, :], AF.Copy, ["x1_%d" % b, "st_rstd%d" % b], [xk], scale=st[:, sc0 + 1:sc0 + 2])
            pb = 2 + b

            def f(e, xs=xs, pb=pb):
                ins = None
                for kc in range(8):
                    ins = e.transpose(psb(pb)[:, kc * 128:(kc + 1) * 128], xs[:, kc * 128:(kc + 1) * 128], ident_b[:])
                return ins
            pe(f, [xk, "ident_b"], ["ps%d" % pb], cost=1.0)
            for kc in range(8):
                if kc % 2 == 0:
                    act(hT[:, kc, c * 128:(c + 1) * 128], psb(pb)[:, kc * 128:(kc + 1) * 128], AF.Identity, ["ps%d" % pb, "vec"], ["hT%d" % c],
                        scale=vec[:, V_G + kc:V_G + kc + 1], bias=vec[:, V_SH + kc:V_SH + kc + 1])
                else:
                    ts("dve", hT[:, kc, c * 128:(c + 1) * 128], psb(pb)[:, kc * 128:(kc + 1) * 128],
                       vec[:, V_G + kc:V_G + kc + 1], vec[:, V_SH + kc:V_SH + kc + 1], ALU.mult, ALU.add,
                       ["ps%d" % pb, "vec"], ["hT%d" % c])
        HT_KEYS = ["hT%d" % c for c in range(NTC)]
        if stop == 'p1':
            continue
        if sc == 0:
            dump("hT", hT[:], [128, 8, T], HT_KEYS, BF16)

        wslots = wgroup([(win_d, C_XBC), (win_d, C_XBC + 512), (win_d, C_XBC + 1024)])
        for blk in range(10):
            W, wkey = wslots[blk // 4]
            sub = blk % 4
            bk = nextbank((4, 5, 6, 7))
            rb = blk % 2

            def f(e, W=W, sub=sub, bk=bk):
                ins = None
                for kc in range(8):
                    ins = e.matmul(PS[bk][:], lhsT=W[:, kc, sub * 128:(sub + 1) * 128], rhs=hT[:, kc, :], start=(kc == 0), stop=(kc == 7))
                return ins
            pe(f, [wkey] + HT_KEYS, ["ps%d" % bk], cost=2.1)
            cp("dve", rawt[:, rb, 0:3], halo[:, blk, :], ["halo"], ["rawt%d" % rb])
            act(rawt[:, rb, 3:3 + T], PS[bk][:], AF.Copy, ["ps%d" % bk], ["rawt%d" % rb])
            cp("dve", halo[:, blk, :], rawt[:, rb, T:T + 3], ["rawt%d" % rb], ["halo"])
            act(acc[:, rb, :], PS[bk][:], AF.Identity, ["ps%d" % bk, "vec"], ["acc%d" % rb],
                scale=vec[:, V_CW + blk * 4 + 3:V_CW + blk * 4 + 4], bias=vec[:, V_CB + blk:V_CB + blk + 1])
            for s_ in (1, 2, 3):
                stt(acc[:, rb, :], rawt[:, rb, 3 - s_:3 - s_ + T], vec[:, V_CW + blk * 4 + 3 - s_:V_CW + blk * 4 + 4 - s_],
                    acc[:, rb, :], ALU.mult, ALU.add, ["rawt%d" % rb, "acc%d" % rb, "vec"], ["acc%d" % rb])
            act(xbcT[:, blk, :], acc[:, rb, :], AF.Silu, ["acc%d" % rb], ["xbcT%d" % blk])
        if sc == 0:
            dump("xbcT", xbcT, [128, 10, T], ["xbcT%d" % b for b in range(10)], BF16)
        for g in range(2):
            act(BCz[:, 0, g, :], xbcT[:, 9, :], AF.Copy, ["xbcT9", "cst"], ["Cz"], scale=cst[:, K_MSK + g:K_MSK + g + 1])
        if stop == 'p2a':
            continue
        for c in range(NTC):
            def f(e, c=c):
                ins = None
                for kc in range(8):
                    ins = e.matmul(PS[3][:, 256:272], lhsT=hT[:, kc, c * 128:(c + 1) * 128], rhs=wdt[:, kc, :], start=(kc == 0), stop=(kc == 7))
                return ins
            pe(f, ["hT%d" % c, "wdt"], ["ps3"], cost=0.6)
            tt("dve", sm[:, 96:112], PS[3][:, 256:272], rowv[:, 0:16], ALU.add, ["ps3", "rowv"], ["sm_dtpre"])
            act(sm[:, 112:128], sm[:, 96:112], AF.Exp, ["sm_dtpre"], ["sm_e"])
            act(dtt[:, c, :], sm[:, 112:128], AF.Ln, ["sm_e"], ["dtt%d" % c], bias=1.0)
        if sc == 0:
            dump("dt", dtt[:], [128, NTC, 16], ["dtt%d" % c for c in range(NTC)])
        for (dst, dkey, c0) in ((qr, "qr", C_Q), (kr, "kr", C_K)):
            W, wkey = wnext(win_d, c0)
            for c in range(NTC):
                bk = nextbank()

                def f(e, W=W, c=c, bk=bk):
                    ins = None
                    for kc in range(8):
                        ins = e.matmul(PS[bk][:], lhsT=hT[:, kc, c * 128:(c + 1) * 128], rhs=W[:, kc, :], start=(kc == 0), stop=(kc == 7))
                    return ins
                pe(f, [wkey, "hT%d" % c], ["ps%d" % bk], cost=2.1)
                psv = mk(PS[bk], 0, 128, 0, [[64, 8], [32, 2], [1, 32]])
                cosb = mk(cs_t, 0, 128, c * 32, [[0, 8], [0, 2], [1, 32]])
                sinb = mk(cs_t, 0, 128, NTC * 32 + c * 32, [[0, 8], [0, 2], [1, 32]])
                tc_ = mk(t1, 0, 128, 0, [[64, 8], [32, 2], [1, 32]])
                ts_ = mk(t1, 0, 128, 512, [[64, 8], [32, 2], [1, 32]])
                tt("dve", tc_, psv, cosb, ALU.mult, ["ps%d" % bk, "cs_t"], ["t1a"])
                tt("dve", ts_, psv, sinb, ALU.mult, ["ps%d" % bk, "cs_t"], ["t1b"])
                dv = mk(uni, 0, 128, (O_QR if dkey == "qr" else O_KR) + c * 512, [[64, 8], [32, 2], [1, 32]])
                tt("dve", dv[:, :, 0, :], tc_[:, :, 0, :], ts_[:, :, 1, :], ALU.subtract, ["t1a", "t1b"], [dkey])
                tt("dve", dv[:, :, 1, :], ts_[:, :, 0, :], tc_[:, :, 1, :], ALU.add, ["t1a", "t1b"], [dkey])
        for half in range(2):
            W, wkey = wnext(win_d, C_V + half * 512)
            for c in range(NTC):
                bk = nextbank((4, 5, 6, 7))

                def f(e, W=W, c=c, bk=bk):
                    ins = None
                    for kc in range(8):
                        ins = e.matmul(PS[bk][:], lhsT=hT[:, kc, c * 128:(c + 1) * 128], rhs=W[:, kc, :], start=(kc == 0), stop=(kc == 7))
                    return ins
                pe(f, [wkey, "hT%d" % c], ["ps%d" % bk], cost=2.1)
                act(vv[:, c, half * 512:(half + 1) * 512], PS[bk][:], AF.Copy, ["ps%d" % bk], ["v"])
        if sc == 0:
            dump("qr", qr, [128, NTC, 512], ["qr"], BF16)
            dump("kr", kr, [128, NTC, 512], ["kr"], BF16)

        if stop == 'p2':
            continue
        for c in range(NTC):
            cs_ = slice(c * 128, (c + 1) * 128)
            a_sb, acs_sb, dE2, E1, CD, dtE2 = sm[:, 0:16], sm[:, 16:32], sm[:, 32:48], sm[:, 48:64], sm[:, 64:80], sm[:, 80:96]
            tt("dve", a_sb, dtt[:, c, :], rowv[:, 16:32], ALU.mult, ["dtt%d" % c, "rowv"], ["sm_a"])

            def f(e):
                e.matmul(PS[3][:, 256:272], lhsT=U_f, rhs=a_sb, start=True, stop=True)
                return e.matmul(PS[3][:, 272:288], lhsT=ones_f, rhs=a_sb, start=True, stop=True)
            pe(f, ["sm_a", "cst"], ["ps3"], cost=0.3)
            act(acs_sb, PS[3][:, 256:272], AF.Copy, ["ps3"], ["sm_acs"])
            tt("dve", dE2, PS[3][:, 272:288], acs_sb, ALU.subtract, ["ps3", "sm_acs"], ["sm_d"])
            act(dE2, dE2, AF.Exp, ["sm_d"], ["sm_d"])
            act(E1, acs_sb, AF.Exp, ["sm_acs"], ["sm_E1"])
            act(CD, PS[3][:, 272:288], AF.Exp, ["ps3"], ["sm_CD"])
            tt("dve", dtE2, dtt[:, c, :], dE2, ALU.mult, ["dtt%d" % c, "sm_d"], ["sm_dtE2"])

            if stop == 'p3a':
                continue
            def f(e, cs_=cs_):
                ins = None
                for blk in range(8):
                    ins = e.transpose(psb(2)[:, blk * 128:(blk + 1) * 128], xbcT[:, blk, cs_], ident_b[:])
                ins = e.transpose(psb(3)[:, 640:768], xbcT[:, 8, cs_], ident_b[:])
                return ins
            pe(f, ["xbcT%d" % b for b in range(9)] + ["ident_b"], ["ps2", "ps3"], cost=1.1)
            psxb = psb(2).rearrange("p (h q) -> p h q", h=16)

            def bc16(base_ap_tensor, col0):
                return mk(base_ap_tensor, 0, 128, col0, [[1, 16], [0, 64]])
            x_dt = x_dt2[:, c % 2, :]
            xw = xw2[:, c % 2, :]
            kxd, kxw = "x_dt%d" % (c % 2), "xw%d" % (c % 2)
            tt("dve", x_dt.rearrange("p (h q) -> p h q", h=16), psxb, mk(dtt, 0, 128, c * 16, [[1, 16], [0, 64]]), ALU.mult, ["ps2", "dtt%d" % c], [kxd])
            tt("dve", xw.rearrange("p (h q) -> p h q", h=16), psxb, bc16(sm, 80), ALU.mult, ["ps2", "sm_dtE2"], [kxw])
            act(Btok[:], psb(3)[:, 640:768], AF.Copy, ["ps3"], ["Btok"])

            if stop == 'p3b':
                continue
            def f(e, cs_=cs_):
                ins = None
                for g in range(2):
                    ins = e.matmul(PS[g][:, 0:128], lhsT=xbcT[:, 8, cs_], rhs=BCz[:, 0, g, cs_], start=True, stop=True)
                return ins
            pe(f, ["xbcT8", "Cz"], ["ps0", "ps1"], cost=0.2)
            for g in range(2):
                tt("dve", Gm[:, g, :], PS[g][:, 0:128], U_f, ALU.mult, ["ps%d" % g, "cst"], ["Gm%d" % g])

            for g in range(2):
                tt(OFFE("rhsA"), rhsA[:, 0, :, :], mk(cst, 0, 128, K_U, [[0, 8], [1, 128]]), mk(sm, 0, 128, g * 8, [[1, 8], [0, 128]]), ALU.mult, ["cst", "sm_a"], ["rhsA"])

                def f(e, g=g):
                    e.matmul(PS[4][:], lhsT=L_f, rhs=rhsA[:, 0, 0:4, :].rearrange("p h i -> p (h i)"), start=True, stop=True)
                    return e.matmul(PS[5][:], lhsT=L_f, rhs=rhsA[:, 0, 4:8, :].rearrange("p h i -> p (h i)"), start=True, stop=True)
                pe(f, ["rhsA", "cst"], ["ps4", "ps5"], cost=1.8)
                act(expseg[:, g, 0:4, :].rearrange("p h i -> p (h i)"), PS[4][:], AF.Exp, ["ps4"], ["expseg%da" % g])
                act(expseg[:, g, 4:8, :].rearrange("p h i -> p (h i)"), PS[5][:], AF.Exp, ["ps5"], ["expseg%db" % g])
                tt(OFFE("Wt"), Wt[:, g * 8:(g + 1) * 8, :], expseg[:, g, :, :], mk(Gm, 0, 128, g * 128, [[0, 8], [1, 128]]), ALU.mult,
                   ["expseg%da" % g, "expseg%db" % g, "Gm%d" % g], ["Wt%d" % g])

            if stop == 'p3c':
                continue
            for g in range(2):
                def f(e, g=g, cs_=cs_, x_dt=x_dt):
                    for j in range(4):
                        e.matmul(PS[6 + g][:, j * 128:(j + 1) * 128], lhsT=xbcT[:, g * 4 + j, cs_], rhs=DD[:, g * 4 + j, :], start=(j == 0), stop=False)
                    ins = None
                    for hl in range(8):
                        h = g * 8 + hl
                        ins = e.matmul(PS[6 + g][:, hl * 64:(hl + 1) * 64], lhsT=Wt[:, h, :], rhs=x_dt[:, h * 64:(h + 1) * 64], start=False, stop=(hl == 7))
                    return ins
                pe(f, ["DD", kxd, "Wt%d" % g] + ["xbcT%d" % (g * 4 + j) for j in range(4)], ["ps%d" % (6 + g)], cost=1.1)

                def f(e, g=g, cs_=cs_):
                    return e.matmul(PS[g][:], lhsT=BCz[:, 0, g, cs_], rhs=Sbf[:, :], start=True, stop=True)
                pe(f, ["Cz", "Sbf"], ["ps%d" % g], cost=0.27)
                tt("dve", t1[:, g * 512:(g + 1) * 512].rearrange("p (h q) -> p h q", h=8), PS[g][:].rearrange("p (h q) -> p h q", h=8),
                   mk(sm, 0, 128, 48 + g * 8, [[1, 8], [0, 64]]), ALU.mult, ["ps%d" % g, "sm_E1"], ["t1a" if g == 0 else "t1b"])
                tt("dve", y[:, c, g * 512:(g + 1) * 512], PS[6 + g][:], t1[:, g * 512:(g + 1) * 512], ALU.add,
                   ["ps%d" % (6 + g), "t1a" if g == 0 else "t1b"], ["y%d" % c])
            if stop == 'p3d':
                continue
            for g in range(2):
                def f(e, g=g, xw=xw):
                    return e.matmul(PS[4 + g][:], lhsT=Btok[:], rhs=xw[:, g * 512:(g + 1) * 512], start=True, stop=True)
                pe(f, ["Btok", kxw], ["ps%d" % (4 + g)], cost=0.27)
                r0 = g * 64
                tt(OFFE("SstCD"), mk(Sst, r0, 64, 0, [[64, 8], [1, 64]]), mk(Sst, r0, 64, 0, [[64, 8], [1, 64]]), mk(sm, r0, 64, 64 + g * 8, [[1, 8], [0, 64]]),
                   ALU.mult, ["Sst", "sm_CD"], ["Sst"])
                tt("dve", Sst[r0:r0 + 64, :], Sst[r0:r0 + 64, :], PS[4 + g][r0:r0 + 64, :], ALU.add, ["Sst", "ps%d" % (4 + g)], ["Sst"])
                act(Sbf[r0:r0 + 64, :], Sst[r0:r0 + 64, :], AF.Copy, ["Sst"], ["Sbf"])

            if stop == 'p3e':
                continue
            def f(e, c=c):
                ins = None
                for blk in range(4):
                    e.transpose(psb(2)[:, blk * 128:(blk + 1) * 128], qr[:, c, blk * 128:(blk + 1) * 128], ident_b[:])
                    ins = e.transpose(psb(2)[:, 512 + blk * 128:512 + (blk + 1) * 128], kr[:, c, blk * 128:(blk + 1) * 128], ident_b[:])
                return ins
            pe(f, ["qr", "kr", "ident_b"], ["ps2"], cost=1.0)
            act(qT[:].rearrange("p b i -> p (b i)"), psb(2)[:, 0:512], AF.Copy, ["ps2"], ["qT"])
            for hh in range(2):
                act(kT[:, hh, :, :].rearrange("p b i -> p (b i)"), psb(2)[:, 512:1024], AF.Copy, ["ps2", "cst"], ["kT"], scale=cst[:, K_MSK + hh:K_MSK + hh + 1])
            tt("dve", qwT[:].rearrange("p b i -> p (b i)"), psb(2)[:, 0:512], cst[:, K_GAM:K_GAM + 512], ALU.mult, ["ps2", "cst"], ["qwT"])
            tt(OFFE("kw"), kw[:].rearrange("p (h d) -> p h d", h=8), kr[:, c, :].rearrange("p (h d) -> p h d", h=8),
               mk(cst, 0, 128, K_KWV, [[1, 8], [0, 64]]), ALU.mult, ["kr", "cst"], ["kw"])

            def f(e):
                ins = None
                for h in range(8):
                    blk, hh = h // 2, h % 2
                    ins = e.matmul(PS[hh][:, blk * 128:(blk + 1) * 128], lhsT=kT[:, hh, blk, :],
                                   rhs=qT[:, blk, :], start=True, stop=True)
                return ins
            pe(f, ["qT", "kT"], ["ps0", "ps1"], cost=0.8)
            for bk in range(2):
                tt("dve", mk(PT, 0, 128, bk * 128, [[256, 4], [1, 128]]), PS[bk][:].rearrange("p (b i) -> p b i", b=4),
                   mk(cst, 0, 128, K_DMAT + bk * 128, [[256, 4], [1, 128]]), ALU.mult, ["ps%d" % bk, "cst"], ["PT%d" % bk])

            if stop == 'p3f':
                continue
            def f(e, c=c):
                ins = None
                for h in range(8):
                    blk, hh = h // 2, h % 2
                    o_ = PS[6 + hh][:, blk * 128:(blk + 1) * 128]
                    e.matmul(o_, lhsT=qwT[:, blk, :], rhs=Rbf[:, hh, blk, :], start=True, stop=False)
                    ins = e.matmul(o_, lhsT=PT[:, h, :], rhs=vv[:, c, h * 128:(h + 1) * 128], start=False, stop=True)
                return ins
            pe(f, ["qwT", "Rbf", "PT0", "PT1", "v"], ["ps6", "ps7"], cost=1.6)

            def f(e, c=c):
                ins = None
                for h in range(8):
                    blk = h // 2
                    ins = e.matmul(PS[4 + h // 4][:, (h % 4) * 128:(h % 4 + 1) * 128], lhsT=kw[:, blk * 128:(blk + 1) * 128],
                                   rhs=vv[:, c, h * 128:(h + 1) * 128], start=True, stop=True)
                return ins
            pe(f, ["kw", "v"], ["ps4", "ps5"], cost=0.8)
            tt(OFFE("RstRD"), Rst[:], Rst[:], mk(cst, 0, 128, K_RD, [[1, 4], [0, 128]]), ALU.mult, ["Rst", "cst"], ["Rst"])
            for bk in range(2):
                for hh in range(2):
                    r0 = hh * 64
                    tt("dve", Rst[r0:r0 + 64, 2 * bk:2 * bk + 2, :], Rst[r0:r0 + 64, 2 * bk:2 * bk + 2, :],
                       mk(PS[4 + bk], r0, 64, hh * 128, [[256, 2], [1, 128]]), ALU.add, ["Rst", "ps%d" % (4 + bk)], ["Rst"])
            for hh in range(2):
                act(Rbf[:, hh, :, :], Rst[:], AF.Copy, ["Rst", "cst"], ["Rbf"], scale=cst[:, K_MSK + hh:K_MSK + hh + 1])
            if stop == 'p3g':
                continue
            for h in range(8):
                blk, hh = h // 2, h % 2
                act(junk2[:], PS[6 + hh][:, blk * 128:(blk + 1) * 128], AF.Square, ["ps%d" % (6 + hh)], ["junk2", "st_ss8_%d" % h],
                    accum_out=st[:, 24 + h:25 + h])
            rstd_pool(st[:, 32:40], st[:, 24:32], 1.0 / 128, None, "st_rstd8", 40, reads=["st_ss8_%d" % h for h in range(8)])
            for bk in range(2):
                tt("dve", mk(on, 0, 128, c * 1024 + bk * 128, [[256, 4], [1, 128]]), PS[6 + bk][:].rearrange("p (b e) -> p b e", b=4),
                   mk(st, 0, 128, 32 + bk, [[2, 4], [0, 128]]), ALU.mult, ["ps%d" % (6 + bk), "st_rstd8"], ["on%d" % c])
        if sc == 0:
            dump("y", y[:], [128, NTC, D], ["y%d" % c for c in range(NTC)])
            dump("on", on[:], [128, NTC, D], ["on%d" % c for c in range(NTC)], BF16)

        if stop is not None and stop.startswith('p3'):
            continue
        for zb in range(2):
            W, wkey = wnext(win_d, C_Z + zb * 512)
            for c in range(NTC):
                bk = nextbank((0, 1, 4, 5))
                sb_ = c % 2
                cs_ = slice(c * 128, (c + 1) * 128)

                def f(e, W=W, c=c, bk=bk):
                    ins = None
                    for kc in range(8):
                        ins = e.matmul(PS[bk][:], lhsT=hT[:, kc, c * 128:(c + 1) * 128], rhs=W[:, kc, :], start=(kc == 0), stop=(kc == 7))
                    return ins
                pe(f, [wkey, "hT%d" % c], ["ps%d" % bk], cost=2.1)
                act(sz[:, sb_, :], PS[bk][:], AF.Silu, ["ps%d" % bk], ["sz%d" % sb_])
                ysl = y[:, c, zb * 512:(zb + 1) * 512]
                tt(OFFE("yz"), ysl, ysl, sz[:, sb_, :], ALU.mult, ["y%d" % c, "sz%d" % sb_], ["y%d" % c])
                act(junk[:, 0:512], ysl, AF.Square, ["y%d" % c], ["xs0", "st_ssg"], accum_out=st[:, 48:49])
                rstd_pool(st[:, 49:50], st[:, 48:49], 1.0 / 512, "st_ssg", "st_rstdg", 50)
                ts("dve", ymb[:, sb_, :], ysl, st[:, 49:50], None, ALU.mult, None, ["y%d" % c, "st_rstdg"], ["ymb%d" % sb_])

                def f(e, sb_=sb_):
                    ins = None
                    for j in range(4):
                        ins = e.transpose(psb(2 + sb_)[:, j * 128:(j + 1) * 128], ymb[:, sb_, j * 128:(j + 1) * 128], ident_b[:])
                    return ins
                pe(f, ["ymb%d" % sb_, "ident_b"], ["ps%d" % (2 + sb_)], cost=0.5)
                tt("dve", ymT[:, zb * 4:(zb + 1) * 4, cs_], psb(2 + sb_)[:, 0:512].rearrange("p (j t) -> p j t", j=4),
                   mk(vec, 0, 128, V_MNW + zb * 4, [[1, 4], [0, 128]]), ALU.mult, ["ps%d" % (2 + sb_), "vec"], ["ymT"])
        for gb in range(2):
            W, wkey = wnext(win_d, C_G + gb * 512)
            for c in range(NTC):
                bk = nextbank((0, 1, 4, 5))
                sb_ = c % 2
                cs_ = slice(c * 128, (c + 1) * 128)

                def f(e, W=W, c=c, bk=bk):
                    ins = None
                    for kc in range(8):
                        ins = e.matmul(PS[bk][:], lhsT=hT[:, kc, c * 128:(c + 1) * 128], rhs=W[:, kc, :], start=(kc == 0), stop=(kc == 7))
                    return ins
                pe(f, [wkey, "hT%d" % c], ["ps%d" % bk], cost=2.1)
                act(sz[:, sb_, :], PS[bk][:], AF.Silu, ["ps%d" % bk], ["sz%d" % sb_])
                tt(OFFE("yrb"), ymb[:, sb_, :], on[:, c, gb * 512:(gb + 1) * 512], sz[:, sb_, :], ALU.mult, ["on%d" % c, "sz%d" % sb_], ["ymb%d" % sb_])

                def f(e, sb_=sb_):
                    ins = None
                    for j in range(4):
                        ins = e.transpose(psb(2 + sb_)[:, j * 128:(j + 1) * 128], ymb[:, sb_, j * 128:(j + 1) * 128], ident_b[:])
                    return ins
                pe(f, ["ymb%d" % sb_, "ident_b"], ["ps%d" % (2 + sb_)], cost=0.5)
                act(yrT[:, gb * 4:(gb + 1) * 4, cs_], psb(2 + sb_)[:, 0:512].rearrange("p (j t) -> p j t", j=4), AF.Copy, ["ps%d" % (2 + sb_)], ["yrT"])
        if sc == 0:
            dump("ymT", ymT, [128, 8, T], ["ymT"], BF16)
            dump("yrT", yrT, [128, 8, T], ["yrT"], BF16)
        if stop == 'p4':
            continue
        if sc == 0:
            gate_setup()
        for hf in range(2):
            (Wm_, kWm), (Wgm, kWgm), (Wr_, kWr), (Wgr, kWgr) = wgroup(
                [(wm_d, hf * 512), (win_d, C_GAM + hf * 512), (wr_d, hf * 512), (win_d, C_GAR + hf * 512)])
            for c in range(NTC):
                cs_ = slice(c * 128, (c + 1) * 128)
                B5 = (4, 5, 6, 7) if c % 2 == 0 else (0, 1, 2, 3)
                for (bk, lh, lkey, W, wkey) in ((B5[0], ymT, "ymT", Wm_, kWm), (B5[1], hT, "hT%d" % c, Wgm, kWgm), (B5[2], yrT, "yrT", Wr_, kWr), (B5[3], hT, "hT%d" % c, Wgr, kWgr)):
                    def f(e, bk=bk, lh=lh, W=W, cs_=cs_):
                        ins = None
                        for kc in range(8):
                            ins = e.matmul(PS[bk][:], lhsT=lh[:, kc, cs_], rhs=W[:, kc, :], start=(kc == 0), stop=(kc == 7))
                        return ins
                    pe(f, [lkey, wkey], ["ps%d" % bk], cost=2.1)
                act(tmg[:, 0, :], PS[B5[1]][:], AF.Tanh, ["ps%d" % B5[1]], ["sz0"], scale=0.5)
                act(tmg[:, 1, :], PS[B5[3]][:], AF.Tanh, ["ps%d" % B5[3]], ["sz1"], scale=0.5)
                stt(amr[:, 0, :], tmg[:, 0, :], 1.0, PS[B5[0]][:], ALU.add, ALU.mult, ["sz0", "ps%d" % B5[0]], ["amr0"])
                stt(amr[:, 1, :], tmg[:, 1, :], 1.0, PS[B5[2]][:], ALU.add, ALU.mult, ["sz1", "ps%d" % B5[2]], ["amr1"])
                tt(OFFE("mrg"), mrg[:, c, hf * 512:(hf + 1) * 512], amr[:, 0, :], amr[:, 1, :], ALU.add, ["amr0", "amr1"], ["mrg"])
        for c in range(NTC):
            def f(e, c=c):
                ins = None
                for kc in range(8):
                    ins = e.transpose(psb(2)[:, kc * 128:(kc + 1) * 128], mrg[:, c, kc * 128:(kc + 1) * 128], ident_b[:])
                return ins
            pe(f, ["mrg", "ident_b"], ["ps2"], cost=1.0)
            act(mT[:, :, c * 128:(c + 1) * 128], psb(2).rearrange("p (k t) -> p k t", k=8), AF.Copy, ["ps2"], ["mT"])
        if sc == 0:
            dump("mrg", mrg, [128, NTC, D], ["mrg"], BF16)
        if stop == 'p5':
            continue
        (Wo0, kWo0), (Wo1, kWo1) = wgroup([(wo_d, 0), (wo_d, 512)])
        for c in range(NTC):
            gc = sc * NTC + c
            b = gc % 2
            cs_ = slice(c * 128, (c + 1) * 128)
            dma("sp", xo[:, b, :], x_d[gc * 128:(gc + 1) * 128, :], [], ["x1_%d" % b], "x1_%d" % b)
            for hf, (W, wkey) in enumerate(((Wo0, kWo0), (Wo1, kWo1))):
                bk = nextbank((0, 1, 4, 5))

                def f(e, W=W, cs_=cs_, bk=bk):
                    ins = None
                    for kc in range(8):
                        ins = e.matmul(PS[bk][:], lhsT=mT[:, kc, cs_], rhs=W[:, kc, :], start=(kc == 0), stop=(kc == 7))
                    return ins
                pe(f, ["mT", wkey], ["ps%d" % bk], cost=2.1)
                tt("dve", amr[:, hf, :], PS[bk][:], gate_bc[:, hf * 512:(hf + 1) * 512], ALU.mult, ["ps%d" % bk, "gate_bc"], ["amr%d" % hf])
                tt(OFFE("xoadd"), xo[:, b, hf * 512:(hf + 1) * 512], xo[:, b, hf * 512:(hf + 1) * 512], amr[:, hf, :], ALU.add, ["x1_%d" % b, "amr%d" % hf], ["x1_%d" % b])
            act(junk, xo[:, b, :], AF.Square, ["x1_%d" % b], ["xs0", "st_ssf"], accum_out=st[:, 52:53])
            rstd_pool(st[:, 53:54], st[:, 52:53], 1.0 / D, "st_ssf", "st_rstdf", 54)
            stt(xo[:, b, :], xo[:, b, :], st[:, 53:54], fnw_bc[:], ALU.mult, ALU.mult, ["x1_%d" % b, "st_rstdf", "fnw_bc"], ["x1_%d" % b])
            dma("sp", out_d[gc * 128:(gc + 1) * 128, :], xo[:, b, :], ["x1_%d" % b], ["out%d" % gc], "out%d" % b)
    allout = ["out%d" % gc for gc in range(NCH)] + ["dbgout_" + k for k in dbg_outs] + ["ring%d" % i for i in range(NSLOT)] + ["vec", "rowv", "gate_bc", "fnw_bc", "cst", "wdt", "cs_t", "x1_0", "x1_1"]
    S.add("sp", None, allout, [])

    S.reorder(REORDER)
    with nc.Block() as block:
        @block.sync
        def _(e):
            S.emit("sp", e, engsem, dmasem)

        @block.gpsimd
        def _(e):
            S.emit("pool", e, engsem, dmasem)

        @block.vector
        def _(e):
            S.emit("dve", e, engsem, dmasem)

        @block.scalar
        def _(e):
            S.emit("act", e, engsem, dmasem)

        @block.tensor
        def _(e):
            S.emit("pe", e, engsem, dmasem)
    es.close()
    return nc, dbg_outs


def _consts():
    H, Q = 8, 128
    log_g = np.log1p(-np.exp2(-5.0 - np.arange(H, dtype=np.float64)))
    idx = np.arange(Q, dtype=np.float64)
    cst = np.zeros((128, NCST), np.float64)
    rel = idx[None, :] - idx[:, None]
    for h in range(H):
        m = np.where(rel >= 0, np.exp(rel * log_g[h]), 0.0) * (64 ** -0.5)
        cst[:, K_DMAT + h * 128:K_DMAT + (h + 1) * 128] = m
    p = np.arange(128)
    hh = p // 64
    for blk in range(4):
        h = 2 * blk + hh
        cst[:, K_GAM + blk * 128:K_GAM + (blk + 1) * 128] = np.exp((idx[None, :] + 1) * log_g[h][:, None])
        cst[:, K_RD + blk] = np.exp(Q * log_g[h])
    for h in range(H):
        cst[:, K_KWV + h] = np.exp((Q - 1 - idx) * log_g[h]) * (64 ** -0.5)
    cst[:, K_U:K_U + 128] = (idx[:, None] <= idx[None, :])
    cst[:, K_L:K_L + 128] = (idx[:, None] > idx[None, :])
    cst[:, K_ONES:K_ONES + 128] = 1.0
    cst[:, K_ID:K_ID + 128] = np.eye(128)
    cst[:64, K_MSK] = 1.0
    cst[64:, K_MSK + 1] = 1.0
    half = 32
    inv = 10000.0 ** (-np.arange(half, dtype=np.float64) / half)
    ang = np.arange(L, dtype=np.float64)[:, None] * inv[None, :]
    cosT = np.cos(ang).astype(np.float32)
    sinT = np.sin(ang).astype(np.float32)
    return cst.astype(np.float32), cosT, sinT


_CACHE = {}


def _prep_inputs(inputs):
    f = lambda a: np.ascontiguousarray(np.asarray(a, dtype=np.float32))
    x = f(inputs["x"]); c = f(inputs["c"])
    cst, cosT, sinT = _consts()
    w_ada = f(inputs["w_ada"][0]); b_ada = f(inputs["b_ada"][0])
    shared = {
        "w_ada": w_ada,
        "b_adaT": np.ascontiguousarray(b_ada.reshape(24, 128).T),
        "b_gate": np.ascontiguousarray(b_ada[2048:3072].reshape(1, D)),
        "norm_wT": np.ascontiguousarray(f(inputs["norm_w"][0]).reshape(8, 128).T),
        "w_in": f(inputs["w_in"][0]),
        "conv_wT": np.ascontiguousarray(f(inputs["conv_w"][0]).reshape(4, 10, 128).transpose(2, 1, 0).reshape(128, 40)),
        "conv_bT": np.ascontiguousarray(f(inputs["conv_b"][0]).reshape(10, 128).T),
        "dt_bias": f(inputs["dt_bias"][0]).reshape(1, 16),
        "a_log": f(inputs["a_log"][0]).reshape(1, 16),
        "d_skip": f(inputs["d_skip"][0]).reshape(1, 16),
        "dskT": np.ascontiguousarray(np.repeat(f(inputs["d_skip"][0]), 64).reshape(8, 128).T),
        "m_norm_wT": np.ascontiguousarray(f(inputs["m_norm_w"][0]).reshape(8, 128).T),
        "w_proj_m": f(inputs["w_proj_m"][0]),
        "w_proj_r": f(inputs["w_proj_r"][0]),
        "w_out": f(inputs["w_out"][0]),
        "fnw": f(inputs["final_norm_w"]).reshape(1, D),
        "cosT": cosT, "sinT": sinT, "cst": cst,
    }
    in_maps = []
    for b in range(8):
        m = dict(shared)
        m["x"] = np.ascontiguousarray(x[b])
        cT = np.zeros((128, 16), np.float32)
        cT[:, 0:8] = c[b].reshape(8, 128).T
        m["cT"] = cT
        in_maps.append(m)
    return in_maps


def kernel(**inputs):
    in_maps = _prep_inputs(inputs)
    if "nc" not in _CACHE:
        _CACHE["nc"] = build(False)[0]
    nc = _CACHE["nc"]
    res = run_bass_kernel_spmd(nc, in_maps, core_ids=list(range(8)))
    out = np.stack([np.asarray(r["out"], dtype=np.float32) for r in res.results], axis=0)
    return out
```
